# Optimizing a Trainium2 kernel written in Bass

```python
import math
import jax, jax.numpy as jnp
from jax import lax
import numpy as np

D_MODEL = 2048
BATCH = 4
SEQ = 4096
DEPTH = 2

RMS_EPS = 1e-6
ROPE_THETA = 500000.0
HEAD_DIM = 128
ROT_DIMS = HEAD_DIM // 4
Q_BLOCK = 128

NSA_HEADS = 8
NSA_GROUPS = 2
NSA_HPG = NSA_HEADS // NSA_GROUPS
NSA_CMP_LEN = 32
NSA_CMP_STRIDE = 16
NSA_CMP_HIDDEN = 256
NSA_SEL_BLOCK = 64
NSA_SEL_TOPK = 16
NSA_WINDOW = 512
NSA_SEL_QCHUNK = 32
NSA_WIDTH = NSA_HEADS * HEAD_DIM
NSA_KV = NSA_GROUPS * HEAD_DIM
NSA_COLS = (NSA_WIDTH, NSA_KV, NSA_KV, NSA_KV, NSA_KV, NSA_KV, NSA_KV, 3 * NSA_HEADS)
NSA_IN = NSA_WIDTH + 6 * NSA_KV + 3 * NSA_HEADS

DIFF_HEADS = 4
DIFF_VDIM = 2 * HEAD_DIM
DIFF_WIDTH = DIFF_HEADS * DIFF_VDIM
DIFF_QK = 2 * DIFF_HEADS * HEAD_DIM
DIFF_COLS = (DIFF_QK, DIFF_QK, DIFF_WIDTH)
DIFF_IN = 2 * DIFF_QK + DIFF_WIDTH

RWKV_HEAD = 64
RWKV_WIDTH = 1024
RWKV_HEADS = RWKV_WIDTH // RWKV_HEAD
RWKV_W_LORA = 64
RWKV_A_LORA = 64
RWKV_G_LORA = 160
RWKV_LNX_EPS = 64e-5
RWKV_COLS = (RWKV_WIDTH, RWKV_WIDTH, RWKV_WIDTH, RWKV_W_LORA, RWKV_A_LORA, RWKV_G_LORA)
RWKV_IN = 3 * RWKV_WIDTH + RWKV_W_LORA + RWKV_A_LORA + RWKV_G_LORA

IN_COLS = NSA_IN + DIFF_IN + RWKV_IN
N_BRANCH = 3
BRANCH_WIDTH = 1024
D_FF = 4 * D_MODEL

kernel_name = "hybrid_nsa_diff_rwkv7_gated_block"


def _split(z, sizes):
    offs = np.cumsum(sizes)[:-1].tolist()
    return jnp.split(z, offs, axis=-1)


def rmsnorm(z, g, eps=RMS_EPS):
    z32 = z.astype(jnp.float32)
    y = z32 * lax.rsqrt(jnp.mean(z32 * z32, axis=-1, keepdims=True) + eps)
    return (y * g.astype(jnp.float32)).astype(z.dtype)


def partial_rope(z, pos):
    half = ROT_DIMS // 2
    inv_freq = ROPE_THETA ** (-jnp.arange(half, dtype=jnp.float32) / half)
    ang = pos.astype(jnp.float32)[..., None] * inv_freq
    ang = ang.reshape(pos.shape + (1,) * (z.ndim - pos.ndim - 1) + (half,))
    cos, sin = jnp.cos(ang), jnp.sin(ang)
    zr = z[..., :ROT_DIMS].astype(jnp.float32)
    z1, z2 = zr[..., :half], zr[..., half:]
    rot = jnp.concatenate([z1 * cos - z2 * sin, z2 * cos + z1 * sin], axis=-1)
    return jnp.concatenate([rot.astype(z.dtype), z[..., ROT_DIMS:]], axis=-1)


def masked_softmax(s, mask):
    s = jnp.where(mask, s.astype(jnp.float32), -1e30)
    return jax.nn.softmax(s, axis=-1) * mask


def nsa_mixer(q, kc, vc, ks, vs, kw, vw, gates, pos, cmp_pos, cmp_w1, cmp_w2):
    B, S = q.shape[:2]
    dh = HEAD_DIM
    scale = dh ** -0.5
    t_idx = jnp.arange(S)
    qg = partial_rope(q, pos).reshape(B, S, NSA_GROUPS, NSA_HPG, dh).transpose(0, 2, 3, 1, 4)

    ratio = NSA_CMP_LEN // NSA_CMP_STRIDE
    n_chunk = S // NSA_CMP_STRIDE
    n_c = n_chunk - ratio + 1

    def compress(z, pe, w1, w2):
        ch = z.reshape(B, n_chunk, NSA_CMP_STRIDE, NSA_GROUPS, dh)
        blk = jnp.concatenate([ch[:, i:i + n_c] for i in range(ratio)], axis=2) + pe[None, None, :, None, :]
        flat = blk.transpose(0, 1, 3, 2, 4).reshape(B, n_c, NSA_GROUPS, NSA_CMP_LEN * dh)
        return jax.nn.gelu(flat @ w1) @ w2

    c_start = jnp.arange(n_c) * NSA_CMP_STRIDE
    cmp_end = c_start + NSA_CMP_LEN - 1
    kcmp = partial_rope(compress(kc, cmp_pos[0], cmp_w1[0], cmp_w2[0]), jnp.take(pos, cmp_end, axis=1))
    vcmp = compress(vc, cmp_pos[1], cmp_w1[1], cmp_w2[1])
    s_c = jnp.einsum('bghqd,bcgd->bghqc', qg, kcmp) * scale
    p_c = masked_softmax(s_c, cmp_end[None, :] <= t_idx[:, None])
    o_cmp = jnp.einsum('bghqc,bcgd->bghqd', p_c.astype(vcmp.dtype), vcmp)

    n_sel = S // NSA_SEL_BLOCK
    top = min(NSA_SEL_TOPK, n_sel)
    j_start = jnp.arange(n_sel) * NSA_SEL_BLOCK
    overlap = ((c_start[:, None] <= j_start[None, :] + NSA_SEL_BLOCK - 1)
               & (cmp_end[:, None] >= j_start[None, :])).astype(jnp.float32)
    imp = jnp.einsum('bghqc,cj->bgqj', p_c, overlap)
    jj = jnp.arange(n_sel)[None, :]
    blk_t = (t_idx // NSA_SEL_BLOCK)[:, None]
    forced = (jj == 0) | (jj == blk_t) | (jj == blk_t - 1)
    imp = jnp.where(forced, 1e9, jnp.where(jj > blk_t, -jnp.inf, imp))
    _, sel_idx = lax.top_k(imp, top)

    k_blk = partial_rope(ks, pos).reshape(B, n_sel, NSA_SEL_BLOCK, NSA_GROUPS, dh).transpose(0, 3, 1, 2, 4)
    v_blk = vs.reshape(B, n_sel, NSA_SEL_BLOCK, NSA_GROUPS, dh).transpose(0, 3, 1, 2, 4)
    n_qc = S // NSA_SEL_QCHUNK
    q_ch = jnp.moveaxis(qg.reshape(B, NSA_GROUPS, NSA_HPG, n_qc, NSA_SEL_QCHUNK, dh), 3, 0)
    i_ch = jnp.moveaxis(sel_idx.reshape(B, NSA_GROUPS, n_qc, NSA_SEL_QCHUNK, top), 2, 0)
    t_ch = t_idx.reshape(n_qc, NSA_SEL_QCHUNK)
    gather = jax.vmap(jax.vmap(lambda blk, ix: blk[ix]))

    def sel_chunk(args):
        qc, ic, tc = args
        kg = gather(k_blk, ic)
        vg = gather(v_blk, ic)
        s = jnp.einsum('bghqd,bgqnld->bghqnl', qc, kg) * scale
        kpos = ic[..., None] * NSA_SEL_BLOCK + jnp.arange(NSA_SEL_BLOCK)
        mask = (kpos <= tc[None, None, :, None, None]).reshape(B, NSA_GROUPS, 1, NSA_SEL_QCHUNK, top * NSA_SEL_BLOCK)
        p = masked_softmax(s.reshape(B, NSA_GROUPS, NSA_HPG, NSA_SEL_QCHUNK, top * NSA_SEL_BLOCK), mask)
        p = p.reshape(B, NSA_GROUPS, NSA_HPG, NSA_SEL_QCHUNK, top, NSA_SEL_BLOCK)
        return jnp.einsum('bghqnl,bgqnld->bghqd', p.astype(vg.dtype), vg)

    o_slc = jnp.moveaxis(lax.map(sel_chunk, (q_ch, i_ch, t_ch)), 0, 3).reshape(B, NSA_GROUPS, NSA_HPG, S, dh)

    pad = ((0, 0), (0, 0), (NSA_WINDOW, 0), (0, 0))
    kw_p = jnp.pad(partial_rope(kw, pos).transpose(0, 2, 1, 3), pad)
    vw_p = jnp.pad(vw.transpose(0, 2, 1, 3), pad)
    n_qb = S // Q_BLOCK
    span = Q_BLOCK + NSA_WINDOW
    q_bl = jnp.moveaxis(qg.reshape(B, NSA_GROUPS, NSA_HPG, n_qb, Q_BLOCK, dh), 3, 0)

    def win_block(args):
        qb, i = args
        start = i * Q_BLOCK
        kb = lax.dynamic_slice_in_dim(kw_p, start, span, axis=2)
        vb = lax.dynamic_slice_in_dim(vw_p, start, span, axis=2)
        tq = start + jnp.arange(Q_BLOCK)
        tk = start - NSA_WINDOW + jnp.arange(span)
        dlt = tq[:, None] - tk[None, :]
        mask = (dlt >= 0) & (dlt < NSA_WINDOW) & (tk[None, :] >= 0)
        p = masked_softmax(jnp.einsum('bghqd,bgkd->bghqk', qb, kb) * scale, mask)
        return jnp.einsum('bghqk,bgkd->bghqd', p.astype(vb.dtype), vb)

    o_win = jnp.moveaxis(lax.map(win_block, (q_bl, jnp.arange(n_qb))), 0, 3).reshape(B, NSA_GROUPS, NSA_HPG, S, dh)

    g = jax.nn.sigmoid(gates.astype(jnp.float32)).reshape(B, S, NSA_GROUPS, NSA_HPG, 3).transpose(0, 2, 3, 1, 4)
    o = g[..., 0:1] * o_cmp + g[..., 1:2] * o_slc + g[..., 2:3] * o_win
    return o.transpose(0, 3, 1, 2, 4).reshape(B, S, NSA_WIDTH).astype(q.dtype)


def diff_mixer(q, k, v, pos, lam, subln, layer):
    B, S = q.shape[:2]
    dh = HEAD_DIM
    q = partial_rope(q.reshape(B, S, DIFF_HEADS, 2, dh), pos).transpose(0, 2, 3, 1, 4)
    k = partial_rope(k.reshape(B, S, DIFF_HEADS, 2, dh), pos).transpose(0, 2, 3, 1, 4)
    v = v.reshape(B, S, DIFF_HEADS, DIFF_VDIM).transpose(0, 2, 1, 3)
    lam_init = 0.8 - 0.6 * math.exp(-0.3 * layer)
    l32 = lam.astype(jnp.float32)
    lam_full = jnp.exp(jnp.sum(l32[0] * l32[1])) - jnp.exp(jnp.sum(l32[2] * l32[3])) + lam_init
    scale = dh ** -0.5
    n_qb = S // Q_BLOCK
    q_bl = jnp.moveaxis(q.reshape(B, DIFF_HEADS, 2, n_qb, Q_BLOCK, dh), 3, 0)
    tk = jnp.arange(S)

    def blk(args):
        qb, i = args
        tq = i * Q_BLOCK + jnp.arange(Q_BLOCK)
        p = masked_softmax(jnp.einsum('bhmqd,bhmkd->bhmqk', qb, k) * scale, tk[None, :] <= tq[:, None])
        a = p[:, :, 0] - lam_full * p[:, :, 1]
        return jnp.einsum('bhqk,bhkd->bhqd', a.astype(v.dtype), v)

    o = jnp.moveaxis(lax.map(blk, (q_bl, jnp.arange(n_qb))), 0, 2).reshape(B, DIFF_HEADS, S, DIFF_VDIM)
    o = rmsnorm(o, subln, 1e-5) * (1.0 - lam_init)
    return o.transpose(0, 2, 1, 3).reshape(B, S, DIFF_WIDTH)


def rwkv_mixer(p, mu, w0, w_up, a0, a_up, g_up, k_k, k_a, r_k, lnx_w, lnx_b):
    B, S = p.shape[:2]
    f32 = jnp.float32
    H, N = RWKV_HEADS, RWKV_HEAD
    prev = jnp.pad(p, ((0, 0), (1, 0), (0, 0)))[:, :-1]
    p = p + (prev - p) * mu
    r, k, v, wd, ad, gd = _split(p, RWKV_COLS)
    w = jnp.exp(-math.exp(-0.5) * jax.nn.sigmoid((w0 + jnp.tanh(wd) @ w_up).astype(f32)))
    a = jax.nn.sigmoid((a0 + ad @ a_up).astype(f32))
    g = jax.nn.sigmoid(gd) @ g_up
    r, k, v = r.astype(f32), k.astype(f32), v.astype(f32)
    heads = lambda z: z.reshape(B, S, H, N)
    kk = heads(k * k_k)
    kk = kk / jnp.maximum(jnp.sqrt(jnp.sum(kk * kk, axis=-1, keepdims=True)), 1e-12)
    k = k * (1.0 + (a - 1.0) * k_a)
    xs = tuple(jnp.moveaxis(z, 1, 0) for z in (heads(r), heads(w), heads(k), heads(v), kk, heads(a)))

    def step(state, inp):
        rt, wt, kt, vt, kkt, at = inp
        sa = jnp.einsum('bhvk,bhk->bhv', state, -kkt)
        state = state * wt[:, :, None, :] + sa[..., None] * (kkt * at)[:, :, None, :] + vt[..., None] * kt[:, :, None, :]
        return state, jnp.einsum('bhvk,bhk->bhv', state, rt)

    _, y = lax.scan(step, jnp.zeros((B, H, N, N), f32), xs)
    y = jnp.moveaxis(y, 0, 1)
    mean = jnp.mean(y, axis=-1, keepdims=True)
    var = jnp.mean(jnp.square(y - mean), axis=-1, keepdims=True)
    y = ((y - mean) * lax.rsqrt(var + RWKV_LNX_EPS)).reshape(B, S, RWKV_WIDTH) * lnx_w + lnx_b
    bonus = jnp.sum(heads(r) * heads(k) * r_k, axis=-1, keepdims=True) * heads(v)
    return ((y + bonus.reshape(B, S, RWKV_WIDTH)) * g).astype(p.dtype)


def setup_inputs(seed: int = 0) -> dict:
    key = jax.random.key(seed)
    ks = iter(jax.random.split(key, 32))
    L, D = DEPTH, D_MODEL
    nrm = lambda shape, scale: scale * jax.random.normal(next(ks), shape, jnp.float32)
    gain = lambda shape: 1.0 + 0.05 * jax.random.normal(next(ks), shape, jnp.float32)
    return {
        "x": nrm((BATCH, SEQ, D), 1.0),
        "positions": jnp.broadcast_to(jnp.arange(SEQ, dtype=jnp.int32), (BATCH, SEQ)),
        "norm_pre_mix": gain((L, D)),
        "norm_post_mix": gain((L, D)),
        "norm_pre_mlp": gain((L, D)),
        "norm_post_mlp": gain((L, D)),
        "w_in": nrm((L, D, IN_COLS), D ** -0.5),
        "nsa_cmp_pos": nrm((L, 2, NSA_CMP_LEN, HEAD_DIM), 0.1),
        "nsa_cmp_w1": nrm((L, 2, NSA_CMP_LEN * HEAD_DIM, NSA_CMP_HIDDEN), (NSA_CMP_LEN * HEAD_DIM) ** -0.5),
        "nsa_cmp_w2": nrm((L, 2, NSA_CMP_HIDDEN, HEAD_DIM), NSA_CMP_HIDDEN ** -0.5),
        "diff_lambda": nrm((L, 4, HEAD_DIM), 0.1),
        "diff_subln": gain((L, DIFF_VDIM)),
        "rwkv_mu": jax.random.uniform(next(ks), (L, RWKV_IN), jnp.float32),
        "rwkv_w0": jax.random.uniform(next(ks), (L, RWKV_WIDTH), jnp.float32, -4.0, 2.0),
        "rwkv_w_up": nrm((L, RWKV_W_LORA, RWKV_WIDTH), 0.1),
        "rwkv_a0": nrm((L, RWKV_WIDTH), 0.5),
        "rwkv_a_up": nrm((L, RWKV_A_LORA, RWKV_WIDTH), 0.1),
        "rwkv_g_up": nrm((L, RWKV_G_LORA, RWKV_WIDTH), RWKV_G_LORA ** -0.5),
        "rwkv_k_k": 0.85 + nrm((L, RWKV_WIDTH), 0.05),
        "rwkv_k_a": gain((L, RWKV_WIDTH)),
        "rwkv_r_k": nrm((L, RWKV_HEADS, RWKV_HEAD), 0.1),
        "rwkv_lnx_w": gain((L, RWKV_WIDTH)),
        "rwkv_lnx_b": nrm((L, RWKV_WIDTH), 0.01),
        "w_gate": nrm((L, N_BRANCH, D, D), D ** -0.5),
        "b_gate": nrm((L, N_BRANCH, D), 0.02),
        "w_branch": nrm((L, N_BRANCH, BRANCH_WIDTH, D), BRANCH_WIDTH ** -0.5),
        "w_out": nrm((L, D, D), D ** -0.5),
        "w_up": nrm((L, D, D_FF), D ** -0.5),
        "w_down": nrm((L, D_FF, D), D_FF ** -0.5),
    }


def reference(x, positions, norm_pre_mix, norm_post_mix, norm_pre_mlp, norm_post_mlp, w_in,
              nsa_cmp_pos, nsa_cmp_w1, nsa_cmp_w2, diff_lambda, diff_subln,
              rwkv_mu, rwkv_w0, rwkv_w_up, rwkv_a0, rwkv_a_up, rwkv_g_up, rwkv_k_k, rwkv_k_a, rwkv_r_k,
              rwkv_lnx_w, rwkv_lnx_b, w_gate, b_gate, w_branch, w_out, w_up, w_down):
    B, S, _ = x.shape
    for l in range(DEPTH):
        u = rmsnorm(x, norm_pre_mix[l])
        p_nsa, p_diff, p_rwkv = _split(u @ w_in[l], (NSA_IN, DIFF_IN, RWKV_IN))
        q, kc, vc, ks, vs, kw, vw, ng = _split(p_nsa, NSA_COLS)
        grp = lambda z: z.reshape(B, S, NSA_GROUPS, HEAD_DIM)
        y_nsa = nsa_mixer(q.reshape(B, S, NSA_HEADS, HEAD_DIM), grp(kc), grp(vc), grp(ks), grp(vs), grp(kw), grp(vw),
                          ng, positions, nsa_cmp_pos[l], nsa_cmp_w1[l], nsa_cmp_w2[l])
        dq, dk, dv = _split(p_diff, DIFF_COLS)
        y_diff = diff_mixer(dq, dk, dv, positions, diff_lambda[l], diff_subln[l], l)
        y_rwkv = rwkv_mixer(p_rwkv, rwkv_mu[l], rwkv_w0[l], rwkv_w_up[l], rwkv_a0[l], rwkv_a_up[l], rwkv_g_up[l],
                            rwkv_k_k[l], rwkv_k_a[l], rwkv_r_k[l], rwkv_lnx_w[l], rwkv_lnx_b[l])
        gate = lambda b: jax.nn.sigmoid((u @ w_gate[l, b] + b_gate[l, b]).astype(jnp.float32)).astype(x.dtype)
        merged = (gate(0) * (y_nsa @ w_branch[l, 0])
                  + gate(1) * (y_diff @ w_branch[l, 1])
                  + gate(2) * (y_rwkv @ w_branch[l, 2]))
        x = x + rmsnorm(merged @ w_out[l], norm_post_mix[l])
        m = rmsnorm(x, norm_pre_mlp[l])
        x = x + rmsnorm(jnp.square(jax.nn.relu(m @ w_up[l])) @ w_down[l], norm_post_mlp[l])
    return x
```

```python
import numpy as np
from contextlib import ExitStack
import concourse.bass as bass
import concourse.mybir as mybir
from concourse.bass_utils import run_bass_kernel_spmd

F32 = mybir.dt.float32
BF16 = mybir.dt.bfloat16
I32 = mybir.dt.int32
AF = mybir.ActivationFunctionType
ALU = mybir.AluOpType
AX = mybir.AxisListType

SEM_LIM = 16000
NDMA = 24


class Prog:
    ENG = ("pe", "dve", "act", "pool", "sp")

    def __init__(self, name="k"):
        self.nc = bass.Bass("TRN2", target_bir_lowering=False)
        self.es = ExitStack()
        nc = self.nc
        self.eng = {"pe": nc.tensor, "dve": nc.vector, "act": nc.scalar, "pool": nc.gpsimd, "sp": nc.sync}
        self.ops = {e: [] for e in self.ENG}
        self.sem = {}
        self.cnt = {}
        self.nsem = 0
        self.pesems = set()
        for e in self.ENG:
            self._newsem(e)
        self.lastw = {}
        self.rd = {}
        self.waited = {e: {} for e in self.ENG}
        self.dslots = []
        for i in range(NDMA):
            s = self.es.enter_context(nc.semaphore(f"dq{i}"))
            self.dslots.append([s, 0])
        self.dnext = 0
        self.ntile = 0
        self.psn = 0

    def _newsem(self, e):
        self.nsem += 1
        s = self.es.enter_context(self.nc.semaphore(f"s{e}{self.nsem}"))
        self.sem[e] = s
        self.cnt[e] = 0
        if e == "pe":
            self.pesems.add(s.name if hasattr(s, "name") else id(s))

    def dram(self, name, shape, dt, kind):
        return self.nc.dram_tensor(name, list(shape), dt, kind=kind).ap()

    def sb(self, shape, dt, name=None):
        self.ntile += 1
        name = name or f"t{self.ntile}"
        return self.es.enter_context(self.nc.sbuf_tensor(name, list(shape), dt))

    def ps(self, shape, dt=F32, name=None):
        self.psn += 1
        name = name or f"ps{self.psn}"
        return self.es.enter_context(self.nc.psum_tensor(name, list(shape), dt))

    @staticmethod
    def _sk(s):
        return s.name if hasattr(s, "name") else id(s)

    def op(self, e, fn, r=(), w=(), dma=False):
        waits = {}

        def need(ev):
            if ev is None:
                return
            s, v = ev
            k = self._sk(s)
            if k not in waits or waits[k][1] < v:
                waits[k] = (s, v)

        for k in r:
            need(self.lastw.get(k))
        for k in w:
            need(self.lastw.get(k))
            for ev in self.rd.get(k, {}).values():
                need(ev)
        if dma:
            slot = self.dslots[self.dnext % NDMA]
            self.dnext += 1
            if slot[1] > 0:
                need((slot[0], slot[1]))
            slot[1] += 16
            ev = (slot[0], slot[1])
            inc = (slot[0], 16)
        else:
            if self.cnt[e] >= SEM_LIM:
                self._newsem(e)
            self.cnt[e] += 1
            ev = (self.sem[e], self.cnt[e])
            inc = (self.sem[e], 1)
        wl = []
        for k, (s, v) in waits.items():
            if e == "pe" and k in self.pesems:
                continue
            if self.waited[e].get(k, 0) >= v:
                continue
            self.waited[e][k] = v
            wl.append((s, v))
        self.ops[e].append((wl, fn, inc))
        evk = self._sk(ev[0])
        for k in r:
            self.rd.setdefault(k, {})[evk] = ev
        for k in w:
            self.lastw[k] = ev
            self.rd[k] = {}
        return ev

    def dma(self, q, out, in_, r=(), w=()):
        return self.op(q, lambda e: e.dma_start(out=out, in_=in_), r=r, w=w, dma=True)

    def build(self):
        nc = self.nc
        ops = self.ops
        dslots = self.dslots

        def run(e, lst, final=False):
            for wl, fn, inc in lst:
                for s, v in wl:
                    e.wait_ge(s, v)
                fn(e).then_inc(inc[0], inc[1])
            if final:
                for s, v in dslots:
                    if v > 0:
                        e.wait_ge(s, v)

        with nc.Block() as block:
            @block.tensor
            def _(e):
                run(e, ops["pe"])

            @block.vector
            def _(e):
                run(e, ops["dve"])

            @block.scalar
            def _(e):
                run(e, ops["act"])

            @block.gpsimd
            def _(e):
                run(e, ops["pool"])

            @block.sync
            def _(e):
                run(e, ops["sp"], final=True)
        self.es.close()
        return nc

    def counts(self):
        return {e: len(v) for e, v in self.ops.items()}


D = 2048
S = 4096
KT = D // 128
TC = 512
NCH = S // TC

NSA_OFF, DIFF_OFF, RWKV_OFF = 0, 2584, 5656


def core_cols(h):
    c = []
    r = lambda a, n: list(range(a, a + n))
    for hd in range(4 * h, 4 * h + 4):
        c += r(NSA_OFF + hd * 128, 128)
    for i in range(6):
        c += r(NSA_OFF + 1024 + i * 256 + h * 128, 128)
    for base in (DIFF_OFF, DIFF_OFF + 1024):
        c += r(base + 2 * h * 256, 512)
    c += r(DIFF_OFF + 2048 + 2 * h * 256, 512)
    for i in range(3):
        c += r(RWKV_OFF + i * 1024 + h * 512, 512)
    c += r(RWKV_OFF + 3072, 64 + 64 + 160)
    c += r(NSA_OFF + 2560 + 12 * h, 12)
    return np.array(c)


NCOL = 4652
NT = (NCOL + 127) // 128


def emit_rmsnorm_T(p, xT, gcol, uT, ones_f, ps_bank, pskey, stage, t0=0, ntok=S, qs=("sp", "act"), D_=D, eps=1e-6, tag="n"):
    kt = D_ // 128
    xv = xT.rearrange("(k q) t -> q k t", q=128)
    xs, sq, rs = stage
    nsq = 4
    half = kt // 2
    for c in range(ntok // TC):
        tsl = slice(c * TC, (c + 1) * TC)
        gsl = slice(t0 + c * TC, t0 + (c + 1) * TC)
        p.dma(qs[0], xs[:, 0:half, :], xv[:, 0:half, gsl], w=[f"{tag}xs_a"])
        p.dma(qs[1], xs[:, half:kt, :], xv[:, half:kt, gsl], w=[f"{tag}xs_b"])
        for k in range(kt):
            j = k % nsq
            xk = f"{tag}xs_a" if k < half else f"{tag}xs_b"
            if k % 2 == 0:
                p.op("act", lambda e, k=k, j=j: e.activation(out=sq[:, j, :], in_=xs[:, k, :], func=AF.Square), r=[xk], w=[f"{tag}sq{j}"])
            else:
                p.op("pool", lambda e, k=k, j=j: e.tensor_tensor(out=sq[:, j, :], in0=xs[:, k, :], in1=xs[:, k, :], op=ALU.mult), r=[xk], w=[f"{tag}sq{j}"])
            p.op("pe", lambda e, k=k, j=j: e.matmul(ps_bank[:, 0:TC], lhsT=ones_f[:, :], rhs=sq[:, j, :], start=(k == 0), stop=(k == kt - 1)),
                 r=[f"{tag}sq{j}", "ones_f"], w=[pskey])
        p.op("dve", lambda e: e.tensor_scalar(out=rs[:, :], in0=ps_bank[:, 0:TC], scalar1=1.0 / D_, scalar2=eps, op0=ALU.mult, op1=ALU.add),
             r=[pskey], w=[f"{tag}rs"])
        p.op("act", lambda e: e.sqrt(out=rs[:, :], in_=rs[:, :]), r=[f"{tag}rs"], w=[f"{tag}rs"])
        p.op("dve", lambda e: e.reciprocal(out=rs[:, :], in_=rs[:, :]), r=[f"{tag}rs"], w=[f"{tag}rs"])
        for k in range(kt):
            p.op("dve", lambda e, k=k, tsl=tsl: e.scalar_tensor_tensor(out=uT[:, k, tsl], in0=xs[:, k, :], scalar=gcol[:, k:k + 1], in1=rs[:, :],
                                                                     op0=ALU.mult, op1=ALU.mult),
                 r=[f"{tag}xs_a" if k < half else f"{tag}xs_b", f"{tag}rs", "gcol"], w=[f"uT{c}"])


def emit_gemm_T(p, uT, ukeys, wdram, ncols, out_fn, wst, wbf, pss, kt=KT, ntok=S, tag="g", evac=None):
    wv = wdram.rearrange("(k q) n -> q k n", q=128)
    nt_n = (ncols + 127) // 128
    nch = ntok // TC
    psi = 0
    for nt in range(nt_n):
        m = min(128, ncols - nt * 128)
        b = nt % 2
        q = ("sp", "act")[nt % 2]
        p.dma(q, wst[b][:, :, 0:m], wv[:, :, nt * 128:nt * 128 + m], w=[f"{tag}wst{b}"])
        ceng = ("pool", "dve")[nt % 2]
        p.op(ceng, lambda e, b=b, m=m: e.tensor_copy(out=wbf[b][:, :, 0:m], in_=wst[b][:, :, 0:m]),
             r=[f"{tag}wst{b}"], w=[f"{tag}wbf{b}"])
        for c in range(nch):
            ps, pk = pss[psi % len(pss)]
            psi += 1
            for k in range(kt):
                p.op("pe", lambda e, k=k, b=b, m=m, ps=ps, c=c: e.matmul(ps[0:m, 0:TC], lhsT=wbf[b][:, k, 0:m], rhs=uT[:, k, c * TC:(c + 1) * TC],
                                                                        start=(k == 0), stop=(k == kt - 1)),
                     r=[f"{tag}wbf{b}", ukeys(c)], w=[pk])
            out_fn(nt, m, c, ps, pk)


HT = 2048


def build_A():
    p = Prog("A")
    xT = p.dram("xT", [D, S], F32, "ExternalInput")
    w = p.dram("w", [D, NCOL], F32, "ExternalInput")
    g = p.dram("g", [128, KT], F32, "ExternalInput")
    PT = p.dram("PT", [NCOL, S], F32, "ExternalOutput")
    uT = p.sb([128, KT, HT], BF16, "uT")
    gcol = p.sb([128, KT], F32, "gcol")
    ones_f = p.sb([128, 128], F32, "ones_f")
    xs = p.sb([128, KT, TC], F32, "xs")
    sq = p.sb([128, 4, TC], F32, "sq")
    rs = p.sb([128, TC], F32, "rs")
    wst = [p.sb([128, KT, 128], F32, f"wst{i}") for i in range(2)]
    wbf = [p.sb([128, KT, 128], BF16, f"wbf{i}") for i in range(2)]
    ost = [p.sb([128, HT], F32, f"ost{i}") for i in range(2)]
    pss = [(p.ps([128, 512], F32, f"psb{i}"), f"psb{i}") for i in range(6)]
    psn = (p.ps([128, 512], F32, "psn"), "psn")
    p.dma("sp", gcol[:, :], g[:, :], w=["gcol"])
    p.op("dve", lambda e: e.memset(ones_f[:, :], 1.0), w=["ones_f"])
    with p.nc.allow_low_precision("bf16 matmul operands, fp32 accumulation"):
        for hp in range(S // HT):
            emit_rmsnorm_T(p, xT, gcol, uT, ones_f, psn[0], psn[1], (xs, sq, rs), t0=hp * HT, ntok=HT)

            def out_fn(nt, m, c, ps, pk, hp=hp):
                b = nt % 2
                if c % 2 == 0:
                    p.op("act", lambda e: e.copy(out=ost[b][0:m, c * TC:(c + 1) * TC], in_=ps[0:m, 0:TC]), r=[pk], w=[f"ost{b}"])
                else:
                    p.op("dve", lambda e: e.tensor_copy(out=ost[b][0:m, c * TC:(c + 1) * TC], in_=ps[0:m, 0:TC]), r=[pk], w=[f"ost{b}"])
                if c == HT // TC - 1:
                    p.dma("pool", PT[nt * 128:nt * 128 + m, hp * HT:(hp + 1) * HT], ost[b][0:m, :], r=[f"ost{b}"], w=[f"PT{nt}_{hp}"])

            emit_gemm_T(p, uT, lambda c: f"uT{c}", w, NCOL, out_fn, wst, wbf, pss, ntok=HT)
    print("A op counts", p.counts())
    return p.build()


def run_A(x, w_in_l, g_l):
    nc = build_A()
    in_maps = []
    for c in range(8):
        b, h = c // 2, c % 2
        in_maps.append({
            "xT": np.ascontiguousarray(x[b].T),
            "w": np.ascontiguousarray(w_in_l[:, core_cols(h)]),
            "g": np.ascontiguousarray(g_l.reshape(KT, 128).T),
        })
    res = run_bass_kernel_spmd(nc, in_maps, core_ids=list(range(8)))
    return [r["PT"] for r in res.results]


import math

S = 4096
TC = 512
NQC = S // TC
PI = math.pi
SCALE = 128 ** -0.5


def rope_consts():
    half = 16
    invf = (500000.0 ** (-np.arange(half, dtype=np.float32) / half)).astype(np.float32)
    c = np.zeros((32, 4), np.float32)
    c[:, 0] = np.concatenate([invf, invf])
    c[:16, 1] = -1.0; c[16:, 1] = 1.0
    c[:16, 2] = -PI; c[16:, 2] = PI
    c[:, 3] = PI
    return c


def causal_masks():
    m = np.zeros((4, 128, 512), np.float32)
    p = np.arange(128)[:, None]; f = np.arange(512)[None, :]
    for j in range(4):
        m[j] = (128 * j + p <= f)
    return m


def emit_sincos(p, ang, angk, C32, S32, tmp_i, tmp2, tmp2k, rc):
    H = tmp_i.shape[1]
    for dst, dk, shift in ((S32, "S32", 0.0), (C32, "C32", PI / 2)):
        for hf in range(S // H):
            sl = slice(hf * H, (hf + 1) * H)
            a = ang[:, sl]; d = dst[:, sl]; t2 = tmp2[:, sl]
            p.op("dve", lambda e, a=a, d=d, shift=shift: e.tensor_scalar(out=d, in0=a, scalar1=shift, scalar2=None, op0=ALU.add), r=[angk], w=[dk])
            p.op("dve", lambda e, d=d, t2=t2: e.tensor_scalar(out=t2, in0=d, scalar1=1.0 / (2 * PI), scalar2=None, op0=ALU.mult), r=[dk], w=[tmp2k])
            p.op("dve", lambda e, t2=t2: e.tensor_copy(out=tmp_i[:, :], in_=t2), r=[tmp2k], w=["tmp_i"])
            p.op("dve", lambda e, t2=t2: e.tensor_copy(out=t2, in_=tmp_i[:, :]), r=["tmp_i"], w=[tmp2k])
            p.op("dve", lambda e, d=d, t2=t2: e.scalar_tensor_tensor(out=d, in0=t2, scalar=-2 * PI, in1=d, op0=ALU.mult, op1=ALU.add), r=[tmp2k, dk], w=[dk])
            p.op("dve", lambda e, d=d, t2=t2: e.tensor_single_scalar(out=t2, in_=d, scalar=PI, op=ALU.is_gt), r=[dk], w=[tmp2k])
            p.op("dve", lambda e, d=d, t2=t2: e.scalar_tensor_tensor(out=d, in0=t2, scalar=-2 * PI, in1=d, op0=ALU.mult, op1=ALU.add), r=[tmp2k, dk], w=[dk])
            p.op("dve", lambda e, d=d, t2=t2: e.tensor_single_scalar(out=t2, in_=d, scalar=-PI, op=ALU.is_lt), r=[dk], w=[tmp2k])
            p.op("dve", lambda e, d=d, t2=t2: e.scalar_tensor_tensor(out=d, in0=t2, scalar=2 * PI, in1=d, op0=ALU.mult, op1=ALU.add), r=[tmp2k, dk], w=[dk])
    p.op("act", lambda e: e.activation(out=S32[:, :], in_=S32[:, :], func=AF.Sin, scale=rc[:, 1:2]), r=["S32", "rc"], w=["S32"])
    p.op("act", lambda e: e.activation(out=C32[:, :], in_=C32[:, :], func=AF.Sin), r=["C32"], w=["C32"])


def emit_rope_load(p, zT_dram, zsw_dram, dest_bf, dkey, stage, stsw, C32, S32, tag, q="sp"):
    p.dma(q, stage[:, :], zT_dram, w=[f"{tag}st"])
    p.dma("act" if q == "sp" else "sp", stsw[:, :], zsw_dram, w=["sw"])
    p.op("dve", lambda e: e.tensor_tensor(out=stage[0:32, :], in0=stage[0:32, :], in1=C32[:, :], op=ALU.mult), r=[f"{tag}st", "C32"], w=[f"{tag}st"])
    p.op("pool", lambda e: e.tensor_tensor(out=stsw[:, :], in0=stsw[:, :], in1=S32[:, :], op=ALU.mult), r=["sw", "S32"], w=["sw"])
    p.op("dve", lambda e: e.tensor_tensor(out=stage[0:32, :], in0=stage[0:32, :], in1=stsw[:, :], op=ALU.add), r=[f"{tag}st", "sw"], w=[f"{tag}st"])
    p.op("act", lambda e: e.copy(out=dest_bf, in_=stage[:, :]), r=[f"{tag}st"], w=[dkey])


def build_diff():
    p = Prog("diff")
    qT = p.dram("qT", [4, 128, S], F32, "ExternalInput")
    kT = p.dram("kT", [4, 128, S], F32, "ExternalInput")
    qsw = p.dram("qsw", [4, 32, S], F32, "ExternalInput")
    ksw = p.dram("ksw", [4, 32, S], F32, "ExternalInput")
    v = p.dram("v", [2, S, 256], F32, "ExternalInput")
    pos = p.dram("pos", [32, S], I32, "ExternalInput")
    rcd = p.dram("rc", [32, 4], F32, "ExternalInput")
    lamd = p.dram("lam", [128, 4, 128], F32, "ExternalInput")
    sublnd = p.dram("subln", [128, 256], F32, "ExternalInput")
    cmd = p.dram("cmask", [4, 128, 512], F32, "ExternalInput")
    lamcd = p.dram("lamc", [128, 2], F32, "ExternalInput")
    y = p.dram("y", [2, S, 256], F32, "ExternalOutput")

    QT = p.sb([128, 4, S], BF16, "QT")
    KT_ = p.sb([128, 4, S], BF16, "KT")
    VA = p.sb([128, 2, 32, 257], BF16, "VA")
    stage = [p.sb([128, S], F32, f"stage{i}") for i in range(2)]
    stsw = [p.sb([32, S], F32, "stsw0")] * 2
    C32 = p.sb([32, S], F32, "C32")
    S32 = p.sb([32, S], F32, "S32")
    tmp_i = p.sb([32, S // 2], I32, "tmp_i")
    rc = p.sb([32, 4], F32, "rc_sb")
    lam = p.sb([128, 4, 128], F32, "lam_sb")
    lamw = p.sb([128, 8], F32, "lamw")
    lamc = p.sb([128, 2], F32, "lamc_sb")
    subln = p.sb([128, 256], F32, "subln_sb")
    cm = p.sb([128, 4, 512], BF16, "cm")
    E = [p.sb([128, 512], BF16, f"E{i}") for i in range(3)]
    o1 = p.sb([128, 4, 256], F32, "o1")
    o2 = p.sb([128, 256], F32, "o2")
    sqj = p.sb([128, 256], F32, "sqj")
    small = p.sb([128, 8], F32, "small")
    yst = [p.sb([128, 256], F32, f"yst{i}") for i in range(2)]
    pss = [(p.ps([128, 512], F32, f"pss{i}"), f"pss{i}") for i in range(3)]
    acc = [(p.ps([128, 512], F32, f"acc{i}"), f"acc{i}") for i in range(4)]

    with p.nc.allow_low_precision("bf16 matmul operands, fp32 accumulation"):
        p.dma("sp", rc[:, :], rcd, w=["rc"])
        p.dma("act", lam[:, :, :], lamd, w=["lam"])
        p.dma("act", lamc[:, :], lamcd, w=["lamc"])
        p.dma("act", subln[:, :], sublnd, w=["subln"])
        tmp_f = stage[0][0:32, :]
        for hf in range(2):
            p.dma("sp", tmp_i[:, :], pos[:, hf * 2048:(hf + 1) * 2048], w=["tmp_i"])
            p.op("dve", lambda e, hf=hf: e.tensor_copy(out=stage[0][0:32, hf * 2048:(hf + 1) * 2048], in_=tmp_i[:, :]), r=["tmp_i"], w=["qst"])
        p.op("dve", lambda e: e.tensor_scalar(out=tmp_f, in0=tmp_f, scalar1=rc[:, 0:1], scalar2=None, op0=ALU.mult), r=["qst", "rc"], w=["qst"])
        emit_sincos(p, tmp_f, "qst", C32, S32, tmp_i, stage[1][0:32, :], "kst", rc)
        p.op("dve", lambda e: e.tensor_tensor(out=lam[:, 0, :], in0=lam[:, 0, :], in1=lam[:, 1, :], op=ALU.mult), r=["lam"], w=["lam"])
        p.op("dve", lambda e: e.tensor_tensor(out=lam[:, 2, :], in0=lam[:, 2, :], in1=lam[:, 3, :], op=ALU.mult), r=["lam"], w=["lam"])
        p.op("dve", lambda e: e.reduce_sum(out=lamw[:, 0:1], in_=lam[:, 0, :], axis=AX.X), r=["lam"], w=["lamw"])
        p.op("dve", lambda e: e.reduce_sum(out=lamw[:, 1:2], in_=lam[:, 2, :], axis=AX.X), r=["lam"], w=["lamw"])
        p.op("act", lambda e: e.activation(out=lamw[:, 2:4], in_=lamw[:, 0:2], func=AF.Exp), r=["lamw"], w=["lamw"])
        p.op("dve", lambda e: e.tensor_tensor(out=lamw[:, 4:5], in0=lamw[:, 3:4], in1=lamw[:, 2:3], op=ALU.subtract), r=["lamw"], w=["lamw"])
        p.op("dve", lambda e: e.tensor_scalar(out=lamw[:, 5:6], in0=lamw[:, 4:5], scalar1=lamc[:, 0:1], scalar2=None, op0=ALU.add), r=["lamw", "lamc"], w=["lamw"])
        p.op("dve", lambda e: e.tensor_scalar(out=subln[:, :], in0=subln[:, :], scalar1=lamc[:, 1:2], scalar2=None, op0=ALU.mult), r=["subln", "lamc"], w=["subln"])
        for j in range(4):
            p.dma("sp", stage[1][:, j * 512:(j + 1) * 512], cmd[j], w=["kst"])
        p.op("dve", lambda e: e.tensor_copy(out=cm[:, :, :].rearrange("p a b -> p (a b)"), in_=stage[1][:, 0:2048]), r=["kst"], w=["cm"])
        for i in range(4):
            emit_rope_load(p, qT[i], qsw[i], QT[:, i, :], f"QT{i}", stage[0], stsw[0], C32, S32, "q", q="sp")
            emit_rope_load(p, kT[i], ksw[i], KT_[:, i, :], f"KT{i}", stage[1], stsw[1], C32, S32, "k", q="act")
        for j in range(2):
            for hh in range(2):
                st = stage[hh]
                key = ("qst", "kst")[hh]
                p.dma(("sp", "act")[hh], st[:, :].rearrange("p (a d) -> p a d", d=256),
                      v[j, hh * 2048:(hh + 1) * 2048, :].rearrange("(a p) d -> p a d", p=128), w=[key])
                p.op(("dve", "pool")[hh], lambda e, st=st, j=j, hh=hh: e.tensor_copy(out=VA[:, j, hh * 16:(hh + 1) * 16, 0:256],
                                                                                    in_=st[:, :].rearrange("p (a d) -> p a d", d=256)), r=[key], w=[f"VA{j}"])
            p.op("pool", lambda e, j=j: e.memset(VA[:, j, :, 256:257], 1.0), w=[f"VA{j}"])

        ei = 0
        psi = 0
        yi = 0
        for j in range(2):
            for qc in range(NQC):
                for m in range(2):
                    jm = 2 * j + m
                    nkt = 4 * qc + 4
                    for kt in range(nkt):
                        ps, pk = pss[psi % 3]; psi += 1
                        Eb = E[ei % 3]; ek = f"E{ei % 3}"; ei += 1
                        p.op("pe", lambda e, ps=ps, kt=kt, jm=jm, qc=qc: e.matmul(ps[:, :], lhsT=KT_[:, jm, kt * 128:(kt + 1) * 128], rhs=QT[:, jm, qc * TC:(qc + 1) * TC], start=True, stop=True),
                             r=[f"KT{jm}", f"QT{jm}"], w=[pk])
                        p.op("act", lambda e, ps=ps, Eb=Eb: e.activation(out=Eb[:, :], in_=ps[:, :], func=AF.Exp, scale=SCALE), r=[pk], w=[ek])
                        dj = kt - 4 * qc
                        if dj >= 0:
                            p.op("pool", lambda e, Eb=Eb, dj=dj: e.tensor_tensor(out=Eb[:, :], in0=Eb[:, :], in1=cm[:, dj, :], op=ALU.mult), r=[ek, "cm"], w=[ek])
                        for qs in range(4):
                            if dj > qs:
                                continue
                            first = (kt == 0)
                            last = (kt == min(nkt - 1, 4 * qc + qs))
                            a, ak = acc[qs]
                            p.op("pe", lambda e, a=a, Eb=Eb, qs=qs, kt=kt, j=j, first=first, last=last: e.matmul(a[:, 0:257], lhsT=Eb[:, qs * 128:(qs + 1) * 128], rhs=VA[:, j, kt, :], start=first, stop=last),
                                 r=[ek, f"VA{j}"], w=[ak])
                    for qs in range(4):
                        a, ak = acc[qs]
                        p.op("dve", lambda e, a=a: e.reciprocal(out=small[:, 0:1], in_=a[:, 256:257]), r=[ak], w=["small"])
                        if m == 0:
                            p.op("dve", lambda e, a=a, qs=qs: e.tensor_scalar(out=o1[:, qs, :], in0=a[:, 0:256], scalar1=small[:, 0:1], scalar2=None, op0=ALU.mult), r=[ak, "small"], w=[f"o1_{qs}"])
                        else:
                            p.op("dve", lambda e, a=a: e.tensor_scalar(out=o2[:, :], in0=a[:, 0:256], scalar1=small[:, 0:1], scalar2=None, op0=ALU.mult), r=[ak, "small"], w=["o2"])
                            p.op("dve", lambda e, qs=qs: e.scalar_tensor_tensor(out=o2[:, :], in0=o2[:, :], scalar=lamw[:, 5:6], in1=o1[:, qs, :], op0=ALU.mult, op1=ALU.add), r=["o2", "lamw", f"o1_{qs}"], w=["o2"])
                            p.op("pool", lambda e: e.tensor_tensor(out=sqj[:, :], in0=o2[:, :], in1=o2[:, :], op=ALU.mult), r=["o2"], w=["sqj"])
                            p.op("dve", lambda e: e.reduce_sum(out=small[:, 1:2], in_=sqj[:, :], axis=AX.X), r=["sqj"], w=["small"])
                            p.op("dve", lambda e: e.tensor_scalar(out=small[:, 2:3], in0=small[:, 1:2], scalar1=1.0 / 256, scalar2=1e-5, op0=ALU.mult, op1=ALU.add), r=["small"], w=["small"])
                            p.op("act", lambda e: e.sqrt(out=small[:, 2:3], in_=small[:, 2:3]), r=["small"], w=["small"])
                            p.op("dve", lambda e: e.reciprocal(out=small[:, 3:4], in_=small[:, 2:3]), r=["small"], w=["small"])
                            ys = yst[yi % 2]; yk = f"yst{yi % 2}"; yi += 1
                            p.op("dve", lambda e, ys=ys: e.scalar_tensor_tensor(out=ys[:, :], in0=o2[:, :], scalar=small[:, 3:4], in1=subln[:, :], op0=ALU.mult, op1=ALU.mult), r=["o2", "small", "subln"], w=[yk])
                            r0 = qc * TC + qs * 128
                            p.dma("sp", y[j, r0:r0 + 128, :], ys[:, :], r=[yk], w=[f"y{j}_{qc}_{qs}"])
    print("diff op counts", p.counts())
    return p.build()


def diff_inputs(PTc, pos_b, lam_l, subln_l, layer=0):
    T = lambda t: PTc[t * 128:(t + 1) * 128]
    sw = lambda a: np.concatenate([a[16:32], a[0:16]], 0)
    qT = np.stack([T(10 + i) for i in range(4)])
    kT = np.stack([T(14 + i) for i in range(4)])
    v = np.stack([np.ascontiguousarray(PTc[(18 + 2 * j) * 128:(20 + 2 * j) * 128].T) for j in range(2)])
    return {
        "qT": np.ascontiguousarray(qT), "kT": np.ascontiguousarray(kT),
        "qsw": np.stack([sw(a) for a in qT]), "ksw": np.stack([sw(a) for a in kT]),
        "v": v, "pos": np.ascontiguousarray(np.broadcast_to(pos_b[None, :], (32, S))).astype(np.int32),
        "rc": rope_consts(), "lam": np.ascontiguousarray(np.broadcast_to(lam_l[None], (128, 4, 128))),
        "subln": np.ascontiguousarray(np.broadcast_to(subln_l[None], (128, 256))), "cmask": causal_masks(),
        "lamc": np.ascontiguousarray(np.broadcast_to(np.array([[-(0.8 - 0.6 * math.exp(-0.3 * layer)), 1.0 - (0.8 - 0.6 * math.exp(-0.3 * layer))]], np.float32), (128, 2))),
    }


import math

NC_ = 255
GC0 = math.sqrt(2.0 / math.pi)


def nsa_consts():
    p = np.arange(128)[:, None]; f = np.arange(512)[None, :]
    m8 = np.zeros((8, 128, 512), np.float32)
    for dj in range(-4, 4):
        if dj >= 0:
            m8[dj + 4] = (128 * dj + p <= f)
        else:
            m8[dj + 4] = (f < 512 + 128 * dj + p)
    cmpm = np.zeros((2, 8, 128, 512), np.float32)
    for ct in range(2):
        for qc in range(8):
            cmpm[ct, qc] = (16 * (128 * ct + p) + 31 <= 512 * qc + f)
    c = np.arange(256)[:, None]; j = np.arange(64)[None, :]
    ov = ((16 * c <= 64 * j + 63) & (16 * c + 31 >= 64 * j)).astype(np.float32)
    ov[255] = 0
    t = np.arange(S)[:, None]; bt = t // 64
    addm = np.zeros((S, 64), np.float32)
    addm[(j == bt - 1) & (np.ones_like(t) > 0)] = 3e9
    addm[(j == bt) & (np.ones_like(t) > 0)] = 2e9
    addm[(j == 0) & (np.ones_like(t) > 0)] = 1e9
    addm[(j > bt) & (np.ones_like(t) > 0)] = -1e30
    k = np.arange(128)[None, None, :]; kt = np.arange(32)[None, :, None]; jj = np.arange(64)[:, None, None]
    eall = ((kt * 128 + k) // 64 == jj).astype(np.float32)
    eall_p = np.zeros((128, 32, 128), np.float32); eall_p[:64] = eall
    return dict(m8=m8, cmpm=cmpm, ov=ov.reshape(2, 128, 64), addm=np.ascontiguousarray(addm.reshape(32, 128, 64).transpose(1, 0, 2)),
                eall=eall_p, ident=np.eye(128, dtype=np.float32))


def build_nsa():
    p = Prog("nsa")
    D_ = lambda n, s, dt=F32: p.dram(n, s, dt, "ExternalInput")
    qT = D_("qT", [4, 128, S]); qsw = D_("qsw", [4, 32, S])
    kcvT = D_("kcvT", [2, 128, S])
    ksT = D_("ksT", [2, 128, S]); kssw = D_("kssw", [2, 32, S])
    vsw = D_("vsw", [2, S, 128])
    ngd = D_("ng", [S, 12])
    pos = D_("pos", [32, S], I32); rcd = D_("rc", [32, 4])
    ped = D_("pe", [2, 128, 32])
    w1d = D_("w1", [2, 4096, 256]); w2d = D_("w2", [2, 256, 128]); w2swd = D_("w2sw", [256, 32])
    m8d = D_("m8", [8, 128, 512]); cmpmd = D_("cmpm", [2, 8, 128, 512]); ovd = D_("ov", [2, 128, 64])
    addmd = D_("addm", [128, 32, 64]); ealld = D_("eall", [128, 32, 128]); identd = D_("ident", [128, 128])
    y = p.dram("y", [S, 512], F32, "ExternalOutput")

    QT = p.sb([128, 4, S], BF16, "QT")
    KS = p.sb([128, 2, S], BF16, "KS")
    KC = p.sb([128, 256], BF16, "KC")
    VS = p.sb([128, 2, 32, 129], BF16, "VS")
    VC = p.sb([128, 2, 193], BF16, "VC")
    SA = p.sb([128, 8192], F32, "SA")
    SB_ = p.sb([128, 8192], F32, "SB")
    stage = [SA[:, 0:4096], SA[:, 4096:8192]]
    SAb = SA[:, :].bitcast(BF16)
    MSK = SAb.rearrange("p (a b) -> p a b", b=512)
    X = SAb[:, 8192:16384].rearrange("p (a b) -> p a b", b=256)
    C32 = SB_[0:32, 0:4096]; S32 = SB_[0:32, 4096:8192]
    YACC = SB_[:, 0:2048].rearrange("p (a b c) -> p a b c", a=4, b=4)
    ADDM = SB_[:, 2048:4096].rearrange("p (a b) -> p a b", b=64)
    M8 = SB_[:, 4096:6144].bitcast(BF16).rearrange("p (a b) -> p a b", b=512)
    EALL = SB_[:, 6144:8192].bitcast(BF16).rearrange("p (a b) -> p a b", b=128)
    stsw = p.sb([32, S], F32, "stsw")
    tmp_i = p.sb([32, S // 4], I32, "tmp_i")
    rc = p.sb([32, 4], F32, "rc_sb")
    pe = p.sb([128, 2, 32], F32, "pe_sb")
    w1st = [p.sb([128, 8, 256], F32, f"w1st{i}") for i in range(2)]
    w1bf = [p.sb([128, 8, 256], BF16, f"w1bf{i}") for i in range(2)]
    w2st = p.sb([128, 2, 2, 128], F32, "w2st")
    w2bf = p.sb([128, 2, 2, 128], BF16, "w2bf")
    w2swst = p.sb([128, 2, 32], F32, "w2swst")
    w2swbf = p.sb([128, 2, 32], BF16, "w2swbf")
    hx = p.sb([128, 256], F32, "hx")
    ht_ = p.sb([128, 256], F32, "ht")
    hT = p.sb([128, 2, 2, 256], BF16, "hT")
    krot = p.sb([128, 256], F32, "krot")
    ksw_s = p.sb([32, 256], F32, "ksw_s")
    ovst = p.sb([128, 2, 64], F32, "ovst")
    ident = p.sb([128, 128], F32, "ident_sb")
    NG = p.sb([128, 32, 12], F32, "NG")
    G = p.sb([128, 32, 12], F32, "G")
    cmst = [p.sb([128, 512], F32, f"cmst{i}") for i in range(2)]
    E = [p.sb([128, 512], BF16, f"E{i}") for i in range(3)]
    IMP = p.sb([128, 4, 64], F32, "IMP")
    IMPW = p.sb([128, 64], F32, "IMPW")
    SEL = p.sb([128, 64], F32, "SEL")
    SELT = p.sb([64, 512], BF16, "SELT")
    m8a = p.sb([128, 8], F32, "m8a")
    m8b = p.sb([128, 8], F32, "m8b")
    small = p.sb([128, 8], F32, "small")
    otmp = p.sb([128, 128], F32, "otmp")
    pss = [(p.ps([128, 512], F32, f"pss{i}"), f"pss{i}") for i in range(3)]
    acc = [(p.ps([128, 512], F32, f"acc{i}"), f"acc{i}") for i in range(4)]
    psm = (p.ps([128, 512], F32, "psm"), "psm")

    cnt = dict(e=0, ps=0, cm=0)

    def nextE():
        i = cnt["e"] % 3; cnt["e"] += 1
        return E[i], f"E{i}"

    def nextps():
        i = cnt["ps"] % 3; cnt["ps"] += 1
        return pss[i]

    with p.nc.allow_low_precision("bf16 matmul operands, fp32 accumulation"):
        p.dma("sp", rc[:, :], rcd, w=["rc"])
        p.dma("act", pe[:, :, :], ped.rearrange("a d l -> d a l"), w=["pe"])
        p.dma("act", ident[:, :], identd, w=["ident"])
        p.dma("act", NG[:, :, :], ngd.rearrange("(a q) g -> q a g", q=128), w=["NG"])
        p.op("act", lambda e: e.activation(out=G[:, :, :], in_=NG[:, :, :], func=AF.Sigmoid), r=["NG"], w=["G"])
        for hf in range(4):
            p.dma("sp", tmp_i[:, :], pos[:, hf * 1024:(hf + 1) * 1024], w=["tmp_i"])
            p.op("dve", lambda e, hf=hf: e.tensor_copy(out=stage[0][0:32, hf * 1024:(hf + 1) * 1024], in_=tmp_i[:, :]), r=["tmp_i"], w=["qst"])
        p.op("dve", lambda e: e.tensor_scalar(out=stage[0][0:32, :], in0=stage[0][0:32, :], scalar1=rc[:, 0:1], scalar2=None, op0=ALU.mult), r=["qst", "rc"], w=["qst"])
        emit_sincos(p, stage[0][0:32, :], "qst", C32, S32, tmp_i, stage[1][0:32, :], "kst", rc)
        for i in range(4):
            emit_rope_load(p, qT[i], qsw[i], QT[:, i, :], f"QT{i}", stage[i % 2], stsw, C32, S32, ("q", "k")[i % 2], q=("sp", "act")[i % 2])
        for i in range(2):
            emit_rope_load(p, ksT[i], kssw[i], KS[:, i, :], f"KS{i}", stage[i], stsw, C32, S32, ("q", "k")[i], q=("sp", "act")[i])
        for i in range(2):
            st = stage[i]; key = ("qst", "kst")[i]
            p.dma(("sp", "act")[i], st.rearrange("p (a d) -> p a d", d=128), vsw[i].rearrange("(a q) d -> q a d", q=128), w=[key])
            p.op(("dve", "pool")[i], lambda e, st=st, i=i: e.tensor_copy(out=VS[:, i, :, 0:128], in_=st.rearrange("p (a d) -> p a d", d=128)), r=[key], w=[f"VS{i}"])
            p.op("pool", lambda e, i=i: e.memset(VS[:, i, :, 128:129], 1.0), w=[f"VS{i}"])
        p.dma("sp", w2st[:, :, :, :], w2d.rearrange("a (t q) d -> q a t d", q=128), w=["w2st"])
        p.op("dve", lambda e: e.tensor_copy(out=w2bf[:, :, :, :], in_=w2st[:, :, :, :]), r=["w2st"], w=["w2bf"])
        p.dma("sp", w2swst[:, :, :], w2swd.rearrange("(t q) d -> q t d", q=128), w=["w2swst"])
        p.op("dve", lambda e: e.tensor_copy(out=w2swbf[:, :, :], in_=w2swst[:, :, :]), r=["w2swst"], w=["w2swbf"])
        p.dma("act", ovst[:, :, :], ovd.rearrange("t q j -> q t j"), w=["ovst"])
        p.op("dve", lambda e: e.tensor_copy(out=VC[:, :, 129:193], in_=ovst[:, :, :]), r=["ovst"], w=["VC"])
        p.op("dve", lambda e: e.memset(VC[:, :, 128:129], 1.0), w=["VC"])
        wi = 0
        for kv in range(2):
            p.dma("sp", stage[0], kcvT[kv], w=["qst"])
            for l in range(32):
                src = stage[0][:, l:l + 16 * 254 + 1:16]
                p.op(("dve", "pool")[l % 2], lambda e, l=l, src=src, kv=kv: e.tensor_scalar(out=X[:, l, 0:255], in0=src, scalar1=pe[:, kv, l:l + 1], scalar2=None, op0=ALU.add),
                     r=["qst", "pe"], w=["kst"])
            w1v = w1d[kv].rearrange("(l d) h -> d l h", d=128)
            for lc in range(4):
                b = wi % 2; wi += 1
                p.dma(("sp", "act")[b], w1st[b][:, :, :], w1v[:, lc * 8:(lc + 1) * 8, :], w=[f"w1st{b}"])
                p.op(("dve", "pool")[b], lambda e, b=b: e.tensor_copy(out=w1bf[b][:, :, :], in_=w1st[b][:, :, :]), r=[f"w1st{b}"], w=[f"w1bf{b}"])
                for ht in range(2):
                    a, ak = acc[ht]
                    for ll in range(8):
                        l = lc * 8 + ll
                        p.op("pe", lambda e, a=a, b=b, ll=ll, l=l, ht=ht: e.matmul(a[:, 0:255], lhsT=w1bf[b][:, ll, ht * 128:(ht + 1) * 128], rhs=X[:, l, 0:255], start=(l == 0), stop=(l == 31)),
                             r=[f"w1bf{b}", "kst"], w=[ak])
            for ht in range(2):
                a, ak = acc[ht]
                p.op("act", lambda e, a=a: e.copy(out=hx[:, 0:255], in_=a[:, 0:255]), r=[ak], w=["hx"])
                p.op("dve", lambda e: e.tensor_tensor(out=ht_[:, 0:255], in0=hx[:, 0:255], in1=hx[:, 0:255], op=ALU.mult), r=["hx"], w=["ht"])
                p.op("dve", lambda e: e.tensor_scalar(out=ht_[:, 0:255], in0=ht_[:, 0:255], scalar1=0.044715, scalar2=1.0, op0=ALU.mult, op1=ALU.add), r=["ht"], w=["ht"])
                p.op("dve", lambda e: e.tensor_tensor(out=ht_[:, 0:255], in0=ht_[:, 0:255], in1=hx[:, 0:255], op=ALU.mult), r=["ht", "hx"], w=["ht"])
                p.op("act", lambda e: e.activation(out=ht_[:, 0:255], in_=ht_[:, 0:255], func=AF.Tanh, scale=GC0), r=["ht"], w=["ht"])
                p.op("dve", lambda e: e.tensor_scalar(out=ht_[:, 0:255], in0=ht_[:, 0:255], scalar1=0.5, scalar2=0.5, op0=ALU.mult, op1=ALU.add), r=["ht"], w=["ht"])
                p.op("dve", lambda e, kv=kv, ht=ht: e.tensor_tensor(out=hT[:, kv, ht, 0:255], in0=ht_[:, 0:255], in1=hx[:, 0:255], op=ALU.mult), r=["ht", "hx"], w=[f"hT{kv}"])
            if kv == 0:
                a, ak = acc[2]
                for ht in range(2):
                    p.op("pe", lambda e, a=a, ht=ht: e.matmul(a[:, 0:255], lhsT=w2bf[:, 0, ht, :], rhs=hT[:, 0, ht, 0:255], start=(ht == 0), stop=(ht == 1)), r=["w2bf", "hT0"], w=[ak])
                a2, ak2 = acc[3]
                for ht in range(2):
                    p.op("pe", lambda e, a2=a2, ht=ht: e.matmul(a2[0:32, 0:255], lhsT=w2swbf[:, ht, :], rhs=hT[:, 0, ht, 0:255], start=(ht == 0), stop=(ht == 1)), r=["w2swbf", "hT0"], w=[ak2])
                p.op("act", lambda e, a=a: e.copy(out=krot[:, 0:255], in_=a[:, 0:255]), r=[ak], w=["krot"])
                p.op("act", lambda e, a2=a2: e.copy(out=ksw_s[:, 0:255], in_=a2[0:32, 0:255]), r=[ak2], w=["ksw_s"])
                Cs = C32[:, 31:31 + 16 * 254 + 1:16]; Ss = S32[:, 31:31 + 16 * 254 + 1:16]
                p.op("dve", lambda e: e.tensor_tensor(out=krot[0:32, 0:255], in0=krot[0:32, 0:255], in1=Cs, op=ALU.mult), r=["krot", "C32"], w=["krot"])
                p.op("dve", lambda e: e.tensor_tensor(out=ksw_s[:, 0:255], in0=ksw_s[:, 0:255], in1=Ss, op=ALU.mult), r=["ksw_s", "S32"], w=["ksw_s"])
                p.op("dve", lambda e: e.tensor_tensor(out=krot[0:32, 0:255], in0=krot[0:32, 0:255], in1=ksw_s[:, 0:255], op=ALU.add), r=["krot", "ksw_s"], w=["krot"])
                p.op("dve", lambda e: e.tensor_copy(out=KC[:, 0:255], in_=krot[:, 0:255]), r=["krot"], w=["KC"])
            else:
                for ct in range(2):
                    mc = 128 if ct == 0 else 127
                    a, ak = acc[2 + ct]
                    for ht in range(2):
                        p.op("pe", lambda e, a=a, ht=ht, ct=ct, mc=mc: e.matmul(a[0:mc, 0:128], lhsT=hT[:, 1, ht, ct * 128:ct * 128 + mc], rhs=w2bf[:, 1, ht, :], start=(ht == 0), stop=(ht == 1)),
                             r=["w2bf", "hT1"], w=[ak])
                    p.op("act", lambda e, a=a, ct=ct, mc=mc: e.copy(out=VC[0:mc, ct, 0:128], in_=a[0:mc, 0:128]), r=[ak], w=["VC"])
        mkeys = [f"m{k}" for k in range(32)]
        p.op("dve", lambda e: e.memset(small[:, :], 0.0), r=["qst", "kst", "C32", "S32"], w=mkeys + ["qst", "kst", "C32", "S32", "YACC", "ADDM", "M8", "EALL", "small"])
        p.dma("sp", ADDM, addmd, w=["ADDM"])
        for j in range(8):
            b = j % 2
            p.dma(("sp", "act")[b], cmst[b][:, :], m8d[j], w=[f"cmst{b}"])
            p.op("dve", lambda e, b=b, j=j: e.tensor_copy(out=M8[:, j, :], in_=cmst[b][:, :]), r=[f"cmst{b}"], w=["M8"])
        for j in range(8):
            b = j % 2
            p.dma(("sp", "act")[b], cmst[b][:, :].rearrange("p (a b) -> p a b", b=128), ealld[:, j * 4:(j + 1) * 4, :], w=[f"cmst{b}"])
            p.op("dve", lambda e, b=b, j=j: e.tensor_copy(out=EALL[:, j * 4:(j + 1) * 4, :], in_=cmst[b][:, :].rearrange("p (a b) -> p a b", b=128)), r=[f"cmst{b}"], w=["EALL"])

        def evac(hh, qs, qc, ncol, gi, first_branch, imp_mode=None):
            a, ak = acc[qs]
            qt = qc * 4 + qs
            p.op("dve", lambda e: e.tensor_scalar(out=small[:, 0:1], in0=a[:, 128:129], scalar1=1e-30, scalar2=None, op0=ALU.max), r=[ak], w=["small"])
            p.op("dve", lambda e: e.reciprocal(out=small[:, 1:2], in_=small[:, 0:1]), r=["small"], w=["small"])
            if imp_mode is not None:
                if imp_mode == 0:
                    p.op("dve", lambda e: e.tensor_scalar(out=IMP[:, qs, :], in0=a[:, 129:193], scalar1=small[:, 1:2], scalar2=None, op0=ALU.mult), r=[ak, "small"], w=[f"IMP{qs}"])
                else:
                    p.op("dve", lambda e: e.scalar_tensor_tensor(out=IMP[:, qs, :], in0=a[:, 129:193], scalar=small[:, 1:2], in1=IMP[:, qs, :], op0=ALU.mult, op1=ALU.add), r=[ak, "small", f"IMP{qs}"], w=[f"IMP{qs}"])
            p.op("dve", lambda e: e.tensor_tensor(out=small[:, 2:3], in0=small[:, 1:2], in1=G[:, qt, hh * 3 + gi:hh * 3 + gi + 1], op=ALU.mult), r=["small", "G"], w=["small"])
            if first_branch:
                p.op("dve", lambda e: e.tensor_scalar(out=YACC[:, qs, hh, :], in0=a[:, 0:128], scalar1=small[:, 2:3], scalar2=None, op0=ALU.mult), r=[ak, "small"], w=[f"Y{qs}"])
            else:
                p.op("dve", lambda e: e.scalar_tensor_tensor(out=YACC[:, qs, hh, :], in0=a[:, 0:128], scalar=small[:, 2:3], in1=YACC[:, qs, hh, :], op0=ALU.mult, op1=ALU.add), r=[ak, "small", f"Y{qs}"], w=[f"Y{qs}"])

        for qc in range(NQC):
            qsl = slice(qc * TC, (qc + 1) * TC)
            cts = [0] if qc <= 3 else [0, 1]
            cmk = {}
            for ct in cts:
                if ct == 0 and qc >= 5:
                    continue
                b = cnt["cm"] % 2; cnt["cm"] += 1
                p.dma("sp", cmst[b][:, :], cmpmd[ct, qc], w=[f"cmst{b}"])
                cmk[ct] = b
            for hh in range(4):
                for ct in cts:
                    mc = 128 if ct == 0 else 127
                    ps, pk = nextps(); Eb, ek = nextE()
                    p.op("pe", lambda e, ps=ps, ct=ct, mc=mc, hh=hh, qsl=qsl: e.matmul(ps[0:mc, :], lhsT=KC[:, ct * 128:ct * 128 + mc], rhs=QT[:, hh, qsl], start=True, stop=True), r=["KC", f"QT{hh}"], w=[pk])
                    p.op("act", lambda e, ps=ps, Eb=Eb, mc=mc: e.activation(out=Eb[0:mc, :], in_=ps[0:mc, :], func=AF.Exp, scale=SCALE), r=[pk], w=[ek])
                    if ct in cmk:
                        b = cmk[ct]
                        p.op("pool", lambda e, Eb=Eb, mc=mc, b=b: e.tensor_tensor(out=Eb[0:mc, :], in0=Eb[0:mc, :], in1=cmst[b][0:mc, :], op=ALU.mult), r=[ek, f"cmst{b}"], w=[ek])
                    for qs in range(4):
                        a, ak = acc[qs]
                        p.op("pe", lambda e, a=a, Eb=Eb, qs=qs, ct=ct, mc=mc, lastct=cts[-1]: e.matmul(a[:, 0:193], lhsT=Eb[0:mc, qs * 128:(qs + 1) * 128], rhs=VC[0:mc, ct, :], start=(ct == 0), stop=(ct == lastct)),
                             r=[ek, "VC"], w=[ak])
                for qs in range(4):
                    evac(hh, qs, qc, 193, 0, True, imp_mode=(0 if hh == 0 else 1))
            for qs in range(4):
                qt = qc * 4 + qs
                p.op("dve", lambda e, qs=qs, qt=qt: e.tensor_tensor(out=IMP[:, qs, :], in0=IMP[:, qs, :], in1=ADDM[:, qt, :], op=ALU.add), r=[f"IMP{qs}", "ADDM"], w=[f"IMP{qs}"])
                p.op("dve", lambda e, qs=qs: e.max(out=m8a[:, :], in_=IMP[:, qs, :]), r=[f"IMP{qs}"], w=["m8a"])
                p.op("dve", lambda e, qs=qs: e.match_replace(out=IMPW[:, :], in_to_replace=m8a[:, :], in_values=IMP[:, qs, :], imm_value=-3e38), r=[f"IMP{qs}", "m8a"], w=["IMPW"])
                p.op("dve", lambda e: e.max(out=m8b[:, :], in_=IMPW[:, :]), r=["IMPW"], w=["m8b"])
                p.op("dve", lambda e, qs=qs: e.tensor_scalar(out=SEL[:, :], in0=IMP[:, qs, :], scalar1=m8b[:, 7:8], scalar2=None, op0=ALU.is_ge), r=[f"IMP{qs}", "m8b"], w=["SEL"])
                pm, pmk = psm
                p.op("pe", lambda e: e.transpose(pm[0:64, 0:128], SEL[:, :], ident[:, :]), r=["SEL", "ident"], w=[pmk])
                p.op("act", lambda e, qs=qs: e.copy(out=SELT[:, qs * 128:(qs + 1) * 128], in_=pm[0:64, 0:128]), r=[pmk], w=["SELT"])
            nkt = 4 * qc + 4
            for kt in range(nkt):
                pm, pmk = psm
                p.op("pe", lambda e, kt=kt: e.matmul(pm[:, :], lhsT=EALL[0:64, kt, :], rhs=SELT[:, :], start=True, stop=True), r=["EALL", "SELT"], w=[pmk])
                dj = kt - 4 * qc
                if dj >= 0:
                    p.op("dve", lambda e, kt=kt, dj=dj: e.tensor_tensor(out=MSK[:, kt, :], in0=pm[:, :], in1=M8[:, dj + 4, :], op=ALU.mult), r=[pmk, "M8"], w=[f"m{kt}"])
                else:
                    p.op("act", lambda e, kt=kt: e.copy(out=MSK[:, kt, :], in_=pm[:, :]), r=[pmk], w=[f"m{kt}"])
            for br in (1, 2):
                for hh in range(4):
                    kt0 = 0 if br == 1 else max(0, 4 * qc - 4)
                    for kt in range(kt0, nkt):
                        dj = kt - 4 * qc
                        ps, pk = nextps(); Eb, ek = nextE()
                        p.op("pe", lambda e, ps=ps, kt=kt, hh=hh, br=br, qsl=qsl: e.matmul(ps[:, :], lhsT=KS[:, br - 1, kt * 128:(kt + 1) * 128], rhs=QT[:, hh, qsl], start=True, stop=True), r=[f"KS{br - 1}", f"QT{hh}"], w=[pk])
                        p.op("act", lambda e, ps=ps, Eb=Eb: e.activation(out=Eb[:, :], in_=ps[:, :], func=AF.Exp, scale=SCALE), r=[pk], w=[ek])
                        meng = ("pool", "dve")[kt % 2]
                        if br == 1:
                            p.op(meng, lambda e, Eb=Eb, kt=kt: e.tensor_tensor(out=Eb[:, :], in0=Eb[:, :], in1=MSK[:, kt, :], op=ALU.mult), r=[ek, f"m{kt}"], w=[ek])
                        else:
                            p.op(meng, lambda e, Eb=Eb, dj=dj: e.tensor_tensor(out=Eb[:, :], in0=Eb[:, :], in1=M8[:, dj + 4, :], op=ALU.mult), r=[ek, "M8"], w=[ek])
                        for qs in range(4):
                            if dj > qs:
                                continue
                            if br == 2 and qs > 4 + dj:
                                continue
                            kfirst = 0 if br == 1 else max(0, 4 * qc + qs - 4)
                            klast = 4 * qc + qs
                            a, ak = acc[qs]
                            p.op("pe", lambda e, a=a, Eb=Eb, qs=qs, kt=kt, br=br, kfirst=kfirst, klast=klast: e.matmul(a[:, 0:129], lhsT=Eb[:, qs * 128:(qs + 1) * 128], rhs=VS[:, br - 1, kt, :], start=(kt == kfirst), stop=(kt == klast)),
                                 r=[ek, f"VS{br - 1}"], w=[ak])
                    for qs in range(4):
                        evac(hh, qs, qc, 129, br, False)
            for qs in range(4):
                r0 = qc * TC + qs * 128
                p.dma("sp", y[r0:r0 + 128, :], YACC[:, qs, :, :].rearrange("p a b -> p (a b)"), r=[f"Y{qs}"], w=[f"y{qc}_{qs}"])
    print("nsa op counts", p.counts())
    return p.build()


def nsa_inputs(PTc, pos_b, pe_l, w1_l, w2_l, consts):
    T = lambda t: PTc[t * 128:(t + 1) * 128]
    sw = lambda a: np.concatenate([a[16:32], a[0:16]], 0)
    qT = np.stack([T(i) for i in range(4)])
    ks2 = np.stack([T(6), T(8)])
    ng = PTc[36 * 128 + 32:36 * 128 + 44]
    d = {
        "qT": np.ascontiguousarray(qT), "qsw": np.stack([sw(a) for a in qT]),
        "kcvT": np.ascontiguousarray(np.stack([T(4), T(5)])),
        "ksT": np.ascontiguousarray(ks2), "kssw": np.stack([sw(a) for a in ks2]),
        "vsw": np.ascontiguousarray(np.stack([T(7).T, T(9).T])),
        "ng": np.ascontiguousarray(ng.T),
        "pos": np.ascontiguousarray(np.broadcast_to(pos_b[None, :], (32, S))).astype(np.int32), "rc": rope_consts(),
        "pe": np.ascontiguousarray(pe_l.transpose(0, 2, 1)),
        "w1": np.ascontiguousarray(w1_l), "w2": np.ascontiguousarray(w2_l),
        "w2sw": np.ascontiguousarray(np.concatenate([w2_l[0][:, 16:32], w2_l[0][:, 0:16]], 1)),
    }
    d.update(consts)
    return d


import math

S = 4096
C = 64
NB = 128
NBLK = S // NB
H = 8
CDEC = math.exp(-0.5)
LNX_EPS = 64e-5


def rwkv_consts():
    s = np.arange(128)[:, None]; t = np.arange(128)[None, :]
    same = (s // 64 == t // 64)
    tri_i = (same & (s <= t)).astype(np.float32)
    tri_e = (same & (s < t)).astype(np.float32)
    s6 = np.arange(64)[:, None]; t6 = np.arange(64)[None, :]
    mu2 = np.zeros((64, 8, 2, 64), np.float32)
    mu2[:, :, 0, :] = (s6 < t6)[:, None, :]
    mu2[:, :, 1, :] = (s6 <= t6)[:, None, :]
    ml = np.zeros((64, 8, 64), np.float32)
    ml[:] = (s6 > t6)[:, None, :]
    id8 = np.zeros((64, 8, 64), np.float32)
    id8[:] = np.eye(64, dtype=np.float32)[:, None, :]
    return dict(tri_i=tri_i, tri_e=tri_e, mu2=mu2, ml=ml, id8=id8, ident=np.eye(64, dtype=np.float32), ones64=np.ones((64, 64), np.float32))


def build_rwkv():
    p = Prog("rwkv")
    D_ = lambda n, s, dt=F32: p.dram(n, s, dt, "ExternalInput")
    rkv_d = D_("rkv", [3, 64, H, S + 1]); wad_d = D_("wad", [128, S + 1]); gd_d = D_("gd", [160, S + 1])
    mu_d = D_("mu", [64, 3, H]); muw_d = D_("muw", [128, 3])
    wa_up_d = D_("wa_up", [128, 512]); g_up_d = D_("g_up", [160, 512]); w0b_d = D_("w0b", [128, 512])
    par_d = D_("par", [64, 4, H])
    lnwb_d = D_("lnwb", [2, 64, 512])
    tri_i_d = D_("tri_i", [128, 128]); tri_e_d = D_("tri_e", [128, 128])
    mu2_d = D_("mu2", [64, 8, 2, 64]); ml_d = D_("ml", [64, 8, 64]); id8_d = D_("id8", [64, 8, 64])
    ident_d = D_("ident", [64, 64]); ones_d = D_("ones64", [64, 64])
    y = p.dram("y", [S, 512], F32, "ExternalOutput")

    sb = p.sb
    RAW = sb([64, 3, H, NB + 1], F32, "RAW"); XS = sb([64, 3, H, NB], F32, "XS")
    T1 = sb([64, H, NB], F32, "T1")
    WAD = sb([128, NB + 1], F32, "WAD"); WS = sb([128, NB], F32, "WS")
    GD = sb([128, NB + 1], F32, "GD"); GD2 = sb([32, NB + 1], F32, "GD2"); GS = sb([128, NB], F32, "GS"); GS2 = sb([32, NB], F32, "GS2")
    TW = sb([128, NB], BF16, "TW"); SG = sb([128, NB], BF16, "SG"); SG2 = sb([32, NB], BF16, "SG2")
    LW = sb([128, 512], F32, "LW")
    G = sb([64, H, NB], F32, "G"); GI = sb([64, H, NB], F32, "GI"); GM = sb([64, H, NB], F32, "GM")
    A = sb([64, H, NB], F32, "A"); KK = sb([64, H, NB], F32, "KK"); KP = sb([64, H, NB], F32, "KP"); BV = sb([64, H, NB], F32, "BV")
    RN = sb([64, H, NB], F32, "RN")
    QR = sb([64, H, 2, 2, 64], F32, "QR"); KTL = sb([64, H, NB], F32, "KTL"); BTL = sb([64, H, NB], F32, "BTL"); RK = sb([64, H, NB], F32, "RK")
    BTT = sb([64, 2, H, 64], F32, "BTT"); KTT = sb([64, 2, H, 64], F32, "KTT"); VT = sb([64, 2, H, 64], F32, "VT")
    GB = sb([64, 2, H, 2, 64], F32, "GB"); GK = sb([64, 2, H, 2, 64], F32, "GK")
    NL = [[sb([64, H, 64], F32, f"inv{n}{i}") for i in range(2)] for n in "NLAB"]
    TMT = sb([64, 2, H, 64], F32, "TMT")
    M0 = sb([64, H, 64], F32, "M0"); XSB = sb([64, H, 64], F32, "XSB"); US = sb([64, H, 64], F32, "US")
    YS = sb([64, H, 64], F32, "YS"); YT = sb([64, H, 64], F32, "YT"); OUT = [sb([64, 512], F32, f"OUT{i}") for i in range(2)]
    ST = sb([64, 4, H], F32, "ST")
    MU = sb([64, 3, H], F32, "MU"); MUW = sb([128, 3], F32, "MUW"); PAR = sb([64, 5, H], F32, "PAR")
    WAUPs = sb([128, 512], F32, "WAUPs"); WAUP = sb([128, 512], BF16, "WAUP")
    GUPs = sb([128, 512], F32, "GUPs"); GUP = sb([128, 512], BF16, "GUP"); GUP2s = sb([32, 512], F32, "GUP2s"); GUP2 = sb([32, 512], BF16, "GUP2")
    W0B = sb([128, 512], F32, "W0B"); LNW = sb([64, 512], F32, "LNW"); LNB = sb([64, 512], F32, "LNB")
    TRI_I = sb([128, 128], F32, "TRI_I"); TRI_E = sb([128, 128], F32, "TRI_E")
    MU2 = sb([64, 8, 2, 64], F32, "MU2"); ML = sb([64, 8, 64], F32, "ML"); ID8 = sb([64, 8, 64], F32, "ID8")
    IDENT = sb([64, 64], F32, "IDENT"); ONES = sb([64, 64], F32, "ONES")
    banks = [(p.ps([128, 512], F32, f"bk{i}"), f"bk{i}") for i in range(8)]
    bi = [0]

    def bank():
        b = banks[bi[0] % 8]; bi[0] += 1
        return b

    bc = lambda ap2, n: ap2.unsqueeze(2).to_broadcast([64, ap2.shape[1], n])
    f3 = lambda t: t[:, :, :]
    TT = lambda eng, out, a, b, op, r, w: p.op(eng, lambda e: e.tensor_tensor(out=out, in0=a, in1=b, op=op), r=r, w=w)

    with p.nc.allow_low_precision("bf16 lora matmuls; fp32 state math"):
        ld = [("sp", MU[:, :, :], mu_d, "MU"), ("act", MUW[:, :], muw_d, "MUW"), ("sp", PAR[:, 0:4, :], par_d, "PAR"),
              ("act", WAUPs[:, :], wa_up_d, "WAUPs"), ("sp", GUPs[:, :], g_up_d[0:128, :], "GUPs"), ("act", GUP2s[:, :], g_up_d[128:160, :], "GUP2s"),
              ("sp", W0B[:, :], w0b_d, "W0B"), ("act", LNW[:, :], lnwb_d[0], "LNW"), ("sp", LNB[:, :], lnwb_d[1], "LNB"),
              ("act", TRI_I[:, :], tri_i_d, "TRI_I"), ("sp", TRI_E[:, :], tri_e_d, "TRI_E"), ("act", MU2[:, :, :, :], mu2_d, "MU2"),
              ("sp", ML[:, :, :], ml_d, "ML"), ("act", ID8[:, :, :], id8_d, "ID8"), ("sp", IDENT[:, :], ident_d, "IDENT"), ("act", ONES[:, :], ones_d, "ONES")]
        for q, o, i, k in ld:
            p.dma(q, o, i, w=[k])
        p.op("dve", lambda e: e.tensor_copy(out=WAUP[:, :], in_=WAUPs[:, :]), r=["WAUPs"], w=["WAUP"])
        p.op("dve", lambda e: e.tensor_copy(out=GUP[:, :], in_=GUPs[:, :]), r=["GUPs"], w=["GUP"])
        p.op("dve", lambda e: e.tensor_copy(out=GUP2[:, :], in_=GUP2s[:, :]), r=["GUP2s"], w=["GUP2"])
        p.op("dve", lambda e: e.tensor_scalar(out=PAR[:, 4, :], in0=PAR[:, 2, :], scalar1=-1.0, scalar2=1.0, op0=ALU.mult, op1=ALU.add), r=["PAR"], w=["PAR"])
        p.op("dve", lambda e: e.memset(M0[:, :, :], 0.0), w=["M0"])

        for blk in range(NBLK):
            t0 = blk * NB
            for i in range(3):
                p.dma(("sp", "act", "sp")[i], RAW[:, i, :, :], rkv_d[i, :, :, t0:t0 + NB + 1], w=[f"RAW{i}"])
            p.dma("act", WAD[:, :], wad_d[:, t0:t0 + NB + 1], w=["WAD"])
            p.dma("sp", GD[:, :], gd_d[0:128, t0:t0 + NB + 1], w=["GD"])
            p.dma("act", GD2[:, :], gd_d[128:160, t0:t0 + NB + 1], w=["GD2"])
            for i in range(3):
                eng = ("dve", "pool", "dve")[i]
                TT(eng, f3(T1), RAW[:, i, :, 0:NB], RAW[:, i, :, 1:NB + 1], ALU.subtract, [f"RAW{i}"], ["T1"])
                TT(eng, f3(T1), f3(T1), bc(MU[:, i, :], NB), ALU.mult, ["T1", "MU"], ["T1"])
                TT(eng, XS[:, i, :, :], f3(T1), RAW[:, i, :, 1:NB + 1], ALU.add, ["T1", f"RAW{i}"], [f"XS{i}"])
            for (raw, dst, col, n, k) in ((WAD, WS, 0, 128, "W"), (GD, GS, 1, 128, "G1"), (GD2, GS2, 2, 32, "G2")):
                TT("pool", dst[0:n, :], raw[0:n, 0:NB], raw[0:n, 1:NB + 1], ALU.subtract, [k + "raw" if False else {"W": "WAD", "G1": "GD", "G2": "GD2"}[k]], [k + "s"])
                p.op("dve", lambda e, raw=raw, dst=dst, col=col, n=n: e.scalar_tensor_tensor(out=dst[0:n, :], in0=dst[0:n, :], scalar=MUW[0:n, col:col + 1], in1=raw[0:n, 1:NB + 1], op0=ALU.mult, op1=ALU.add),
                     r=[k + "s", "MUW", {"W": "WAD", "G1": "GD", "G2": "GD2"}[k]], w=[k + "s"])
            p.op("act", lambda e: e.activation(out=TW[0:64, :], in_=WS[0:64, :], func=AF.Tanh), r=["Ws"], w=["TW"])
            p.op("act", lambda e: e.copy(out=TW[64:128, :], in_=WS[64:128, :]), r=["Ws"], w=["TW"])
            p.op("act", lambda e: e.activation(out=SG[:, :], in_=GS[:, :], func=AF.Sigmoid), r=["G1s"], w=["SG"])
            p.op("act", lambda e: e.activation(out=SG2[:, :], in_=GS2[:, :], func=AF.Sigmoid), r=["G2s"], w=["SG2"])
            bk, bkk = bank()
            p.op("pe", lambda e, bk=bk: e.matmul(bk[:, :], lhsT=TW[0:64, :], rhs=WAUP[0:64, :], start=True, stop=True), r=["TW", "WAUP"], w=[bkk])
            TT("dve", LW[:, :], bk[:, :], W0B[:, :], ALU.add, [bkk, "W0B"], ["LW"])
            p.op("act", lambda e: e.activation(out=LW[:, :], in_=LW[:, :], func=AF.Sigmoid), r=["LW"], w=["LW"])
            for hq in range(2):
                bI, bIk = bank(); bE, bEk = bank()
                for h4 in range(4):
                    h = 4 * hq + h4
                    p.op("pe", lambda e, bI=bI, h=h, h4=h4: e.matmul(bI[0:64, h4 * 128:(h4 + 1) * 128], lhsT=LW[:, h * 64:(h + 1) * 64], rhs=TRI_I[:, :], start=True, stop=True), r=["LW", "TRI_I"], w=[bIk])
                    p.op("pe", lambda e, bE=bE, h=h, h4=h4: e.matmul(bE[0:64, h4 * 128:(h4 + 1) * 128], lhsT=LW[:, h * 64:(h + 1) * 64], rhs=TRI_E[:, :], start=True, stop=True), r=["LW", "TRI_E"], w=[bEk])
                hs = slice(4 * hq, 4 * hq + 4)
                v4 = lambda t: t[:, hs, :].rearrange("p a b -> p (a b)")
                p.op("act", lambda e, bI=bI, hs=hs: e.activation(out=G[:, hs, :].rearrange("p a b -> p (a b)"), in_=bI[0:64, :], func=AF.Exp, scale=-CDEC), r=[bIk], w=["G"])
                p.op("act", lambda e, bI=bI, hs=hs: e.activation(out=GI[:, hs, :].rearrange("p a b -> p (a b)"), in_=bI[0:64, :], func=AF.Exp, scale=CDEC), r=[bIk], w=["GI"])
                p.op("act", lambda e, bE=bE, hs=hs: e.activation(out=GM[:, hs, :].rearrange("p a b -> p (a b)"), in_=bE[0:64, :], func=AF.Exp, scale=-CDEC), r=[bEk], w=["GM"])
            for hq in range(2):
                bk, bkk = bank()
                for h4 in range(4):
                    h = 4 * hq + h4
                    p.op("pe", lambda e, bk=bk, h=h, h4=h4: e.matmul(bk[0:64, h4 * 128:(h4 + 1) * 128], lhsT=WAUP[64:128, h * 64:(h + 1) * 64], rhs=TW[64:128, :], start=True, stop=True), r=["WAUP", "TW"], w=[bkk])
                hs = slice(4 * hq, 4 * hq + 4)
                p.op("dve", lambda e, bk=bk, hs=hs: e.tensor_tensor(out=A[:, hs, :], in0=bk[0:64, :].rearrange("p (a b) -> p a b", b=128), in1=bc(PAR[:, 0, hs], NB), op=ALU.add), r=[bkk, "PAR"], w=["A"])
            p.op("act", lambda e: e.activation(out=f3(A), in_=f3(A), func=AF.Sigmoid), r=["A"], w=["A"])
            TT("dve", f3(KK), XS[:, 1, :, :], bc(PAR[:, 1, :], NB), ALU.mult, ["XS1", "PAR"], ["KK"])
            TT("pool", f3(T1), f3(KK), f3(KK), ALU.mult, ["KK"], ["T1"])
            for hq in range(2):
                bk, bkk = bank()
                for h4 in range(4):
                    h = 4 * hq + h4
                    p.op("pe", lambda e, bk=bk, h=h, h4=h4: e.matmul(bk[0:64, h4 * 128:(h4 + 1) * 128], lhsT=ONES[:, :], rhs=T1[:, h, :], start=True, stop=True), r=["ONES", "T1"], w=[bkk])
                hs = slice(4 * hq, 4 * hq + 4)
                p.op("act", lambda e, bk=bk, hs=hs: e.sqrt(out=RN[:, hs, :].rearrange("p a b -> p (a b)"), in_=bk[0:64, :]), r=[bkk], w=["RN"])
            p.op("dve", lambda e: e.tensor_scalar(out=f3(RN), in0=f3(RN), scalar1=1e-12, scalar2=None, op0=ALU.max), r=["RN"], w=["RN"])
            p.op("dve", lambda e: e.reciprocal(out=f3(RN), in_=f3(RN)), r=["RN"], w=["RN"])
            TT("dve", f3(KK), f3(KK), f3(RN), ALU.mult, ["KK", "RN"], ["KK"])
            TT("pool", f3(T1), f3(A), bc(PAR[:, 2, :], NB), ALU.mult, ["A", "PAR"], ["T1"])
            TT("pool", f3(T1), f3(T1), bc(PAR[:, 4, :], NB), ALU.add, ["T1", "PAR"], ["T1"])
            TT("pool", f3(KP), XS[:, 1, :, :], f3(T1), ALU.mult, ["XS1", "T1"], ["KP"])
            TT("dve", f3(BV), f3(KK), f3(A), ALU.mult, ["KK", "A"], ["BV"])
            c4 = lambda t: t[:, :, :].rearrange("p h (c t) -> p h c t", c=2)
            TT("dve", QR[:, :, :, 0, :], c4(KK), c4(GM), ALU.mult, ["KK", "GM"], ["QR"])
            TT("pool", QR[:, :, :, 1, :], XS[:, 0, :, :].rearrange("p h (c t) -> p h c t", c=2), c4(G), ALU.mult, ["XS0", "G"], ["QR"])
            TT("dve", f3(KTL), f3(KP), f3(GI), ALU.mult, ["KP", "GI"], ["KTL"])
            TT("pool", f3(BTL), f3(BV), f3(GI), ALU.mult, ["BV", "GI"], ["BTL"])
            TT("dve", f3(RK), XS[:, 0, :, :], f3(KP), ALU.mult, ["XS0", "KP"], ["RK"])
            for c in range(2):
                cs = slice(c * 64, (c + 1) * 64)
                for src, skey, dst, dkey in ((BTL, "BTL", BTT, "BTT"), (KTL, "KTL", KTT, "KTT"), (XS[:, 2, :, :], "XS2", VT, "VT")):
                    bk, bkk = bank()
                    for h in range(H):
                        p.op("pe", lambda e, bk=bk, h=h, src=src, cs=cs: e.transpose(bk[0:64, h * 64:(h + 1) * 64], src[:, h, cs], IDENT[:, :]), r=[skey, "IDENT"], w=[bkk])
                    p.op("act", lambda e, bk=bk, dst=dst, c=c: e.copy(out=dst[:, c, :, :].rearrange("p a b -> p (a b)"), in_=bk[0:64, :]), r=[bkk], w=[f"{dkey}{c}"])
                for src, skey, dst, dkey in ((BTL, "BTL", GB, "GB"), (KTL, "KTL", GK, "GK")):
                    for hq in range(2):
                        bk, bkk = bank()
                        for h4 in range(4):
                            h = 4 * hq + h4
                            p.op("pe", lambda e, bk=bk, h=h, h4=h4, src=src, cs=cs, c=c: e.matmul(bk[0:64, h4 * 128:(h4 + 1) * 128], lhsT=src[:, h, cs], rhs=QR[:, h, c, :, :].rearrange("p a b -> p (a b)"), start=True, stop=True),
                                 r=[skey, "QR"], w=[bkk])
                        hs = slice(4 * hq, 4 * hq + 4)
                        p.op("dve", lambda e, bk=bk, dst=dst, c=c, hs=hs: e.tensor_tensor(out=dst[:, c, hs, :, :].rearrange("p a b d -> p (a b d)"), in0=bk[0:64, :], in1=MU2[:, 0:4, :, :].rearrange("p a b d -> p (a b d)"), op=ALU.mult),
                             r=[bkk, "MU2"], w=[f"{dkey}{c}"])
                bk, bkk = bank()
                for h in range(H):
                    p.op("pe", lambda e, bk=bk, h=h, c=c, cs=cs: e.matmul(bk[0:64, h * 64:(h + 1) * 64], lhsT=QR[:, h, c, 0, :], rhs=BTL[:, h, cs], start=True, stop=True), r=["QR", "BTL"], w=[bkk])
                Ncur, Lcur, Acur, Bcur = NL[0][0], NL[1][0], NL[2][0], NL[3][0]
                nk = lambda n, i: f"inv{n}{i}"
                p.op("dve", lambda e, bk=bk, Lcur=Lcur: e.tensor_tensor(out=Lcur[:, :, :].rearrange("p a b -> p (a b)"), in0=bk[0:64, :], in1=ML[:, :, :].rearrange("p a b -> p (a b)"), op=ALU.mult), r=[bkk, "ML"], w=[nk("L", 0)])
                p.op("act", lambda e, Ncur=Ncur, c=c: e.copy(out=Ncur[:, :, :], in_=GB[:, c, :, 0, :]), r=[f"GB{c}"], w=[nk("N", 0)])
                TT("pool", f3(Acur), f3(ID8), f3(Ncur), ALU.subtract, ["ID8", nk("N", 0)], [nk("A", 0)])
                TT("pool", f3(Bcur), f3(ID8), f3(Lcur), ALU.subtract, ["ID8", nk("L", 0)], [nk("B", 0)])
                cur = 0
                for j in range(1, 6):
                    nxt = 1 - cur
                    Nc, Lc, Ac, Bc = (NL[n][cur] for n in range(4))
                    Nn, Ln, An, Bn = (NL[n][nxt] for n in range(4))
                    bN, bNk = bank()
                    for h in range(H):
                        p.op("pe", lambda e, bN=bN, h=h, Lc=Lc, Nc=Nc: e.matmul(bN[0:64, h * 64:(h + 1) * 64], lhsT=Lc[:, h, :], rhs=Nc[:, h, :], start=True, stop=True), r=[nk("L", cur), nk("N", cur)], w=[bNk])
                    p.op("act", lambda e, bN=bN, Nn=Nn: e.copy(out=Nn[:, :, :].rearrange("p a b -> p (a b)"), in_=bN[0:64, :]), r=[bNk], w=[nk("N", nxt)])
                    if j <= 4:
                        bL, bLk = bank()
                        for h in range(H):
                            p.op("pe", lambda e, bL=bL, h=h, Lc=Lc, Nc=Nc: e.matmul(bL[0:64, h * 64:(h + 1) * 64], lhsT=Nc[:, h, :], rhs=Lc[:, h, :], start=True, stop=True), r=[nk("L", cur), nk("N", cur)], w=[bLk])
                        p.op("act", lambda e, bL=bL, Ln=Ln: e.copy(out=Ln[:, :, :].rearrange("p a b -> p (a b)"), in_=bL[0:64, :]), r=[bLk], w=[nk("L", nxt)])
                    bA, bAk = bank()
                    for h in range(H):
                        p.op("pe", lambda e, bA=bA, h=h, Bc=Bc, Nn=Nn: e.matmul(bA[0:64, h * 64:(h + 1) * 64], lhsT=Bc[:, h, :], rhs=Nn[:, h, :], start=True, stop=True), r=[nk("B", cur), nk("N", nxt)], w=[bAk])
                    dstA = An[:, :, :] if j < 5 else TMT[:, c, :, :]
                    p.op("dve", lambda e, bA=bA, dstA=dstA, Ac=Ac: e.tensor_tensor(out=dstA.rearrange("p a b -> p (a b)"), in0=bA[0:64, :], in1=Ac[:, :, :].rearrange("p a b -> p (a b)"), op=ALU.add),
                         r=[bAk, nk("A", cur)], w=[nk("A", nxt) if j < 5 else f"TMT{c}"])
                    if j <= 4:
                        bB, bBk = bank()
                        for h in range(H):
                            p.op("pe", lambda e, bB=bB, h=h, Ac=Ac, Ln=Ln: e.matmul(bB[0:64, h * 64:(h + 1) * 64], lhsT=Ac[:, h, :], rhs=Ln[:, h, :], start=True, stop=True), r=[nk("A", cur), nk("L", nxt)], w=[bBk])
                        p.op("dve", lambda e, bB=bB, Bn=Bn, Bc=Bc: e.tensor_tensor(out=Bn[:, :, :].rearrange("p a b -> p (a b)"), in0=bB[0:64, :], in1=Bc[:, :, :].rearrange("p a b -> p (a b)"), op=ALU.add),
                             r=[bBk, nk("B", cur)], w=[nk("B", nxt)])
                    cur = nxt
            for c in range(2):
                cs = slice(c * 64, (c + 1) * 64)
                bX, bXk = bank()
                for h in range(H):
                    p.op("pe", lambda e, bX=bX, h=h, c=c: e.matmul(bX[0:64, h * 64:(h + 1) * 64], lhsT=QR[:, h, c, 0, :], rhs=M0[:, h, :], start=True, stop=False), r=["QR", "M0"], w=[bXk])
                    p.op("pe", lambda e, bX=bX, h=h, c=c: e.matmul(bX[0:64, h * 64:(h + 1) * 64], lhsT=GK[:, c, h, 0, :], rhs=VT[:, c, h, :], start=False, stop=True), r=[f"GK{c}", f"VT{c}"], w=[bXk])
                p.op("act", lambda e, bX=bX: e.copy(out=XSB[:, :, :].rearrange("p a b -> p (a b)"), in_=bX[0:64, :]), r=[bXk], w=["XSB"])
                bU, bUk = bank()
                for h in range(H):
                    p.op("pe", lambda e, bU=bU, h=h, c=c: e.matmul(bU[0:64, h * 64:(h + 1) * 64], lhsT=TMT[:, c, h, :], rhs=XSB[:, h, :], start=True, stop=True), r=[f"TMT{c}", "XSB"], w=[bUk])
                p.op("act", lambda e, bU=bU: e.mul(out=US[:, :, :].rearrange("p a b -> p (a b)"), in_=bU[0:64, :], mul=-1.0), r=[bUk], w=["US"])
                bY, bYk = bank()
                for h in range(H):
                    p.op("pe", lambda e, bY=bY, h=h, c=c: e.matmul(bY[0:64, h * 64:(h + 1) * 64], lhsT=QR[:, h, c, 1, :], rhs=M0[:, h, :], start=True, stop=False), r=["QR", "M0"], w=[bYk])
                    p.op("pe", lambda e, bY=bY, h=h, c=c: e.matmul(bY[0:64, h * 64:(h + 1) * 64], lhsT=GB[:, c, h, 1, :], rhs=US[:, h, :], start=False, stop=False), r=[f"GB{c}", "US"], w=[bYk])
                    p.op("pe", lambda e, bY=bY, h=h, c=c: e.matmul(bY[0:64, h * 64:(h + 1) * 64], lhsT=GK[:, c, h, 1, :], rhs=VT[:, c, h, :], start=False, stop=True), r=[f"GK{c}", f"VT{c}"], w=[bYk])
                bM, bMk = bank()
                for h in range(H):
                    p.op("pe", lambda e, bM=bM, h=h, c=c: e.matmul(bM[0:64, h * 64:(h + 1) * 64], lhsT=BTT[:, c, h, :], rhs=US[:, h, :], start=True, stop=False), r=[f"BTT{c}", "US"], w=[bMk])
                    p.op("pe", lambda e, bM=bM, h=h, c=c: e.matmul(bM[0:64, h * 64:(h + 1) * 64], lhsT=KTT[:, c, h, :], rhs=VT[:, c, h, :], start=False, stop=True), r=[f"KTT{c}", f"VT{c}"], w=[bMk])
                p.op("dve", lambda e, bM=bM: e.tensor_tensor(out=M0[:, :, :].rearrange("p a b -> p (a b)"), in0=bM[0:64, :], in1=M0[:, :, :].rearrange("p a b -> p (a b)"), op=ALU.add), r=[bMk, "M0"], w=["M0"])
                gc = G[:, :, c * 64 + 63:c * 64 + 64].to_broadcast([64, H, 64])
                TT("dve", f3(M0), f3(M0), gc, ALU.mult, ["M0", "G"], ["M0"])
                bR, bRk = bank()
                for h in range(H):
                    p.op("pe", lambda e, bR=bR, h=h, cs=cs: e.matmul(bR[0:64, h:h + 1], lhsT=RK[:, h, cs], rhs=PAR[:, 3, h:h + 1], start=True, stop=True), r=["RK", "PAR"], w=[bRk])
                bG, bGk = bank()
                p.op("pe", lambda e, bG=bG, cs=cs: e.matmul(bG[0:64, :], lhsT=SG[:, cs], rhs=GUP[:, :], start=True, stop=False), r=["SG", "GUP"], w=[bGk])
                p.op("pe", lambda e, bG=bG, cs=cs: e.matmul(bG[0:64, :], lhsT=SG2[:, cs], rhs=GUP2[:, :], start=False, stop=True), r=["SG2", "GUP2"], w=[bGk])
                p.op("act", lambda e, bY=bY: e.copy(out=YS[:, :, :].rearrange("p a b -> p (a b)"), in_=bY[0:64, :]), r=[bYk], w=["YS"])
                p.op("dve", lambda e: e.tensor_reduce(out=ST[:, 0, :], in_=f3(YS), axis=AX.X, op=ALU.add), r=["YS"], w=["ST"])
                p.op("dve", lambda e: e.tensor_scalar(out=ST[:, 0, :], in0=ST[:, 0, :], scalar1=1.0 / 64, scalar2=None, op0=ALU.mult), r=["ST"], w=["ST"])
                TT("dve", f3(YS), f3(YS), bc(ST[:, 0, :], 64), ALU.subtract, ["YS", "ST"], ["YS"])
                TT("pool", f3(YT), f3(YS), f3(YS), ALU.mult, ["YS"], ["YT"])
                p.op("dve", lambda e: e.tensor_reduce(out=ST[:, 1, :], in_=f3(YT), axis=AX.X, op=ALU.add), r=["YT"], w=["ST"])
                p.op("dve", lambda e: e.tensor_scalar(out=ST[:, 1, :], in0=ST[:, 1, :], scalar1=1.0 / 64, scalar2=LNX_EPS, op0=ALU.mult, op1=ALU.add), r=["ST"], w=["ST"])
                p.op("act", lambda e: e.sqrt(out=ST[:, 1, :], in_=ST[:, 1, :]), r=["ST"], w=["ST"])
                p.op("dve", lambda e: e.reciprocal(out=ST[:, 2, :], in_=ST[:, 1, :]), r=["ST"], w=["ST"])
                TT("dve", f3(YS), f3(YS), bc(ST[:, 2, :], 64), ALU.mult, ["YS", "ST"], ["YS"])
                YSf = YS[:, :, :].rearrange("p a b -> p (a b)")
                TT("pool", YSf, YSf, LNW[:, :], ALU.mult, ["YS", "LNW"], ["YS"])
                TT("pool", YSf, YSf, LNB[:, :], ALU.add, ["YS", "LNB"], ["YS"])
                p.op("act", lambda e, bR=bR: e.copy(out=ST[:, 3, :], in_=bR[0:64, 0:H]), r=[bRk], w=["ST"])
                TT("dve", f3(YT), VT[:, c, :, :], bc(ST[:, 3, :], 64), ALU.mult, [f"VT{c}", "ST"], ["YT"])
                TT("dve", f3(YS), f3(YS), f3(YT), ALU.add, ["YS", "YT"], ["YS"])
                ob = OUT[(2 * blk + c) % 2]; obk = f"OUT{(2 * blk + c) % 2}"
                TT("dve", ob[:, :], YSf, bG[0:64, :], ALU.mult, ["YS", bGk], [obk])
                r0 = t0 + c * 64
                p.dma("sp", y[r0:r0 + 64, :], ob[:, :], r=[obk], w=[f"y{r0}"])
    print("rwkv op counts", p.counts())
    return p.build()


def rwkv_inputs(PTc, h, inp, l, consts):
    T = lambda t: PTc[t * 128:(t + 1) * 128]
    pad = lambda a: np.concatenate([np.zeros(a.shape[:-1] + (1,), np.float32), a], -1)
    rkv = np.stack([np.concatenate([T(22 + 4 * i + j) for j in range(4)], 0).reshape(H, 64, S).transpose(1, 0, 2) for i in range(3)])
    own = slice(h * 512, (h + 1) * 512)
    mu = inp["rwkv_mu"][l]
    mu_rkv = np.stack([mu[i * 1024 + h * 512:i * 1024 + (h + 1) * 512].reshape(H, 64).T for i in range(3)], 1)
    muw = np.zeros((128, 3), np.float32)
    muw[:, 0] = mu[3072:3200]; muw[:, 1] = mu[3200:3328]; muw[:32, 2] = mu[3328:3360]
    gd = np.concatenate([T(35), PTc[36 * 128:36 * 128 + 32]], 0)
    fm = lambda a: np.ascontiguousarray(a[own].reshape(H, 64).T)
    par = np.stack([fm(inp["rwkv_a0"][l]), fm(inp["rwkv_k_k"][l]), fm(inp["rwkv_k_a"][l]),
                    np.ascontiguousarray(inp["rwkv_r_k"][l][h * 8:(h + 1) * 8].T)], 1)
    d = {
        "rkv": np.ascontiguousarray(pad(rkv)), "wad": np.ascontiguousarray(pad(T(34))), "gd": np.ascontiguousarray(pad(gd)),
        "mu": np.ascontiguousarray(mu_rkv), "muw": muw,
        "wa_up": np.ascontiguousarray(np.concatenate([inp["rwkv_w_up"][l][:, own], inp["rwkv_a_up"][l][:, own]], 0)),
        "g_up": np.ascontiguousarray(inp["rwkv_g_up"][l][:, own]),
        "w0b": np.ascontiguousarray(np.broadcast_to(inp["rwkv_w0"][l][own][None], (128, 512))),
        "par": np.ascontiguousarray(par),
        "lnwb": np.ascontiguousarray(np.stack([np.broadcast_to(inp["rwkv_lnx_w"][l][own][None], (64, 512)), np.broadcast_to(inp["rwkv_lnx_b"][l][own][None], (64, 512))])),
    }
    d.update(consts)
    return d


D = 2048
KT = 16
NTOK = 2048
PT_ = 1024
TC = 512
DFF = 8192


def build_C():
    p = Prog("C")
    D_ = lambda n, s, dt=F32: p.dram(n, s, dt, "ExternalInput")
    xT = D_("xT", [D, NTOK]); yT = D_("yT", [3, 1024, NTOK])
    wg = D_("wg", [3, D, D]); bgd = D_("bg", [128, 3, KT]); wb = D_("wb", [3, 1024, D]); wo = D_("wo", [D, D])
    wu = D_("wu", [D, DFF]); wd = D_("wd", [DFF, D]); gn = D_("gn", [128, 4, KT])
    x1T = p.dram("x1T", [D, NTOK], F32, "ExternalOutput")
    x2T = p.dram("x2T", [D, NTOK], F32, "ExternalOutput")

    R1 = p.sb([128, 16384], F32, "R1")
    R2 = p.sb([128, 8192], F32, "R2")
    uT = R1[:, 0:8192].bitcast(BF16).rearrange("p (k t) -> p k t", t=PT_)
    y01 = R1[:, 8192:16384].bitcast(BF16).rearrange("p (k t) -> p k t", t=PT_)
    y2 = R2[:, 0:4096].bitcast(BF16).rearrange("p (k t) -> p k t", t=PT_)
    Z = R1[:, :].rearrange("p (k t) -> p k t", t=PT_)
    HG = R2[:, :].bitcast(BF16).rearrange("p (k t) -> p k t", t=PT_)
    mT = p.sb([128, KT, PT_], BF16, "mT")
    xs = R2[:, :].rearrange("p (k t) -> p k t", t=TC)
    sq = p.sb([128, 4, TC], F32, "sq")
    rs = p.sb([128, TC], F32, "rs")
    tmpc = [p.sb([128, TC], F32, f"tmpc{i}") for i in range(2)]
    sg = [p.sb([128, TC], F32, f"sg{i}") for i in range(2)]
    macc = p.sb([128, 2, TC], F32, "macc")
    NW = 3
    wst = [p.sb([128, KT, 128], F32, f"wst{i}") for i in range(NW)]
    wbf = [p.sb([128, KT, 128], BF16, f"wbf{i}") for i in range(NW)]
    gcol = p.sb([128, 4, KT], F32, "gcol")
    bgs = p.sb([128, 3, KT], F32, "bgs")
    ones_f = p.sb([128, 128], F32, "ones_f")
    pss = [(p.ps([128, 512], F32, f"psb{i}"), f"psb{i}") for i in range(7)]
    psn = (p.ps([128, 512], F32, "psn"), "psn")
    cnt = dict(w=0, ps=0, t=0)

    def nextps():
        i = cnt["ps"] % len(pss); cnt["ps"] += 1
        return pss[i]

    def gemm_tile(wcols, kt, act, akey, nch, out_fn):
        b = cnt["w"] % NW; cnt["w"] += 1
        q = ("sp", "act")[cnt["w"] % 2]
        p.dma(q, wst[b][:, 0:kt, :], wcols.rearrange("(k q) n -> q k n", q=128), w=[f"wst{b}"])
        ceng = ("pool", "dve")[cnt["w"] % 2]
        p.op(ceng, lambda e: e.tensor_copy(out=wbf[b][:, 0:kt, :], in_=wst[b][:, 0:kt, :]), r=[f"wst{b}"], w=[f"wbf{b}"])
        for c in range(nch):
            ps, pk = nextps()
            for k in range(kt):
                p.op("pe", lambda e, k=k, ps=ps, c=c: e.matmul(ps[:, :], lhsT=wbf[b][:, k, :], rhs=act[:, k, c * TC:(c + 1) * TC], start=(k == 0), stop=(k == kt - 1)),
                     r=[f"wbf{b}", akey(c)], w=[pk])
            out_fn(c, ps, pk)

    def norm_rs(src, skeys, gi, eps=1e-6):
        for k in range(KT):
            j = k % 4
            sap = src(k)
            if k % 2 == 0:
                p.op("act", lambda e, sap=sap, j=j: e.activation(out=sq[:, j, :], in_=sap, func=AF.Square), r=[skeys(k)], w=[f"sq{j}"])
            else:
                p.op("pool", lambda e, sap=sap, j=j: e.tensor_tensor(out=sq[:, j, :], in0=sap, in1=sap, op=ALU.mult), r=[skeys(k)], w=[f"sq{j}"])
            p.op("pe", lambda e, k=k, j=j: e.matmul(psn[0][:, :], lhsT=ones_f[:, :], rhs=sq[:, j, :], start=(k == 0), stop=(k == KT - 1)), r=[f"sq{j}", "ones_f"], w=[psn[1]])
        p.op("dve", lambda e: e.tensor_scalar(out=rs[:, :], in0=psn[0][:, :], scalar1=1.0 / D, scalar2=eps, op0=ALU.mult, op1=ALU.add), r=[psn[1]], w=["rs"])
        p.op("act", lambda e: e.sqrt(out=rs[:, :], in_=rs[:, :]), r=["rs"], w=["rs"])
        p.op("dve", lambda e: e.reciprocal(out=rs[:, :], in_=rs[:, :]), r=["rs"], w=["rs"])

    def fence(keys):
        p.op("dve", lambda e: e.memset(rs[:, 0:1], 0.0), r=list(keys), w=list(keys) + ["rs"])

    R1K = ["uT0", "uT1", "y0", "y1", "Z0", "Z1"]
    R2K = ["xs_a", "xs_b", "y2", "HG0", "HG1"]
    xv = xT.rearrange("(k q) t -> q k t", q=128)
    x1v = x1T.rearrange("(k q) t -> q k t", q=128)
    x2v = x2T.rearrange("(k q) t -> q k t", q=128)

    with p.nc.allow_low_precision("bf16 matmul operands, fp32 accumulation"):
        p.dma("sp", gcol[:, :, :], gn, w=["gcol"])
        p.dma("sp", bgs[:, :, :], bgd, w=["bgs"])
        p.op("dve", lambda e: e.memset(ones_f[:, :], 1.0), w=["ones_f"])
        for ps_ in range(NTOK // PT_):
            tp0 = ps_ * PT_
            for c in range(PT_ // TC):
                gsl = slice(tp0 + c * TC, tp0 + (c + 1) * TC)
                p.dma("sp", xs[:, 0:8, :], xv[:, 0:8, gsl], w=["xs_a"])
                p.dma("act", xs[:, 8:16, :], xv[:, 8:16, gsl], w=["xs_b"])
                norm_rs(lambda k: xs[:, k, :], lambda k: "xs_a" if k < 8 else "xs_b", 0)
                for k in range(KT):
                    p.op("dve", lambda e, k=k, c=c: e.scalar_tensor_tensor(out=uT[:, k, c * TC:(c + 1) * TC], in0=xs[:, k, :], scalar=gcol[:, 0, k:k + 1], in1=rs[:, :], op0=ALU.mult, op1=ALU.mult),
                         r=["xs_a" if k < 8 else "xs_b", "rs", "gcol"], w=[f"uT{c}"])
            fence(R2K)
            for b in range(3):
                for ft in range(8):
                    i = cnt["t"] % 2; cnt["t"] += 1
                    st = tmpc[i]
                    for c in range(2):
                        buf = (tmpc, sg)[c][i]; bk = ("tmpc", "sg")[c] + str(i)
                        p.dma(("sp", "act")[c], buf[:, :], yT[b, ft * 128:(ft + 1) * 128, tp0 + c * TC:tp0 + (c + 1) * TC], w=[bk])
                        dst = (y01[:, b * 8 + ft, c * TC:(c + 1) * TC] if b < 2 else y2[:, ft, c * TC:(c + 1) * TC])
                        p.op(("pool", "dve")[c], lambda e, buf=buf, dst=dst: e.tensor_copy(out=dst, in_=buf[:, :]), r=[bk], w=[f"y{b}"])
            for nt in range(KT):
                nsl = slice(nt * 128, (nt + 1) * 128)
                for b in range(3):
                    def gate_out(c, ps, pk, b=b, nt=nt):
                        p.op("act", lambda e: e.activation(out=sg[c][:, :], in_=ps[:, :], func=AF.Sigmoid, bias=bgs[:, b, nt:nt + 1]), r=[pk, "bgs"], w=[f"sg{c}"])
                    gemm_tile(wg[b, :, nsl], KT, uT, lambda c: f"uT{c}", 2, gate_out)
                    yact = (y01[:, b * 8:(b + 1) * 8, :] if b < 2 else y2)

                    def br_out(c, ps, pk, b=b):
                        if b == 0:
                            p.op("dve", lambda e: e.tensor_tensor(out=macc[:, c, :], in0=sg[c][:, :], in1=ps[:, :], op=ALU.mult), r=[pk, f"sg{c}"], w=[f"macc{c}"])
                        else:
                            p.op("dve", lambda e: e.tensor_tensor(out=tmpc[c][:, :], in0=sg[c][:, :], in1=ps[:, :], op=ALU.mult), r=[pk, f"sg{c}"], w=[f"tmpc{c}"])
                            p.op("pool", lambda e: e.tensor_tensor(out=macc[:, c, :], in0=macc[:, c, :], in1=tmpc[c][:, :], op=ALU.add), r=[f"macc{c}", f"tmpc{c}"], w=[f"macc{c}"])
                    gemm_tile(wb[b, :, nsl], 8, yact, lambda c, b=b: f"y{b}", 2, br_out)
                for c in range(2):
                    p.op("act", lambda e, c=c, nt=nt: e.copy(out=mT[:, nt, c * TC:(c + 1) * TC], in_=macc[:, c, :]), r=[f"macc{c}"], w=[f"mT{c}"])
            fence(R1K + R2K)
            for nt in range(KT):
                def z_out(c, ps, pk, nt=nt):
                    eng = ("act", "dve")[c]
                    if eng == "act":
                        p.op("act", lambda e: e.copy(out=Z[:, nt, c * TC:(c + 1) * TC], in_=ps[:, :]), r=[pk], w=[f"Z{c}"])
                    else:
                        p.op("dve", lambda e: e.tensor_copy(out=Z[:, nt, c * TC:(c + 1) * TC], in_=ps[:, :]), r=[pk], w=[f"Z{c}"])
                gemm_tile(wo[:, nt * 128:(nt + 1) * 128], KT, mT, lambda c: f"mT{c}", 2, z_out)
            for c in range(PT_ // TC):
                csl = slice(c * TC, (c + 1) * TC)
                gsl = slice(tp0 + c * TC, tp0 + (c + 1) * TC)
                norm_rs(lambda k: Z[:, k, csl], lambda k: f"Z{c}", 1)
                p.dma("sp", xs[:, 0:8, :], xv[:, 0:8, gsl], w=["xs_a"])
                p.dma("act", xs[:, 8:16, :], xv[:, 8:16, gsl], w=["xs_b"])
                for k in range(KT):
                    xk = "xs_a" if k < 8 else "xs_b"
                    i = k % 2
                    p.op("dve", lambda e, k=k, i=i, csl=csl: e.scalar_tensor_tensor(out=tmpc[i][:, :], in0=Z[:, k, csl], scalar=gcol[:, 1, k:k + 1], in1=rs[:, :], op0=ALU.mult, op1=ALU.mult),
                         r=[f"Z{c}", "rs", "gcol"], w=[f"tmpc{i}"])
                    p.op("pool", lambda e, k=k, i=i: e.tensor_tensor(out=xs[:, k, :], in0=xs[:, k, :], in1=tmpc[i][:, :], op=ALU.add), r=[xk, f"tmpc{i}"], w=[xk])
                p.dma("sp", x1v[:, 0:8, gsl], xs[:, 0:8, :], r=["xs_a"], w=[f"x1a{ps_}_{c}"])
                p.dma("act", x1v[:, 8:16, gsl], xs[:, 8:16, :], r=["xs_b"], w=[f"x1b{ps_}_{c}"])
                norm_rs(lambda k: xs[:, k, :], lambda k: "xs_a" if k < 8 else "xs_b", 2)
                for k in range(KT):
                    p.op("dve", lambda e, k=k, csl=csl: e.scalar_tensor_tensor(out=mT[:, k, csl], in0=xs[:, k, :], scalar=gcol[:, 2, k:k + 1], in1=rs[:, :], op0=ALU.mult, op1=ALU.mult),
                         r=["xs_a" if k < 8 else "xs_b", "rs", "gcol"], w=[f"mT{c}"])
            fence(R2K)
            for ffg in range(4):
                for ft in range(16):
                    f = ffg * 16 + ft

                    def up_out(c, ps, pk, ft=ft):
                        p.op("act", lambda e: e.activation(out=tmpc[c][:, :], in_=ps[:, :], func=AF.Relu), r=[pk], w=[f"tmpc{c}"])
                        p.op("pool", lambda e: e.tensor_tensor(out=HG[:, ft, c * TC:(c + 1) * TC], in0=tmpc[c][:, :], in1=tmpc[c][:, :], op=ALU.mult), r=[f"tmpc{c}"], w=[f"HG{c}"])
                    gemm_tile(wu[:, f * 128:(f + 1) * 128], KT, mT, lambda c: f"mT{c}", 2, up_out)
                for nt in range(KT):
                    def dn_out(c, ps, pk, nt=nt, ffg=ffg):
                        if ffg == 0:
                            p.op("act", lambda e: e.copy(out=Z[:, nt, c * TC:(c + 1) * TC], in_=ps[:, :]), r=[pk], w=[f"Z{c}"])
                        else:
                            p.op("dve", lambda e: e.tensor_tensor(out=Z[:, nt, c * TC:(c + 1) * TC], in0=Z[:, nt, c * TC:(c + 1) * TC], in1=ps[:, :], op=ALU.add), r=[pk, f"Z{c}"], w=[f"Z{c}"])
                    gemm_tile(wd[ffg * 2048:(ffg + 1) * 2048, nt * 128:(nt + 1) * 128], KT, HG, lambda c: f"HG{c}", 2, dn_out)
            fence(R2K)
            for c in range(PT_ // TC):
                csl = slice(c * TC, (c + 1) * TC)
                gsl = slice(tp0 + c * TC, tp0 + (c + 1) * TC)
                norm_rs(lambda k: Z[:, k, csl], lambda k: f"Z{c}", 3)
                p.dma("sp", xs[:, 0:8, :], x1v[:, 0:8, gsl], r=[f"x1a{ps_}_{c}"], w=["xs_a"])
                p.dma("act", xs[:, 8:16, :], x1v[:, 8:16, gsl], r=[f"x1b{ps_}_{c}"], w=["xs_b"])
                for k in range(KT):
                    xk = "xs_a" if k < 8 else "xs_b"
                    i = k % 2
                    p.op("dve", lambda e, k=k, i=i, csl=csl: e.scalar_tensor_tensor(out=tmpc[i][:, :], in0=Z[:, k, csl], scalar=gcol[:, 3, k:k + 1], in1=rs[:, :], op0=ALU.mult, op1=ALU.mult),
                         r=[f"Z{c}", "rs", "gcol"], w=[f"tmpc{i}"])
                    p.op("pool", lambda e, k=k, i=i: e.tensor_tensor(out=xs[:, k, :], in0=xs[:, k, :], in1=tmpc[i][:, :], op=ALU.add), r=[xk, f"tmpc{i}"], w=[xk])
                p.dma("sp", x2v[:, 0:8, gsl], xs[:, 0:8, :], r=["xs_a"], w=[f"x2a{ps_}_{c}"])
                p.dma("act", x2v[:, 8:16, gsl], xs[:, 8:16, :], r=["xs_b"], w=[f"x2b{ps_}_{c}"])
            fence(R1K + R2K)
    print("C op counts", p.counts())
    return p.build()


def c_inputs(x_b, h, ycat_b, inp, l):
    tsl = slice(h * NTOK, (h + 1) * NTOK)
    fm = lambda a: np.ascontiguousarray(a.reshape(KT, 128).T)
    gn = np.stack([fm(inp["norm_pre_mix"][l]), fm(inp["norm_post_mix"][l]), fm(inp["norm_pre_mlp"][l]), fm(inp["norm_post_mlp"][l])], 1)
    bg = np.stack([fm(inp["b_gate"][l][b]) for b in range(3)], 1)
    return {
        "xT": np.ascontiguousarray(x_b[tsl].T),
        "yT": np.ascontiguousarray(ycat_b[tsl].T.reshape(3, 1024, NTOK)),
        "wg": np.ascontiguousarray(inp["w_gate"][l]), "bg": np.ascontiguousarray(bg), "wb": np.ascontiguousarray(inp["w_branch"][l]),
        "wo": np.ascontiguousarray(inp["w_out"][l]), "wu": np.ascontiguousarray(inp["w_up"][l]), "wd": np.ascontiguousarray(inp["w_down"][l]),
        "gn": np.ascontiguousarray(gn),
    }


_PROGS = {}


def _prog(name, fn):
    return fn()


import os as _os, time as _time
_DBG = _os.environ.get("KDBG")
_T0 = [_time.time()]


def _run(nc, in_maps, tag=""):
    print("launch", tag, "build+prep done at", round(_time.time() - _T0[0], 1), flush=True)
    r = run_bass_kernel_spmd(nc, in_maps, core_ids=list(range(8))).results
    print("launch", tag, "finished at", round(_time.time() - _T0[0], 1), flush=True)
    if _DBG:
        for c in (0, 1):
            for k, v in r[c].items():
                np.save(f"{_DBG}/{tag}_{c}_{k}.npy", np.asarray(v))
    return r


def kernel(**inp):
    inp = {k: np.asarray(v) for k, v in inp.items()}
    x = np.ascontiguousarray(inp["x"], dtype=np.float32)
    pos = inp["positions"].astype(np.int32)
    B = x.shape[0]
    _T0[0] = _time.time()
    nconst = nsa_consts()
    rconst = rwkv_consts()
    for l in range(2):
        in_maps = []
        for c in range(8):
            b, h = c // 2, c % 2
            in_maps.append({"xT": np.ascontiguousarray(x[b].T), "w": np.ascontiguousarray(inp["w_in"][l][:, core_cols(h)]),
                            "g": np.ascontiguousarray(inp["norm_pre_mix"][l].reshape(KT, 128).T)})
        PT = [r["PT"] for r in _run(build_A(), in_maps, f"A{l}")]
        in_n, in_d, in_r = [], [], []
        for c in range(8):
            b, h = c // 2, c % 2
            in_n.append(nsa_inputs(PT[c], pos[b], inp["nsa_cmp_pos"][l], inp["nsa_cmp_w1"][l], inp["nsa_cmp_w2"][l], nconst))
            in_d.append(diff_inputs(PT[c], pos[b], inp["diff_lambda"][l], inp["diff_subln"][l], layer=l))
            in_r.append(rwkv_inputs(PT[c], h, inp, l, rconst))
        yn = [r["y"] for r in _run(build_nsa(), in_n, f"N{l}")]
        del in_n
        yd = [r["y"] for r in _run(build_diff(), in_d, f"D{l}")]
        del in_d
        yr = [r["y"] for r in _run(build_rwkv(), in_r, f"R{l}")]
        del in_r, PT
        in_c = []
        for c in range(8):
            b, h = c // 2, c % 2
            ycat = np.concatenate([yn[2 * b], yn[2 * b + 1],
                                   yd[2 * b][0], yd[2 * b][1], yd[2 * b + 1][0], yd[2 * b + 1][1],
                                   yr[2 * b], yr[2 * b + 1]], axis=1)
            in_c.append(c_inputs(x[b], h, ycat, inp, l))
        res = _run(build_C(), in_c, f"C{l}")
        xn = np.empty_like(x)
        for c in range(8):
            b, h = c // 2, c % 2
            xn[b, h * NTOK:(h + 1) * NTOK] = res[c]["x2T"].T
        x = xn
    return x
```

```python
import numpy as np
from contextlib import ExitStack
import concourse.bass as bass
import concourse.mybir as mybir
from concourse.bass_utils import run_bass_kernel_spmd

F32 = mybir.dt.float32
BF16 = mybir.dt.bfloat16
I32 = mybir.dt.int32
AF = mybir.ActivationFunctionType
ALU = mybir.AluOpType
AX = mybir.AxisListType

SEM_LIM = 16000
NDMA = 24


class Prog:
    ENG = ("pe", "dve", "act", "pool", "sp")

    def __init__(self, name="k"):
        self.nc = bass.Bass("TRN2", target_bir_lowering=False)
        self.es = ExitStack()
        nc = self.nc
        self.eng = {"pe": nc.tensor, "dve": nc.vector, "act": nc.scalar, "pool": nc.gpsimd, "sp": nc.sync}
        self.ops = {e: [] for e in self.ENG}
        self.sem = {}
        self.cnt = {}
        self.nsem = 0
        self.pesems = set()
        for e in self.ENG:
            self._newsem(e)
        self.lastw = {}
        self.rd = {}
        self.waited = {e: {} for e in self.ENG}
        self.dslots = []
        for i in range(NDMA):
            s = self.es.enter_context(nc.semaphore(f"dq{i}"))
            self.dslots.append([s, 0])
        self.dnext = 0
        self.ntile = 0
        self.psn = 0
        self.prefix = ""
        self.cur_es = None

    def begin_phase(self, prefix):
        self.prefix = prefix
        self.cur_es = ExitStack()

    def end_phase(self):
        for e in self.ENG:
            wl = []
            for e2 in self.ENG:
                if e2 != e and self.cnt[e2] > 0:
                    s = self.sem[e2]; v = self.cnt[e2]; k = self._sk(s)
                    if self.waited[e].get(k, 0) < v:
                        self.waited[e][k] = v
                        wl.append((s, v))
            for s, v in self.dslots:
                k = self._sk(s)
                if v > 0 and self.waited[e].get(k, 0) < v:
                    self.waited[e][k] = v
                    wl.append((s, v))
            self.ops[e].append((wl, None, None))
        self.cur_es.close()
        self.cur_es = None
        self.prefix = ""
        self.lastw = {}
        self.rd = {}

    def _newsem(self, e):
        self.nsem += 1
        s = self.es.enter_context(self.nc.semaphore(f"s{e}{self.nsem}"))
        self.sem[e] = s
        self.cnt[e] = 0
        if e == "pe":
            self.pesems.add(s.name if hasattr(s, "name") else id(s))

    def dram(self, name, shape, dt, kind):
        return self.nc.dram_tensor(self.prefix + name, list(shape), dt, kind=kind).ap()

    def sb(self, shape, dt, name=None):
        self.ntile += 1
        name = self.prefix + (name or f"t{self.ntile}")
        return (self.cur_es or self.es).enter_context(self.nc.sbuf_tensor(name, list(shape), dt))

    def ps(self, shape, dt=F32, name=None):
        self.psn += 1
        name = self.prefix + (name or f"ps{self.psn}")
        return (self.cur_es or self.es).enter_context(self.nc.psum_tensor(name, list(shape), dt))

    @staticmethod
    def _sk(s):
        return s.name if hasattr(s, "name") else id(s)

    def op(self, e, fn, r=(), w=(), dma=False):
        waits = {}

        def need(ev):
            if ev is None:
                return
            s, v = ev
            k = self._sk(s)
            if k not in waits or waits[k][1] < v:
                waits[k] = (s, v)

        for k in r:
            need(self.lastw.get(k))
        for k in w:
            need(self.lastw.get(k))
            for ev in self.rd.get(k, {}).values():
                need(ev)
        if dma:
            slot = self.dslots[self.dnext % NDMA]
            self.dnext += 1
            if slot[1] > 0:
                need((slot[0], slot[1]))
            slot[1] += 16
            ev = (slot[0], slot[1])
            inc = (slot[0], 16)
        else:
            if self.cnt[e] >= SEM_LIM:
                self._newsem(e)
            self.cnt[e] += 1
            ev = (self.sem[e], self.cnt[e])
            inc = (self.sem[e], 1)
        wl = []
        for k, (s, v) in waits.items():
            if e == "pe" and k in self.pesems:
                continue
            if self.waited[e].get(k, 0) >= v:
                continue
            self.waited[e][k] = v
            wl.append((s, v))
        self.ops[e].append((wl, fn, inc))
        evk = self._sk(ev[0])
        for k in r:
            self.rd.setdefault(k, {})[evk] = ev
        for k in w:
            self.lastw[k] = ev
            self.rd[k] = {}
        return ev

    def dma(self, q, out, in_, r=(), w=()):
        return self.op(q, lambda e: e.dma_start(out=out, in_=in_), r=r, w=w, dma=True)

    def build(self):
        nc = self.nc
        ops = self.ops
        dslots = self.dslots

        def run(e, lst, final=False):
            for wl, fn, inc in lst:
                for s, v in wl:
                    e.wait_ge(s, v)
                if fn is not None:
                    fn(e).then_inc(inc[0], inc[1])
            if final:
                for s, v in dslots:
                    if v > 0:
                        e.wait_ge(s, v)

        with nc.Block() as block:
            @block.tensor
            def _(e):
                run(e, ops["pe"])

            @block.vector
            def _(e):
                run(e, ops["dve"])

            @block.scalar
            def _(e):
                run(e, ops["act"])

            @block.gpsimd
            def _(e):
                run(e, ops["pool"])

            @block.sync
            def _(e):
                run(e, ops["sp"], final=True)
        self.es.close()
        return nc

    def counts(self):
        return {e: len(v) for e, v in self.ops.items()}


D = 2048
S = 4096
KT = D // 128
TC = 512
NCH = S // TC

NSA_OFF, DIFF_OFF, RWKV_OFF = 0, 2584, 5656


def core_cols(h):
    c = []
    r = lambda a, n: list(range(a, a + n))
    for hd in range(4 * h, 4 * h + 4):
        c += r(NSA_OFF + hd * 128, 128)
    for i in range(6):
        c += r(NSA_OFF + 1024 + i * 256 + h * 128, 128)
    for base in (DIFF_OFF, DIFF_OFF + 1024):
        c += r(base + 2 * h * 256, 512)
    c += r(DIFF_OFF + 2048 + 2 * h * 256, 512)
    for i in range(3):
        c += r(RWKV_OFF + i * 1024 + h * 512, 512)
    c += r(RWKV_OFF + 3072, 64 + 64 + 160)
    c += r(NSA_OFF + 2560 + 12 * h, 12)
    return np.array(c)


NCOL = 4652
NT = (NCOL + 127) // 128


def emit_rmsnorm_T(p, xT, gcol, uT, ones_f, ps_bank, pskey, stage, t0=0, ntok=S, qs=("sp", "act"), D_=D, eps=1e-6, tag="n"):
    kt = D_ // 128
    xv = xT.rearrange("(k q) t -> q k t", q=128)
    xs, sq, rs = stage
    nsq = 4
    half = kt // 2
    for c in range(ntok // TC):
        tsl = slice(c * TC, (c + 1) * TC)
        gsl = slice(t0 + c * TC, t0 + (c + 1) * TC)
        p.dma(qs[0], xs[:, 0:half, :], xv[:, 0:half, gsl], w=[f"{tag}xs_a"])
        p.dma(qs[1], xs[:, half:kt, :], xv[:, half:kt, gsl], w=[f"{tag}xs_b"])
        for k in range(kt):
            j = k % nsq
            xk = f"{tag}xs_a" if k < half else f"{tag}xs_b"
            if k % 2 == 0:
                p.op("act", lambda e, k=k, j=j: e.activation(out=sq[:, j, :], in_=xs[:, k, :], func=AF.Square), r=[xk], w=[f"{tag}sq{j}"])
            else:
                p.op("pool", lambda e, k=k, j=j: e.tensor_tensor(out=sq[:, j, :], in0=xs[:, k, :], in1=xs[:, k, :], op=ALU.mult), r=[xk], w=[f"{tag}sq{j}"])
            p.op("pe", lambda e, k=k, j=j: e.matmul(ps_bank[:, 0:TC], lhsT=ones_f[:, :], rhs=sq[:, j, :], start=(k == 0), stop=(k == kt - 1)),
                 r=[f"{tag}sq{j}", "ones_f"], w=[pskey])
        p.op("dve", lambda e: e.tensor_scalar(out=rs[:, :], in0=ps_bank[:, 0:TC], scalar1=1.0 / D_, scalar2=eps, op0=ALU.mult, op1=ALU.add),
             r=[pskey], w=[f"{tag}rs"])
        p.op("act", lambda e: e.sqrt(out=rs[:, :], in_=rs[:, :]), r=[f"{tag}rs"], w=[f"{tag}rs"])
        p.op("dve", lambda e: e.reciprocal(out=rs[:, :], in_=rs[:, :]), r=[f"{tag}rs"], w=[f"{tag}rs"])
        for k in range(kt):
            p.op("dve", lambda e, k=k, tsl=tsl: e.scalar_tensor_tensor(out=uT[:, k, tsl], in0=xs[:, k, :], scalar=gcol[:, k:k + 1], in1=rs[:, :],
                                                                     op0=ALU.mult, op1=ALU.mult),
                 r=[f"{tag}xs_a" if k < half else f"{tag}xs_b", f"{tag}rs", "gcol"], w=[f"uT{c}"])


def emit_gemm_T(p, uT, ukeys, wdram, ncols, out_fn, wst, wbf, pss, kt=KT, ntok=S, tag="g", evac=None):
    wv = wdram.rearrange("(k q) n -> q k n", q=128)
    nt_n = (ncols + 127) // 128
    nch = ntok // TC
    psi = 0
    for nt in range(nt_n):
        m = min(128, ncols - nt * 128)
        b = nt % 2
        q = ("sp", "act")[nt % 2]
        p.dma(q, wst[b][:, :, 0:m], wv[:, :, nt * 128:nt * 128 + m], w=[f"{tag}wst{b}"])
        ceng = ("pool", "dve")[nt % 2]
        p.op(ceng, lambda e, b=b, m=m: e.tensor_copy(out=wbf[b][:, :, 0:m], in_=wst[b][:, :, 0:m]),
             r=[f"{tag}wst{b}"], w=[f"{tag}wbf{b}"])
        for c in range(nch):
            ps, pk = pss[psi % len(pss)]
            psi += 1
            for k in range(kt):
                p.op("pe", lambda e, k=k, b=b, m=m, ps=ps, c=c: e.matmul(ps[0:m, 0:TC], lhsT=wbf[b][:, k, 0:m], rhs=uT[:, k, c * TC:(c + 1) * TC],
                                                                        start=(k == 0), stop=(k == kt - 1)),
                     r=[f"{tag}wbf{b}", ukeys(c)], w=[pk])
            out_fn(nt, m, c, ps, pk)


HT = 2048


def build_A():
    p = Prog("A")
    xT = p.dram("xT", [D, S], F32, "ExternalInput")
    w = p.dram("w", [D, NCOL], F32, "ExternalInput")
    g = p.dram("g", [128, KT], F32, "ExternalInput")
    PT = p.dram("PT", [NCOL, S], F32, "ExternalOutput")
    uT = p.sb([128, KT, HT], BF16, "uT")
    gcol = p.sb([128, KT], F32, "gcol")
    ones_f = p.sb([128, 128], F32, "ones_f")
    xs = p.sb([128, KT, TC], F32, "xs")
    sq = p.sb([128, 4, TC], F32, "sq")
    rs = p.sb([128, TC], F32, "rs")
    wst = [p.sb([128, KT, 128], F32, f"wst{i}") for i in range(2)]
    wbf = [p.sb([128, KT, 128], BF16, f"wbf{i}") for i in range(2)]
    ost = [p.sb([128, HT], F32, f"ost{i}") for i in range(2)]
    pss = [(p.ps([128, 512], F32, f"psb{i}"), f"psb{i}") for i in range(6)]
    psn = (p.ps([128, 512], F32, "psn"), "psn")
    p.dma("sp", gcol[:, :], g[:, :], w=["gcol"])
    p.op("dve", lambda e: e.memset(ones_f[:, :], 1.0), w=["ones_f"])
    with p.nc.allow_low_precision("bf16 matmul operands, fp32 accumulation"):
        for hp in range(S // HT):
            emit_rmsnorm_T(p, xT, gcol, uT, ones_f, psn[0], psn[1], (xs, sq, rs), t0=hp * HT, ntok=HT)

            def out_fn(nt, m, c, ps, pk, hp=hp):
                b = nt % 2
                if c % 2 == 0:
                    p.op("act", lambda e: e.copy(out=ost[b][0:m, c * TC:(c + 1) * TC], in_=ps[0:m, 0:TC]), r=[pk], w=[f"ost{b}"])
                else:
                    p.op("dve", lambda e: e.tensor_copy(out=ost[b][0:m, c * TC:(c + 1) * TC], in_=ps[0:m, 0:TC]), r=[pk], w=[f"ost{b}"])
                if c == HT // TC - 1:
                    p.dma("pool", PT[nt * 128:nt * 128 + m, hp * HT:(hp + 1) * HT], ost[b][0:m, :], r=[f"ost{b}"], w=[f"PT{nt}_{hp}"])

            emit_gemm_T(p, uT, lambda c: f"uT{c}", w, NCOL, out_fn, wst, wbf, pss, ntok=HT)
    print("A op counts", p.counts())
    return p.build()


def run_A(x, w_in_l, g_l):
    nc = build_A()
    in_maps = []
    for c in range(8):
        b, h = c // 2, c % 2
        in_maps.append({
            "xT": np.ascontiguousarray(x[b].T),
            "w": np.ascontiguousarray(w_in_l[:, core_cols(h)]),
            "g": np.ascontiguousarray(g_l.reshape(KT, 128).T),
        })
    res = run_bass_kernel_spmd(nc, in_maps, core_ids=list(range(8)))
    return [r["PT"] for r in res.results]


import math

S = 4096
TC = 512
NQC = S // TC
PI = math.pi
SCALE = 128 ** -0.5


def rope_consts():
    half = 16
    invf = (500000.0 ** (-np.arange(half, dtype=np.float32) / half)).astype(np.float32)
    c = np.zeros((32, 4), np.float32)
    c[:, 0] = np.concatenate([invf, invf])
    c[:16, 1] = -1.0; c[16:, 1] = 1.0
    c[:16, 2] = -PI; c[16:, 2] = PI
    c[:, 3] = PI
    return c


def causal_masks():
    m = np.zeros((4, 128, 512), np.float32)
    p = np.arange(128)[:, None]; f = np.arange(512)[None, :]
    for j in range(4):
        m[j] = (128 * j + p <= f)
    return m


def emit_sincos(p, ang, angk, C32, S32, tmp_i, tmp2, tmp2k, rc):
    H = tmp_i.shape[1]
    for dst, dk, shift in ((S32, "S32", 0.0), (C32, "C32", PI / 2)):
        for hf in range(S // H):
            sl = slice(hf * H, (hf + 1) * H)
            a = ang[:, sl]; d = dst[:, sl]; t2 = tmp2[:, sl]
            p.op("dve", lambda e, a=a, d=d, shift=shift: e.tensor_scalar(out=d, in0=a, scalar1=shift, scalar2=None, op0=ALU.add), r=[angk], w=[dk])
            p.op("dve", lambda e, d=d, t2=t2: e.tensor_scalar(out=t2, in0=d, scalar1=1.0 / (2 * PI), scalar2=None, op0=ALU.mult), r=[dk], w=[tmp2k])
            p.op("dve", lambda e, t2=t2: e.tensor_copy(out=tmp_i[:, :], in_=t2), r=[tmp2k], w=["tmp_i"])
            p.op("dve", lambda e, t2=t2: e.tensor_copy(out=t2, in_=tmp_i[:, :]), r=["tmp_i"], w=[tmp2k])
            p.op("dve", lambda e, d=d, t2=t2: e.scalar_tensor_tensor(out=d, in0=t2, scalar=-2 * PI, in1=d, op0=ALU.mult, op1=ALU.add), r=[tmp2k, dk], w=[dk])
            p.op("dve", lambda e, d=d, t2=t2: e.tensor_single_scalar(out=t2, in_=d, scalar=PI, op=ALU.is_gt), r=[dk], w=[tmp2k])
            p.op("dve", lambda e, d=d, t2=t2: e.scalar_tensor_tensor(out=d, in0=t2, scalar=-2 * PI, in1=d, op0=ALU.mult, op1=ALU.add), r=[tmp2k, dk], w=[dk])
            p.op("dve", lambda e, d=d, t2=t2: e.tensor_single_scalar(out=t2, in_=d, scalar=-PI, op=ALU.is_lt), r=[dk], w=[tmp2k])
            p.op("dve", lambda e, d=d, t2=t2: e.scalar_tensor_tensor(out=d, in0=t2, scalar=2 * PI, in1=d, op0=ALU.mult, op1=ALU.add), r=[tmp2k, dk], w=[dk])
    p.op("act", lambda e: e.activation(out=S32[:, :], in_=S32[:, :], func=AF.Sin, scale=rc[:, 1:2]), r=["S32", "rc"], w=["S32"])
    p.op("act", lambda e: e.activation(out=C32[:, :], in_=C32[:, :], func=AF.Sin), r=["C32"], w=["C32"])


def emit_rope_load(p, zT_dram, zsw_dram, dest_bf, dkey, stage, stsw, C32, S32, tag, q="sp"):
    p.dma(q, stage[:, :], zT_dram, w=[f"{tag}st"])
    p.dma("act" if q == "sp" else "sp", stsw[:, :], zsw_dram, w=["sw"])
    p.op("dve", lambda e: e.tensor_tensor(out=stage[0:32, :], in0=stage[0:32, :], in1=C32[:, :], op=ALU.mult), r=[f"{tag}st", "C32"], w=[f"{tag}st"])
    p.op("pool", lambda e: e.tensor_tensor(out=stsw[:, :], in0=stsw[:, :], in1=S32[:, :], op=ALU.mult), r=["sw", "S32"], w=["sw"])
    p.op("dve", lambda e: e.tensor_tensor(out=stage[0:32, :], in0=stage[0:32, :], in1=stsw[:, :], op=ALU.add), r=[f"{tag}st", "sw"], w=[f"{tag}st"])
    p.op("act", lambda e: e.copy(out=dest_bf, in_=stage[:, :]), r=[f"{tag}st"], w=[dkey])


def build_diff(p=None):
    own = p is None
    if own:
        p = Prog("diff")
    qT = p.dram("qT", [4, 128, S], F32, "ExternalInput")
    kT = p.dram("kT", [4, 128, S], F32, "ExternalInput")
    qsw = p.dram("qsw", [4, 32, S], F32, "ExternalInput")
    ksw = p.dram("ksw", [4, 32, S], F32, "ExternalInput")
    v = p.dram("v", [2, S, 256], F32, "ExternalInput")
    pos = p.dram("pos", [32, S], I32, "ExternalInput")
    rcd = p.dram("rc", [32, 4], F32, "ExternalInput")
    lamd = p.dram("lam", [128, 4, 128], F32, "ExternalInput")
    sublnd = p.dram("subln", [128, 256], F32, "ExternalInput")
    cmd = p.dram("cmask", [4, 128, 512], F32, "ExternalInput")
    lamcd = p.dram("lamc", [128, 2], F32, "ExternalInput")
    y = p.dram("y", [2, S, 256], F32, "ExternalOutput")

    QT = p.sb([128, 4, S], BF16, "QT")
    KT_ = p.sb([128, 4, S], BF16, "KT")
    VA = p.sb([128, 2, 32, 257], BF16, "VA")
    stage = [p.sb([128, S], F32, f"stage{i}") for i in range(2)]
    stsw = [p.sb([32, S], F32, "stsw0")] * 2
    C32 = p.sb([32, S], F32, "C32")
    S32 = p.sb([32, S], F32, "S32")
    tmp_i = p.sb([32, S // 2], I32, "tmp_i")
    rc = p.sb([32, 4], F32, "rc_sb")
    lam = p.sb([128, 4, 128], F32, "lam_sb")
    lamw = p.sb([128, 8], F32, "lamw")
    lamc = p.sb([128, 2], F32, "lamc_sb")
    subln = p.sb([128, 256], F32, "subln_sb")
    cm = p.sb([128, 4, 512], BF16, "cm")
    E = [p.sb([128, 512], BF16, f"E{i}") for i in range(3)]
    o1 = p.sb([128, 4, 256], F32, "o1")
    o2 = p.sb([128, 256], F32, "o2")
    sqj = p.sb([128, 256], F32, "sqj")
    small = p.sb([128, 8], F32, "small")
    yst = [p.sb([128, 256], F32, f"yst{i}") for i in range(2)]
    pss = [(p.ps([128, 512], F32, f"pss{i}"), f"pss{i}") for i in range(3)]
    acc = [(p.ps([128, 512], F32, f"acc{i}"), f"acc{i}") for i in range(4)]

    with p.nc.allow_low_precision("bf16 matmul operands, fp32 accumulation"):
        p.dma("sp", rc[:, :], rcd, w=["rc"])
        p.dma("act", lam[:, :, :], lamd, w=["lam"])
        p.dma("act", lamc[:, :], lamcd, w=["lamc"])
        p.dma("act", subln[:, :], sublnd, w=["subln"])
        tmp_f = stage[0][0:32, :]
        for hf in range(2):
            p.dma("sp", tmp_i[:, :], pos[:, hf * 2048:(hf + 1) * 2048], w=["tmp_i"])
            p.op("dve", lambda e, hf=hf: e.tensor_copy(out=stage[0][0:32, hf * 2048:(hf + 1) * 2048], in_=tmp_i[:, :]), r=["tmp_i"], w=["qst"])
        p.op("dve", lambda e: e.tensor_scalar(out=tmp_f, in0=tmp_f, scalar1=rc[:, 0:1], scalar2=None, op0=ALU.mult), r=["qst", "rc"], w=["qst"])
        emit_sincos(p, tmp_f, "qst", C32, S32, tmp_i, stage[1][0:32, :], "kst", rc)
        p.op("dve", lambda e: e.tensor_tensor(out=lam[:, 0, :], in0=lam[:, 0, :], in1=lam[:, 1, :], op=ALU.mult), r=["lam"], w=["lam"])
        p.op("dve", lambda e: e.tensor_tensor(out=lam[:, 2, :], in0=lam[:, 2, :], in1=lam[:, 3, :], op=ALU.mult), r=["lam"], w=["lam"])
        p.op("dve", lambda e: e.reduce_sum(out=lamw[:, 0:1], in_=lam[:, 0, :], axis=AX.X), r=["lam"], w=["lamw"])
        p.op("dve", lambda e: e.reduce_sum(out=lamw[:, 1:2], in_=lam[:, 2, :], axis=AX.X), r=["lam"], w=["lamw"])
        p.op("act", lambda e: e.activation(out=lamw[:, 2:4], in_=lamw[:, 0:2], func=AF.Exp), r=["lamw"], w=["lamw"])
        p.op("dve", lambda e: e.tensor_tensor(out=lamw[:, 4:5], in0=lamw[:, 3:4], in1=lamw[:, 2:3], op=ALU.subtract), r=["lamw"], w=["lamw"])
        p.op("dve", lambda e: e.tensor_scalar(out=lamw[:, 5:6], in0=lamw[:, 4:5], scalar1=lamc[:, 0:1], scalar2=None, op0=ALU.add), r=["lamw", "lamc"], w=["lamw"])
        p.op("dve", lambda e: e.tensor_scalar(out=subln[:, :], in0=subln[:, :], scalar1=lamc[:, 1:2], scalar2=None, op0=ALU.mult), r=["subln", "lamc"], w=["subln"])
        for j in range(4):
            p.dma("sp", stage[1][:, j * 512:(j + 1) * 512], cmd[j], w=["kst"])
        p.op("dve", lambda e: e.tensor_copy(out=cm[:, :, :].rearrange("p a b -> p (a b)"), in_=stage[1][:, 0:2048]), r=["kst"], w=["cm"])
        for i in range(4):
            emit_rope_load(p, qT[i], qsw[i], QT[:, i, :], f"QT{i}", stage[0], stsw[0], C32, S32, "q", q="sp")
            emit_rope_load(p, kT[i], ksw[i], KT_[:, i, :], f"KT{i}", stage[1], stsw[1], C32, S32, "k", q="act")
        for j in range(2):
            for hh in range(2):
                st = stage[hh]
                key = ("qst", "kst")[hh]
                p.dma(("sp", "act")[hh], st[:, :].rearrange("p (a d) -> p a d", d=256),
                      v[j, hh * 2048:(hh + 1) * 2048, :].rearrange("(a p) d -> p a d", p=128), w=[key])
                p.op(("dve", "pool")[hh], lambda e, st=st, j=j, hh=hh: e.tensor_copy(out=VA[:, j, hh * 16:(hh + 1) * 16, 0:256],
                                                                                    in_=st[:, :].rearrange("p (a d) -> p a d", d=256)), r=[key], w=[f"VA{j}"])
            p.op("pool", lambda e, j=j: e.memset(VA[:, j, :, 256:257], 1.0), w=[f"VA{j}"])

        ei = 0
        psi = 0
        yi = 0
        for j in range(2):
            for qc in range(NQC):
                for m in range(2):
                    jm = 2 * j + m
                    nkt = 4 * qc + 4
                    for kt in range(nkt):
                        ps, pk = pss[psi % 3]; psi += 1
                        Eb = E[ei % 3]; ek = f"E{ei % 3}"; ei += 1
                        p.op("pe", lambda e, ps=ps, kt=kt, jm=jm, qc=qc: e.matmul(ps[:, :], lhsT=KT_[:, jm, kt * 128:(kt + 1) * 128], rhs=QT[:, jm, qc * TC:(qc + 1) * TC], start=True, stop=True),
                             r=[f"KT{jm}", f"QT{jm}"], w=[pk])
                        p.op("act", lambda e, ps=ps, Eb=Eb: e.activation(out=Eb[:, :], in_=ps[:, :], func=AF.Exp, scale=SCALE), r=[pk], w=[ek])
                        dj = kt - 4 * qc
                        if dj >= 0:
                            p.op("pool", lambda e, Eb=Eb, dj=dj: e.tensor_tensor(out=Eb[:, :], in0=Eb[:, :], in1=cm[:, dj, :], op=ALU.mult), r=[ek, "cm"], w=[ek])
                        for qs in range(4):
                            if dj > qs:
                                continue
                            first = (kt == 0)
                            last = (kt == min(nkt - 1, 4 * qc + qs))
                            a, ak = acc[qs]
                            p.op("pe", lambda e, a=a, Eb=Eb, qs=qs, kt=kt, j=j, first=first, last=last: e.matmul(a[:, 0:257], lhsT=Eb[:, qs * 128:(qs + 1) * 128], rhs=VA[:, j, kt, :], start=first, stop=last),
                                 r=[ek, f"VA{j}"], w=[ak])
                    for qs in range(4):
                        a, ak = acc[qs]
                        p.op("dve", lambda e, a=a: e.reciprocal(out=small[:, 0:1], in_=a[:, 256:257]), r=[ak], w=["small"])
                        if m == 0:
                            p.op("dve", lambda e, a=a, qs=qs: e.tensor_scalar(out=o1[:, qs, :], in0=a[:, 0:256], scalar1=small[:, 0:1], scalar2=None, op0=ALU.mult), r=[ak, "small"], w=[f"o1_{qs}"])
                        else:
                            p.op("dve", lambda e, a=a: e.tensor_scalar(out=o2[:, :], in0=a[:, 0:256], scalar1=small[:, 0:1], scalar2=None, op0=ALU.mult), r=[ak, "small"], w=["o2"])
                            p.op("dve", lambda e, qs=qs: e.scalar_tensor_tensor(out=o2[:, :], in0=o2[:, :], scalar=lamw[:, 5:6], in1=o1[:, qs, :], op0=ALU.mult, op1=ALU.add), r=["o2", "lamw", f"o1_{qs}"], w=["o2"])
                            p.op("pool", lambda e: e.tensor_tensor(out=sqj[:, :], in0=o2[:, :], in1=o2[:, :], op=ALU.mult), r=["o2"], w=["sqj"])
                            p.op("dve", lambda e: e.reduce_sum(out=small[:, 1:2], in_=sqj[:, :], axis=AX.X), r=["sqj"], w=["small"])
                            p.op("dve", lambda e: e.tensor_scalar(out=small[:, 2:3], in0=small[:, 1:2], scalar1=1.0 / 256, scalar2=1e-5, op0=ALU.mult, op1=ALU.add), r=["small"], w=["small"])
                            p.op("act", lambda e: e.sqrt(out=small[:, 2:3], in_=small[:, 2:3]), r=["small"], w=["small"])
                            p.op("dve", lambda e: e.reciprocal(out=small[:, 3:4], in_=small[:, 2:3]), r=["small"], w=["small"])
                            ys = yst[yi % 2]; yk = f"yst{yi % 2}"; yi += 1
                            p.op("dve", lambda e, ys=ys: e.scalar_tensor_tensor(out=ys[:, :], in0=o2[:, :], scalar=small[:, 3:4], in1=subln[:, :], op0=ALU.mult, op1=ALU.mult), r=["o2", "small", "subln"], w=[yk])
                            r0 = qc * TC + qs * 128
                            p.dma("sp", y[j, r0:r0 + 128, :], ys[:, :], r=[yk], w=[f"y{j}_{qc}_{qs}"])
    print("diff op counts", p.counts())
    if not own:
        return None
    return p.build()


def diff_inputs(PTc, pos_b, lam_l, subln_l, layer=0):
    T = lambda t: PTc[t * 128:(t + 1) * 128]
    sw = lambda a: np.concatenate([a[16:32], a[0:16]], 0)
    qT = np.stack([T(10 + i) for i in range(4)])
    kT = np.stack([T(14 + i) for i in range(4)])
    v = np.stack([np.ascontiguousarray(PTc[(18 + 2 * j) * 128:(20 + 2 * j) * 128].T) for j in range(2)])
    return {
        "qT": np.ascontiguousarray(qT), "kT": np.ascontiguousarray(kT),
        "qsw": np.stack([sw(a) for a in qT]), "ksw": np.stack([sw(a) for a in kT]),
        "v": v, "pos": np.ascontiguousarray(np.broadcast_to(pos_b[None, :], (32, S))).astype(np.int32),
        "rc": rope_consts(), "lam": np.ascontiguousarray(np.broadcast_to(lam_l[None], (128, 4, 128))),
        "subln": np.ascontiguousarray(np.broadcast_to(subln_l[None], (128, 256))), "cmask": causal_masks(),
        "lamc": np.ascontiguousarray(np.broadcast_to(np.array([[-(0.8 - 0.6 * math.exp(-0.3 * layer)), 1.0 - (0.8 - 0.6 * math.exp(-0.3 * layer))]], np.float32), (128, 2))),
    }


import math

NC_ = 255
GC0 = math.sqrt(2.0 / math.pi)


def nsa_consts():
    p = np.arange(128)[:, None]; f = np.arange(512)[None, :]
    m8 = np.zeros((8, 128, 512), np.float32)
    for dj in range(-4, 4):
        if dj >= 0:
            m8[dj + 4] = (128 * dj + p <= f)
        else:
            m8[dj + 4] = (f < 512 + 128 * dj + p)
    cmpm = np.zeros((2, 8, 128, 512), np.float32)
    for ct in range(2):
        for qc in range(8):
            cmpm[ct, qc] = (16 * (128 * ct + p) + 31 <= 512 * qc + f)
    c = np.arange(256)[:, None]; j = np.arange(64)[None, :]
    ov = ((16 * c <= 64 * j + 63) & (16 * c + 31 >= 64 * j)).astype(np.float32)
    ov[255] = 0
    t = np.arange(S)[:, None]; bt = t // 64
    addm = np.zeros((S, 64), np.float32)
    addm[(j == bt - 1) & (np.ones_like(t) > 0)] = 3e9
    addm[(j == bt) & (np.ones_like(t) > 0)] = 2e9
    addm[(j == 0) & (np.ones_like(t) > 0)] = 1e9
    addm[(j > bt) & (np.ones_like(t) > 0)] = -1e30
    k = np.arange(128)[None, None, :]; kt = np.arange(32)[None, :, None]; jj = np.arange(64)[:, None, None]
    eall = ((kt * 128 + k) // 64 == jj).astype(np.float32)
    eall_p = np.zeros((128, 32, 128), np.float32); eall_p[:64] = eall
    return dict(m8=m8, cmpm=cmpm, ov=ov.reshape(2, 128, 64), addm=np.ascontiguousarray(addm.reshape(32, 128, 64).transpose(1, 0, 2)),
                eall=eall_p, ident=np.eye(128, dtype=np.float32))


def build_nsa(p=None):
    own = p is None
    if own:
        p = Prog("nsa")
    D_ = lambda n, s, dt=F32: p.dram(n, s, dt, "ExternalInput")
    qT = D_("qT", [4, 128, S]); qsw = D_("qsw", [4, 32, S])
    kcvT = D_("kcvT", [2, 128, S])
    ksT = D_("ksT", [2, 128, S]); kssw = D_("kssw", [2, 32, S])
    vsw = D_("vsw", [2, S, 128])
    ngd = D_("ng", [S, 12])
    pos = D_("pos", [32, S], I32); rcd = D_("rc", [32, 4])
    ped = D_("pe", [2, 128, 32])
    w1d = D_("w1", [2, 4096, 256]); w2d = D_("w2", [2, 256, 128]); w2swd = D_("w2sw", [256, 32])
    m8d = D_("m8", [8, 128, 512]); cmpmd = D_("cmpm", [2, 8, 128, 512]); ovd = D_("ov", [2, 128, 64])
    addmd = D_("addm", [128, 32, 64]); ealld = D_("eall", [128, 32, 128]); identd = D_("ident", [128, 128])
    y = p.dram("y", [S, 512], F32, "ExternalOutput")

    QT = p.sb([128, 4, S], BF16, "QT")
    KS = p.sb([128, 2, S], BF16, "KS")
    KC = p.sb([128, 256], BF16, "KC")
    VS = p.sb([128, 2, 32, 129], BF16, "VS")
    VC = p.sb([128, 2, 193], BF16, "VC")
    SA = p.sb([128, 8192], F32, "SA")
    SB_ = p.sb([128, 8192], F32, "SB")
    stage = [SA[:, 0:4096], SA[:, 4096:8192]]
    SAb = SA[:, :].bitcast(BF16)
    MSK = SAb.rearrange("p (a b) -> p a b", b=512)
    X = SAb[:, 8192:16384].rearrange("p (a b) -> p a b", b=256)
    C32 = SB_[0:32, 0:4096]; S32 = SB_[0:32, 4096:8192]
    YACC = SB_[:, 0:2048].rearrange("p (a b c) -> p a b c", a=4, b=4)
    ADDM = SB_[:, 2048:4096].rearrange("p (a b) -> p a b", b=64)
    M8 = SB_[:, 4096:6144].bitcast(BF16).rearrange("p (a b) -> p a b", b=512)
    EALL = SB_[:, 6144:8192].bitcast(BF16).rearrange("p (a b) -> p a b", b=128)
    stsw = p.sb([32, S], F32, "stsw")
    tmp_i = p.sb([32, S // 4], I32, "tmp_i")
    rc = p.sb([32, 4], F32, "rc_sb")
    pe = p.sb([128, 2, 32], F32, "pe_sb")
    w1st = [p.sb([128, 8, 256], F32, f"w1st{i}") for i in range(2)]
    w1bf = [p.sb([128, 8, 256], BF16, f"w1bf{i}") for i in range(2)]
    w2st = p.sb([128, 2, 2, 128], F32, "w2st")
    w2bf = p.sb([128, 2, 2, 128], BF16, "w2bf")
    w2swst = p.sb([128, 2, 32], F32, "w2swst")
    w2swbf = p.sb([128, 2, 32], BF16, "w2swbf")
    hx = p.sb([128, 256], F32, "hx")
    ht_ = p.sb([128, 256], F32, "ht")
    hT = p.sb([128, 2, 2, 256], BF16, "hT")
    krot = p.sb([128, 256], F32, "krot")
    ksw_s = p.sb([32, 256], F32, "ksw_s")
    ovst = p.sb([128, 2, 64], F32, "ovst")
    ident = p.sb([128, 128], F32, "ident_sb")
    NG = p.sb([128, 32, 12], F32, "NG")
    G = p.sb([128, 32, 12], F32, "G")
    cmst = [p.sb([128, 512], F32, f"cmst{i}") for i in range(2)]
    E = [p.sb([128, 512], BF16, f"E{i}") for i in range(3)]
    IMP = p.sb([128, 4, 64], F32, "IMP")
    IMPW = p.sb([128, 64], F32, "IMPW")
    SEL = p.sb([128, 64], F32, "SEL")
    SELT = p.sb([64, 512], BF16, "SELT")
    m8a = p.sb([128, 8], F32, "m8a")
    m8b = p.sb([128, 8], F32, "m8b")
    small = p.sb([128, 8], F32, "small")
    otmp = p.sb([128, 128], F32, "otmp")
    pss = [(p.ps([128, 512], F32, f"pss{i}"), f"pss{i}") for i in range(3)]
    acc = [(p.ps([128, 512], F32, f"acc{i}"), f"acc{i}") for i in range(4)]
    psm = (p.ps([128, 512], F32, "psm"), "psm")

    cnt = dict(e=0, ps=0, cm=0)

    def nextE():
        i = cnt["e"] % 3; cnt["e"] += 1
        return E[i], f"E{i}"

    def nextps():
        i = cnt["ps"] % 3; cnt["ps"] += 1
        return pss[i]

    with p.nc.allow_low_precision("bf16 matmul operands, fp32 accumulation"):
        p.dma("sp", rc[:, :], rcd, w=["rc"])
        p.dma("act", pe[:, :, :], ped.rearrange("a d l -> d a l"), w=["pe"])
        p.dma("act", ident[:, :], identd, w=["ident"])
        p.dma("act", NG[:, :, :], ngd.rearrange("(a q) g -> q a g", q=128), w=["NG"])
        p.op("act", lambda e: e.activation(out=G[:, :, :], in_=NG[:, :, :], func=AF.Sigmoid), r=["NG"], w=["G"])
        for hf in range(4):
            p.dma("sp", tmp_i[:, :], pos[:, hf * 1024:(hf + 1) * 1024], w=["tmp_i"])
            p.op("dve", lambda e, hf=hf: e.tensor_copy(out=stage[0][0:32, hf * 1024:(hf + 1) * 1024], in_=tmp_i[:, :]), r=["tmp_i"], w=["qst"])
        p.op("dve", lambda e: e.tensor_scalar(out=stage[0][0:32, :], in0=stage[0][0:32, :], scalar1=rc[:, 0:1], scalar2=None, op0=ALU.mult), r=["qst", "rc"], w=["qst"])
        emit_sincos(p, stage[0][0:32, :], "qst", C32, S32, tmp_i, stage[1][0:32, :], "kst", rc)
        for i in range(4):
            emit_rope_load(p, qT[i], qsw[i], QT[:, i, :], f"QT{i}", stage[i % 2], stsw, C32, S32, ("q", "k")[i % 2], q=("sp", "act")[i % 2])
        for i in range(2):
            emit_rope_load(p, ksT[i], kssw[i], KS[:, i, :], f"KS{i}", stage[i], stsw, C32, S32, ("q", "k")[i], q=("sp", "act")[i])
        for i in range(2):
            st = stage[i]; key = ("qst", "kst")[i]
            p.dma(("sp", "act")[i], st.rearrange("p (a d) -> p a d", d=128), vsw[i].rearrange("(a q) d -> q a d", q=128), w=[key])
            p.op(("dve", "pool")[i], lambda e, st=st, i=i: e.tensor_copy(out=VS[:, i, :, 0:128], in_=st.rearrange("p (a d) -> p a d", d=128)), r=[key], w=[f"VS{i}"])
            p.op("pool", lambda e, i=i: e.memset(VS[:, i, :, 128:129], 1.0), w=[f"VS{i}"])
        p.dma("sp", w2st[:, :, :, :], w2d.rearrange("a (t q) d -> q a t d", q=128), w=["w2st"])
        p.op("dve", lambda e: e.tensor_copy(out=w2bf[:, :, :, :], in_=w2st[:, :, :, :]), r=["w2st"], w=["w2bf"])
        p.dma("sp", w2swst[:, :, :], w2swd.rearrange("(t q) d -> q t d", q=128), w=["w2swst"])
        p.op("dve", lambda e: e.tensor_copy(out=w2swbf[:, :, :], in_=w2swst[:, :, :]), r=["w2swst"], w=["w2swbf"])
        p.dma("act", ovst[:, :, :], ovd.rearrange("t q j -> q t j"), w=["ovst"])
        p.op("dve", lambda e: e.tensor_copy(out=VC[:, :, 129:193], in_=ovst[:, :, :]), r=["ovst"], w=["VC"])
        p.op("dve", lambda e: e.memset(VC[:, :, 128:129], 1.0), w=["VC"])
        wi = 0
        for kv in range(2):
            p.dma("sp", stage[0], kcvT[kv], w=["qst"])
            for l in range(32):
                src = stage[0][:, l:l + 16 * 254 + 1:16]
                p.op(("dve", "pool")[l % 2], lambda e, l=l, src=src, kv=kv: e.tensor_scalar(out=X[:, l, 0:255], in0=src, scalar1=pe[:, kv, l:l + 1], scalar2=None, op0=ALU.add),
                     r=["qst", "pe"], w=["kst"])
            w1v = w1d[kv].rearrange("(l d) h -> d l h", d=128)
            for lc in range(4):
                b = wi % 2; wi += 1
                p.dma(("sp", "act")[b], w1st[b][:, :, :], w1v[:, lc * 8:(lc + 1) * 8, :], w=[f"w1st{b}"])
                p.op(("dve", "pool")[b], lambda e, b=b: e.tensor_copy(out=w1bf[b][:, :, :], in_=w1st[b][:, :, :]), r=[f"w1st{b}"], w=[f"w1bf{b}"])
                for ht in range(2):
                    a, ak = acc[ht]
                    for ll in range(8):
                        l = lc * 8 + ll
                        p.op("pe", lambda e, a=a, b=b, ll=ll, l=l, ht=ht: e.matmul(a[:, 0:255], lhsT=w1bf[b][:, ll, ht * 128:(ht + 1) * 128], rhs=X[:, l, 0:255], start=(l == 0), stop=(l == 31)),
                             r=[f"w1bf{b}", "kst"], w=[ak])
            for ht in range(2):
                a, ak = acc[ht]
                p.op("act", lambda e, a=a: e.copy(out=hx[:, 0:255], in_=a[:, 0:255]), r=[ak], w=["hx"])
                p.op("dve", lambda e: e.tensor_tensor(out=ht_[:, 0:255], in0=hx[:, 0:255], in1=hx[:, 0:255], op=ALU.mult), r=["hx"], w=["ht"])
                p.op("dve", lambda e: e.tensor_scalar(out=ht_[:, 0:255], in0=ht_[:, 0:255], scalar1=0.044715, scalar2=1.0, op0=ALU.mult, op1=ALU.add), r=["ht"], w=["ht"])
                p.op("dve", lambda e: e.tensor_tensor(out=ht_[:, 0:255], in0=ht_[:, 0:255], in1=hx[:, 0:255], op=ALU.mult), r=["ht", "hx"], w=["ht"])
                p.op("act", lambda e: e.activation(out=ht_[:, 0:255], in_=ht_[:, 0:255], func=AF.Tanh, scale=GC0), r=["ht"], w=["ht"])
                p.op("dve", lambda e: e.tensor_scalar(out=ht_[:, 0:255], in0=ht_[:, 0:255], scalar1=0.5, scalar2=0.5, op0=ALU.mult, op1=ALU.add), r=["ht"], w=["ht"])
                p.op("dve", lambda e, kv=kv, ht=ht: e.tensor_tensor(out=hT[:, kv, ht, 0:255], in0=ht_[:, 0:255], in1=hx[:, 0:255], op=ALU.mult), r=["ht", "hx"], w=[f"hT{kv}"])
            if kv == 0:
                a, ak = acc[2]
                for ht in range(2):
                    p.op("pe", lambda e, a=a, ht=ht: e.matmul(a[:, 0:255], lhsT=w2bf[:, 0, ht, :], rhs=hT[:, 0, ht, 0:255], start=(ht == 0), stop=(ht == 1)), r=["w2bf", "hT0"], w=[ak])
                a2, ak2 = acc[3]
                for ht in range(2):
                    p.op("pe", lambda e, a2=a2, ht=ht: e.matmul(a2[0:32, 0:255], lhsT=w2swbf[:, ht, :], rhs=hT[:, 0, ht, 0:255], start=(ht == 0), stop=(ht == 1)), r=["w2swbf", "hT0"], w=[ak2])
                p.op("act", lambda e, a=a: e.copy(out=krot[:, 0:255], in_=a[:, 0:255]), r=[ak], w=["krot"])
                p.op("act", lambda e, a2=a2: e.copy(out=ksw_s[:, 0:255], in_=a2[0:32, 0:255]), r=[ak2], w=["ksw_s"])
                Cs = C32[:, 31:31 + 16 * 254 + 1:16]; Ss = S32[:, 31:31 + 16 * 254 + 1:16]
                p.op("dve", lambda e: e.tensor_tensor(out=krot[0:32, 0:255], in0=krot[0:32, 0:255], in1=Cs, op=ALU.mult), r=["krot", "C32"], w=["krot"])
                p.op("dve", lambda e: e.tensor_tensor(out=ksw_s[:, 0:255], in0=ksw_s[:, 0:255], in1=Ss, op=ALU.mult), r=["ksw_s", "S32"], w=["ksw_s"])
                p.op("dve", lambda e: e.tensor_tensor(out=krot[0:32, 0:255], in0=krot[0:32, 0:255], in1=ksw_s[:, 0:255], op=ALU.add), r=["krot", "ksw_s"], w=["krot"])
                p.op("dve", lambda e: e.tensor_copy(out=KC[:, 0:255], in_=krot[:, 0:255]), r=["krot"], w=["KC"])
            else:
                for ct in range(2):
                    mc = 128 if ct == 0 else 127
                    a, ak = acc[2 + ct]
                    for ht in range(2):
                        p.op("pe", lambda e, a=a, ht=ht, ct=ct, mc=mc: e.matmul(a[0:mc, 0:128], lhsT=hT[:, 1, ht, ct * 128:ct * 128 + mc], rhs=w2bf[:, 1, ht, :], start=(ht == 0), stop=(ht == 1)),
                             r=["w2bf", "hT1"], w=[ak])
                    p.op("act", lambda e, a=a, ct=ct, mc=mc: e.copy(out=VC[0:mc, ct, 0:128], in_=a[0:mc, 0:128]), r=[ak], w=["VC"])
        mkeys = [f"m{k}" for k in range(32)]
        p.op("dve", lambda e: e.memset(small[:, :], 0.0), r=["qst", "kst", "C32", "S32"], w=mkeys + ["qst", "kst", "C32", "S32", "YACC", "ADDM", "M8", "EALL", "small"])
        p.dma("sp", ADDM, addmd, w=["ADDM"])
        for j in range(8):
            b = j % 2
            p.dma(("sp", "act")[b], cmst[b][:, :], m8d[j], w=[f"cmst{b}"])
            p.op("dve", lambda e, b=b, j=j: e.tensor_copy(out=M8[:, j, :], in_=cmst[b][:, :]), r=[f"cmst{b}"], w=["M8"])
        for j in range(8):
            b = j % 2
            p.dma(("sp", "act")[b], cmst[b][:, :].rearrange("p (a b) -> p a b", b=128), ealld[:, j * 4:(j + 1) * 4, :], w=[f"cmst{b}"])
            p.op("dve", lambda e, b=b, j=j: e.tensor_copy(out=EALL[:, j * 4:(j + 1) * 4, :], in_=cmst[b][:, :].rearrange("p (a b) -> p a b", b=128)), r=[f"cmst{b}"], w=["EALL"])

        def evac(hh, qs, qc, ncol, gi, first_branch, imp_mode=None):
            a, ak = acc[qs]
            qt = qc * 4 + qs
            p.op("dve", lambda e: e.tensor_scalar(out=small[:, 0:1], in0=a[:, 128:129], scalar1=1e-30, scalar2=None, op0=ALU.max), r=[ak], w=["small"])
            p.op("dve", lambda e: e.reciprocal(out=small[:, 1:2], in_=small[:, 0:1]), r=["small"], w=["small"])
            if imp_mode is not None:
                if imp_mode == 0:
                    p.op("dve", lambda e: e.tensor_scalar(out=IMP[:, qs, :], in0=a[:, 129:193], scalar1=small[:, 1:2], scalar2=None, op0=ALU.mult), r=[ak, "small"], w=[f"IMP{qs}"])
                else:
                    p.op("dve", lambda e: e.scalar_tensor_tensor(out=IMP[:, qs, :], in0=a[:, 129:193], scalar=small[:, 1:2], in1=IMP[:, qs, :], op0=ALU.mult, op1=ALU.add), r=[ak, "small", f"IMP{qs}"], w=[f"IMP{qs}"])
            p.op("dve", lambda e: e.tensor_tensor(out=small[:, 2:3], in0=small[:, 1:2], in1=G[:, qt, hh * 3 + gi:hh * 3 + gi + 1], op=ALU.mult), r=["small", "G"], w=["small"])
            if first_branch:
                p.op("dve", lambda e: e.tensor_scalar(out=YACC[:, qs, hh, :], in0=a[:, 0:128], scalar1=small[:, 2:3], scalar2=None, op0=ALU.mult), r=[ak, "small"], w=[f"Y{qs}"])
            else:
                p.op("dve", lambda e: e.scalar_tensor_tensor(out=YACC[:, qs, hh, :], in0=a[:, 0:128], scalar=small[:, 2:3], in1=YACC[:, qs, hh, :], op0=ALU.mult, op1=ALU.add), r=[ak, "small", f"Y{qs}"], w=[f"Y{qs}"])

        for qc in range(NQC):
            qsl = slice(qc * TC, (qc + 1) * TC)
            cts = [0] if qc <= 3 else [0, 1]
            cmk = {}
            for ct in cts:
                if ct == 0 and qc >= 5:
                    continue
                b = cnt["cm"] % 2; cnt["cm"] += 1
                p.dma("sp", cmst[b][:, :], cmpmd[ct, qc], w=[f"cmst{b}"])
                cmk[ct] = b
            for hh in range(4):
                for ct in cts:
                    mc = 128 if ct == 0 else 127
                    ps, pk = nextps(); Eb, ek = nextE()
                    p.op("pe", lambda e, ps=ps, ct=ct, mc=mc, hh=hh, qsl=qsl: e.matmul(ps[0:mc, :], lhsT=KC[:, ct * 128:ct * 128 + mc], rhs=QT[:, hh, qsl], start=True, stop=True), r=["KC", f"QT{hh}"], w=[pk])
                    p.op("act", lambda e, ps=ps, Eb=Eb, mc=mc: e.activation(out=Eb[0:mc, :], in_=ps[0:mc, :], func=AF.Exp, scale=SCALE), r=[pk], w=[ek])
                    if ct in cmk:
                        b = cmk[ct]
                        p.op("pool", lambda e, Eb=Eb, mc=mc, b=b: e.tensor_tensor(out=Eb[0:mc, :], in0=Eb[0:mc, :], in1=cmst[b][0:mc, :], op=ALU.mult), r=[ek, f"cmst{b}"], w=[ek])
                    for qs in range(4):
                        a, ak = acc[qs]
                        p.op("pe", lambda e, a=a, Eb=Eb, qs=qs, ct=ct, mc=mc, lastct=cts[-1]: e.matmul(a[:, 0:193], lhsT=Eb[0:mc, qs * 128:(qs + 1) * 128], rhs=VC[0:mc, ct, :], start=(ct == 0), stop=(ct == lastct)),
                             r=[ek, "VC"], w=[ak])
                for qs in range(4):
                    evac(hh, qs, qc, 193, 0, True, imp_mode=(0 if hh == 0 else 1))
            for qs in range(4):
                qt = qc * 4 + qs
                p.op("dve", lambda e, qs=qs, qt=qt: e.tensor_tensor(out=IMP[:, qs, :], in0=IMP[:, qs, :], in1=ADDM[:, qt, :], op=ALU.add), r=[f"IMP{qs}", "ADDM"], w=[f"IMP{qs}"])
                p.op("dve", lambda e, qs=qs: e.max(out=m8a[:, :], in_=IMP[:, qs, :]), r=[f"IMP{qs}"], w=["m8a"])
                p.op("dve", lambda e, qs=qs: e.match_replace(out=IMPW[:, :], in_to_replace=m8a[:, :], in_values=IMP[:, qs, :], imm_value=-3e38), r=[f"IMP{qs}", "m8a"], w=["IMPW"])
                p.op("dve", lambda e: e.max(out=m8b[:, :], in_=IMPW[:, :]), r=["IMPW"], w=["m8b"])
                p.op("dve", lambda e, qs=qs: e.tensor_scalar(out=SEL[:, :], in0=IMP[:, qs, :], scalar1=m8b[:, 7:8], scalar2=None, op0=ALU.is_ge), r=[f"IMP{qs}", "m8b"], w=["SEL"])
                pm, pmk = psm
                p.op("pe", lambda e: e.transpose(pm[0:64, 0:128], SEL[:, :], ident[:, :]), r=["SEL", "ident"], w=[pmk])
                p.op("act", lambda e, qs=qs: e.copy(out=SELT[:, qs * 128:(qs + 1) * 128], in_=pm[0:64, 0:128]), r=[pmk], w=["SELT"])
            nkt = 4 * qc + 4
            for kt in range(nkt):
                pm, pmk = psm
                p.op("pe", lambda e, kt=kt: e.matmul(pm[:, :], lhsT=EALL[0:64, kt, :], rhs=SELT[:, :], start=True, stop=True), r=["EALL", "SELT"], w=[pmk])
                dj = kt - 4 * qc
                if dj >= 0:
                    p.op("dve", lambda e, kt=kt, dj=dj: e.tensor_tensor(out=MSK[:, kt, :], in0=pm[:, :], in1=M8[:, dj + 4, :], op=ALU.mult), r=[pmk, "M8"], w=[f"m{kt}"])
                else:
                    p.op("act", lambda e, kt=kt: e.copy(out=MSK[:, kt, :], in_=pm[:, :]), r=[pmk], w=[f"m{kt}"])
            for br in (1, 2):
                for hh in range(4):
                    kt0 = 0 if br == 1 else max(0, 4 * qc - 4)
                    for kt in range(kt0, nkt):
                        dj = kt - 4 * qc
                        ps, pk = nextps(); Eb, ek = nextE()
                        p.op("pe", lambda e, ps=ps, kt=kt, hh=hh, br=br, qsl=qsl: e.matmul(ps[:, :], lhsT=KS[:, br - 1, kt * 128:(kt + 1) * 128], rhs=QT[:, hh, qsl], start=True, stop=True), r=[f"KS{br - 1}", f"QT{hh}"], w=[pk])
                        p.op("act", lambda e, ps=ps, Eb=Eb: e.activation(out=Eb[:, :], in_=ps[:, :], func=AF.Exp, scale=SCALE), r=[pk], w=[ek])
                        meng = ("pool", "dve")[kt % 2]
                        if br == 1:
                            p.op(meng, lambda e, Eb=Eb, kt=kt: e.tensor_tensor(out=Eb[:, :], in0=Eb[:, :], in1=MSK[:, kt, :], op=ALU.mult), r=[ek, f"m{kt}"], w=[ek])
                        else:
                            p.op(meng, lambda e, Eb=Eb, dj=dj: e.tensor_tensor(out=Eb[:, :], in0=Eb[:, :], in1=M8[:, dj + 4, :], op=ALU.mult), r=[ek, "M8"], w=[ek])
                        for qs in range(4):
                            if dj > qs:
                                continue
                            if br == 2 and qs > 4 + dj:
                                continue
                            kfirst = 0 if br == 1 else max(0, 4 * qc + qs - 4)
                            klast = 4 * qc + qs
                            a, ak = acc[qs]
                            p.op("pe", lambda e, a=a, Eb=Eb, qs=qs, kt=kt, br=br, kfirst=kfirst, klast=klast: e.matmul(a[:, 0:129], lhsT=Eb[:, qs * 128:(qs + 1) * 128], rhs=VS[:, br - 1, kt, :], start=(kt == kfirst), stop=(kt == klast)),
                                 r=[ek, f"VS{br - 1}"], w=[ak])
                    for qs in range(4):
                        evac(hh, qs, qc, 129, br, False)
            for qs in range(4):
                r0 = qc * TC + qs * 128
                p.dma("sp", y[r0:r0 + 128, :], YACC[:, qs, :, :].rearrange("p a b -> p (a b)"), r=[f"Y{qs}"], w=[f"y{qc}_{qs}"])
    print("nsa op counts", p.counts())
    if not own:
        return None
    return p.build()


def nsa_inputs(PTc, pos_b, pe_l, w1_l, w2_l, consts):
    T = lambda t: PTc[t * 128:(t + 1) * 128]
    sw = lambda a: np.concatenate([a[16:32], a[0:16]], 0)
    qT = np.stack([T(i) for i in range(4)])
    ks2 = np.stack([T(6), T(8)])
    ng = PTc[36 * 128 + 32:36 * 128 + 44]
    d = {
        "qT": np.ascontiguousarray(qT), "qsw": np.stack([sw(a) for a in qT]),
        "kcvT": np.ascontiguousarray(np.stack([T(4), T(5)])),
        "ksT": np.ascontiguousarray(ks2), "kssw": np.stack([sw(a) for a in ks2]),
        "vsw": np.ascontiguousarray(np.stack([T(7).T, T(9).T])),
        "ng": np.ascontiguousarray(ng.T),
        "pos": np.ascontiguousarray(np.broadcast_to(pos_b[None, :], (32, S))).astype(np.int32), "rc": rope_consts(),
        "pe": np.ascontiguousarray(pe_l.transpose(0, 2, 1)),
        "w1": np.ascontiguousarray(w1_l), "w2": np.ascontiguousarray(w2_l),
        "w2sw": np.ascontiguousarray(np.concatenate([w2_l[0][:, 16:32], w2_l[0][:, 0:16]], 1)),
    }
    d.update(consts)
    return d


import math

S = 4096
C = 64
NB = 128
NBLK = S // NB
H = 8
CDEC = math.exp(-0.5)
LNX_EPS = 64e-5


def rwkv_consts():
    s = np.arange(128)[:, None]; t = np.arange(128)[None, :]
    same = (s // 64 == t // 64)
    tri_i = (same & (s <= t)).astype(np.float32)
    tri_e = (same & (s < t)).astype(np.float32)
    s6 = np.arange(64)[:, None]; t6 = np.arange(64)[None, :]
    mu2 = np.zeros((64, 8, 2, 64), np.float32)
    mu2[:, :, 0, :] = (s6 < t6)[:, None, :]
    mu2[:, :, 1, :] = (s6 <= t6)[:, None, :]
    ml = np.zeros((64, 8, 64), np.float32)
    ml[:] = (s6 > t6)[:, None, :]
    id8 = np.zeros((64, 8, 64), np.float32)
    id8[:] = np.eye(64, dtype=np.float32)[:, None, :]
    return dict(tri_i=tri_i, tri_e=tri_e, mu2=mu2, ml=ml, id8=id8, ident=np.eye(64, dtype=np.float32), ones64=np.ones((64, 64), np.float32))


def build_rwkv(p=None):
    own = p is None
    if own:
        p = Prog("rwkv")
    D_ = lambda n, s, dt=F32: p.dram(n, s, dt, "ExternalInput")
    rkv_d = D_("rkv", [3, 64, H, S + 1]); wad_d = D_("wad", [128, S + 1]); gd_d = D_("gd", [160, S + 1])
    mu_d = D_("mu", [64, 3, H]); muw_d = D_("muw", [128, 3])
    wa_up_d = D_("wa_up", [128, 512]); g_up_d = D_("g_up", [160, 512]); w0b_d = D_("w0b", [128, 512])
    par_d = D_("par", [64, 4, H])
    lnwb_d = D_("lnwb", [2, 64, 512])
    tri_i_d = D_("tri_i", [128, 128]); tri_e_d = D_("tri_e", [128, 128])
    mu2_d = D_("mu2", [64, 8, 2, 64]); ml_d = D_("ml", [64, 8, 64]); id8_d = D_("id8", [64, 8, 64])
    ident_d = D_("ident", [64, 64]); ones_d = D_("ones64", [64, 64])
    y = p.dram("y", [S, 512], F32, "ExternalOutput")

    sb = p.sb
    RAW = sb([64, 3, H, NB + 1], F32, "RAW"); XS = sb([64, 3, H, NB], F32, "XS")
    T1 = sb([64, H, NB], F32, "T1")
    WAD = sb([128, NB + 1], F32, "WAD"); WS = sb([128, NB], F32, "WS")
    GD = sb([128, NB + 1], F32, "GD"); GD2 = sb([32, NB + 1], F32, "GD2"); GS = sb([128, NB], F32, "GS"); GS2 = sb([32, NB], F32, "GS2")
    TW = sb([128, NB], BF16, "TW"); SG = sb([128, NB], BF16, "SG"); SG2 = sb([32, NB], BF16, "SG2")
    LW = sb([128, 512], F32, "LW")
    G = sb([64, H, NB], F32, "G"); GI = sb([64, H, NB], F32, "GI"); GM = sb([64, H, NB], F32, "GM")
    A = sb([64, H, NB], F32, "A"); KK = sb([64, H, NB], F32, "KK"); KP = sb([64, H, NB], F32, "KP"); BV = sb([64, H, NB], F32, "BV")
    RN = sb([64, H, NB], F32, "RN")
    QR = sb([64, H, 2, 2, 64], F32, "QR"); KTL = sb([64, H, NB], F32, "KTL"); BTL = sb([64, H, NB], F32, "BTL"); RK = sb([64, H, NB], F32, "RK")
    BTT = sb([64, 2, H, 64], F32, "BTT"); KTT = sb([64, 2, H, 64], F32, "KTT"); VT = sb([64, 2, H, 64], F32, "VT")
    GB = sb([64, 2, H, 2, 64], F32, "GB"); GK = sb([64, 2, H, 2, 64], F32, "GK")
    NL = [[sb([64, H, 64], F32, f"inv{n}{i}") for i in range(2)] for n in "NLAB"]
    TMT = sb([64, 2, H, 64], F32, "TMT")
    M0 = sb([64, H, 64], F32, "M0"); XSB = sb([64, H, 64], F32, "XSB"); US = sb([64, H, 64], F32, "US")
    YS = sb([64, H, 64], F32, "YS"); YT = sb([64, H, 64], F32, "YT"); OUT = [sb([64, 512], F32, f"OUT{i}") for i in range(2)]
    ST = sb([64, 4, H], F32, "ST")
    MU = sb([64, 3, H], F32, "MU"); MUW = sb([128, 3], F32, "MUW"); PAR = sb([64, 5, H], F32, "PAR")
    WAUPs = sb([128, 512], F32, "WAUPs"); WAUP = sb([128, 512], BF16, "WAUP")
    GUPs = sb([128, 512], F32, "GUPs"); GUP = sb([128, 512], BF16, "GUP"); GUP2s = sb([32, 512], F32, "GUP2s"); GUP2 = sb([32, 512], BF16, "GUP2")
    W0B = sb([128, 512], F32, "W0B"); LNW = sb([64, 512], F32, "LNW"); LNB = sb([64, 512], F32, "LNB")
    TRI_I = sb([128, 128], F32, "TRI_I"); TRI_E = sb([128, 128], F32, "TRI_E")
    MU2 = sb([64, 8, 2, 64], F32, "MU2"); ML = sb([64, 8, 64], F32, "ML"); ID8 = sb([64, 8, 64], F32, "ID8")
    IDENT = sb([64, 64], F32, "IDENT"); ONES = sb([64, 64], F32, "ONES")
    banks = [(p.ps([128, 512], F32, f"bk{i}"), f"bk{i}") for i in range(8)]
    bi = [0]

    def bank():
        b = banks[bi[0] % 8]; bi[0] += 1
        return b

    bc = lambda ap2, n: ap2.unsqueeze(2).to_broadcast([64, ap2.shape[1], n])
    f3 = lambda t: t[:, :, :]
    TT = lambda eng, out, a, b, op, r, w: p.op(eng, lambda e: e.tensor_tensor(out=out, in0=a, in1=b, op=op), r=r, w=w)

    with p.nc.allow_low_precision("bf16 lora matmuls; fp32 state math"):
        ld = [("sp", MU[:, :, :], mu_d, "MU"), ("act", MUW[:, :], muw_d, "MUW"), ("sp", PAR[:, 0:4, :], par_d, "PAR"),
              ("act", WAUPs[:, :], wa_up_d, "WAUPs"), ("sp", GUPs[:, :], g_up_d[0:128, :], "GUPs"), ("act", GUP2s[:, :], g_up_d[128:160, :], "GUP2s"),
              ("sp", W0B[:, :], w0b_d, "W0B"), ("act", LNW[:, :], lnwb_d[0], "LNW"), ("sp", LNB[:, :], lnwb_d[1], "LNB"),
              ("act", TRI_I[:, :], tri_i_d, "TRI_I"), ("sp", TRI_E[:, :], tri_e_d, "TRI_E"), ("act", MU2[:, :, :, :], mu2_d, "MU2"),
              ("sp", ML[:, :, :], ml_d, "ML"), ("act", ID8[:, :, :], id8_d, "ID8"), ("sp", IDENT[:, :], ident_d, "IDENT"), ("act", ONES[:, :], ones_d, "ONES")]
        for q, o, i, k in ld:
            p.dma(q, o, i, w=[k])
        p.op("dve", lambda e: e.tensor_copy(out=WAUP[:, :], in_=WAUPs[:, :]), r=["WAUPs"], w=["WAUP"])
        p.op("dve", lambda e: e.tensor_copy(out=GUP[:, :], in_=GUPs[:, :]), r=["GUPs"], w=["GUP"])
        p.op("dve", lambda e: e.tensor_copy(out=GUP2[:, :], in_=GUP2s[:, :]), r=["GUP2s"], w=["GUP2"])
        p.op("dve", lambda e: e.tensor_scalar(out=PAR[:, 4, :], in0=PAR[:, 2, :], scalar1=-1.0, scalar2=1.0, op0=ALU.mult, op1=ALU.add), r=["PAR"], w=["PAR"])
        p.op("dve", lambda e: e.memset(M0[:, :, :], 0.0), w=["M0"])

        for blk in range(NBLK):
            t0 = blk * NB
            for i in range(3):
                p.dma(("sp", "act", "sp")[i], RAW[:, i, :, :], rkv_d[i, :, :, t0:t0 + NB + 1], w=[f"RAW{i}"])
            p.dma("act", WAD[:, :], wad_d[:, t0:t0 + NB + 1], w=["WAD"])
            p.dma("sp", GD[:, :], gd_d[0:128, t0:t0 + NB + 1], w=["GD"])
            p.dma("act", GD2[:, :], gd_d[128:160, t0:t0 + NB + 1], w=["GD2"])
            for i in range(3):
                eng = ("dve", "pool", "dve")[i]
                TT(eng, f3(T1), RAW[:, i, :, 0:NB], RAW[:, i, :, 1:NB + 1], ALU.subtract, [f"RAW{i}"], ["T1"])
                TT(eng, f3(T1), f3(T1), bc(MU[:, i, :], NB), ALU.mult, ["T1", "MU"], ["T1"])
                TT(eng, XS[:, i, :, :], f3(T1), RAW[:, i, :, 1:NB + 1], ALU.add, ["T1", f"RAW{i}"], [f"XS{i}"])
            for (raw, dst, col, n, k) in ((WAD, WS, 0, 128, "W"), (GD, GS, 1, 128, "G1"), (GD2, GS2, 2, 32, "G2")):
                TT("pool", dst[0:n, :], raw[0:n, 0:NB], raw[0:n, 1:NB + 1], ALU.subtract, [k + "raw" if False else {"W": "WAD", "G1": "GD", "G2": "GD2"}[k]], [k + "s"])
                p.op("dve", lambda e, raw=raw, dst=dst, col=col, n=n: e.scalar_tensor_tensor(out=dst[0:n, :], in0=dst[0:n, :], scalar=MUW[0:n, col:col + 1], in1=raw[0:n, 1:NB + 1], op0=ALU.mult, op1=ALU.add),
                     r=[k + "s", "MUW", {"W": "WAD", "G1": "GD", "G2": "GD2"}[k]], w=[k + "s"])
            p.op("act", lambda e: e.activation(out=TW[0:64, :], in_=WS[0:64, :], func=AF.Tanh), r=["Ws"], w=["TW"])
            p.op("act", lambda e: e.copy(out=TW[64:128, :], in_=WS[64:128, :]), r=["Ws"], w=["TW"])
            p.op("act", lambda e: e.activation(out=SG[:, :], in_=GS[:, :], func=AF.Sigmoid), r=["G1s"], w=["SG"])
            p.op("act", lambda e: e.activation(out=SG2[:, :], in_=GS2[:, :], func=AF.Sigmoid), r=["G2s"], w=["SG2"])
            bk, bkk = bank()
            p.op("pe", lambda e, bk=bk: e.matmul(bk[:, :], lhsT=TW[0:64, :], rhs=WAUP[0:64, :], start=True, stop=True), r=["TW", "WAUP"], w=[bkk])
            TT("dve", LW[:, :], bk[:, :], W0B[:, :], ALU.add, [bkk, "W0B"], ["LW"])
            p.op("act", lambda e: e.activation(out=LW[:, :], in_=LW[:, :], func=AF.Sigmoid), r=["LW"], w=["LW"])
            for hq in range(2):
                bI, bIk = bank(); bE, bEk = bank()
                for h4 in range(4):
                    h = 4 * hq + h4
                    p.op("pe", lambda e, bI=bI, h=h, h4=h4: e.matmul(bI[0:64, h4 * 128:(h4 + 1) * 128], lhsT=LW[:, h * 64:(h + 1) * 64], rhs=TRI_I[:, :], start=True, stop=True), r=["LW", "TRI_I"], w=[bIk])
                    p.op("pe", lambda e, bE=bE, h=h, h4=h4: e.matmul(bE[0:64, h4 * 128:(h4 + 1) * 128], lhsT=LW[:, h * 64:(h + 1) * 64], rhs=TRI_E[:, :], start=True, stop=True), r=["LW", "TRI_E"], w=[bEk])
                hs = slice(4 * hq, 4 * hq + 4)
                v4 = lambda t: t[:, hs, :].rearrange("p a b -> p (a b)")
                p.op("act", lambda e, bI=bI, hs=hs: e.activation(out=G[:, hs, :].rearrange("p a b -> p (a b)"), in_=bI[0:64, :], func=AF.Exp, scale=-CDEC), r=[bIk], w=["G"])
                p.op("act", lambda e, bI=bI, hs=hs: e.activation(out=GI[:, hs, :].rearrange("p a b -> p (a b)"), in_=bI[0:64, :], func=AF.Exp, scale=CDEC), r=[bIk], w=["GI"])
                p.op("act", lambda e, bE=bE, hs=hs: e.activation(out=GM[:, hs, :].rearrange("p a b -> p (a b)"), in_=bE[0:64, :], func=AF.Exp, scale=-CDEC), r=[bEk], w=["GM"])
            for hq in range(2):
                bk, bkk = bank()
                for h4 in range(4):
                    h = 4 * hq + h4
                    p.op("pe", lambda e, bk=bk, h=h, h4=h4: e.matmul(bk[0:64, h4 * 128:(h4 + 1) * 128], lhsT=WAUP[64:128, h * 64:(h + 1) * 64], rhs=TW[64:128, :], start=True, stop=True), r=["WAUP", "TW"], w=[bkk])
                hs = slice(4 * hq, 4 * hq + 4)
                p.op("dve", lambda e, bk=bk, hs=hs: e.tensor_tensor(out=A[:, hs, :], in0=bk[0:64, :].rearrange("p (a b) -> p a b", b=128), in1=bc(PAR[:, 0, hs], NB), op=ALU.add), r=[bkk, "PAR"], w=["A"])
            p.op("act", lambda e: e.activation(out=f3(A), in_=f3(A), func=AF.Sigmoid), r=["A"], w=["A"])
            TT("dve", f3(KK), XS[:, 1, :, :], bc(PAR[:, 1, :], NB), ALU.mult, ["XS1", "PAR"], ["KK"])
            TT("pool", f3(T1), f3(KK), f3(KK), ALU.mult, ["KK"], ["T1"])
            for hq in range(2):
                bk, bkk = bank()
                for h4 in range(4):
                    h = 4 * hq + h4
                    p.op("pe", lambda e, bk=bk, h=h, h4=h4: e.matmul(bk[0:64, h4 * 128:(h4 + 1) * 128], lhsT=ONES[:, :], rhs=T1[:, h, :], start=True, stop=True), r=["ONES", "T1"], w=[bkk])
                hs = slice(4 * hq, 4 * hq + 4)
                p.op("act", lambda e, bk=bk, hs=hs: e.sqrt(out=RN[:, hs, :].rearrange("p a b -> p (a b)"), in_=bk[0:64, :]), r=[bkk], w=["RN"])
            p.op("dve", lambda e: e.tensor_scalar(out=f3(RN), in0=f3(RN), scalar1=1e-12, scalar2=None, op0=ALU.max), r=["RN"], w=["RN"])
            p.op("dve", lambda e: e.reciprocal(out=f3(RN), in_=f3(RN)), r=["RN"], w=["RN"])
            TT("dve", f3(KK), f3(KK), f3(RN), ALU.mult, ["KK", "RN"], ["KK"])
            TT("pool", f3(T1), f3(A), bc(PAR[:, 2, :], NB), ALU.mult, ["A", "PAR"], ["T1"])
            TT("pool", f3(T1), f3(T1), bc(PAR[:, 4, :], NB), ALU.add, ["T1", "PAR"], ["T1"])
            TT("pool", f3(KP), XS[:, 1, :, :], f3(T1), ALU.mult, ["XS1", "T1"], ["KP"])
            TT("dve", f3(BV), f3(KK), f3(A), ALU.mult, ["KK", "A"], ["BV"])
            c4 = lambda t: t[:, :, :].rearrange("p h (c t) -> p h c t", c=2)
            TT("dve", QR[:, :, :, 0, :], c4(KK), c4(GM), ALU.mult, ["KK", "GM"], ["QR"])
            TT("pool", QR[:, :, :, 1, :], XS[:, 0, :, :].rearrange("p h (c t) -> p h c t", c=2), c4(G), ALU.mult, ["XS0", "G"], ["QR"])
            TT("dve", f3(KTL), f3(KP), f3(GI), ALU.mult, ["KP", "GI"], ["KTL"])
            TT("pool", f3(BTL), f3(BV), f3(GI), ALU.mult, ["BV", "GI"], ["BTL"])
            TT("dve", f3(RK), XS[:, 0, :, :], f3(KP), ALU.mult, ["XS0", "KP"], ["RK"])
            for c in range(2):
                cs = slice(c * 64, (c + 1) * 64)
                for src, skey, dst, dkey in ((BTL, "BTL", BTT, "BTT"), (KTL, "KTL", KTT, "KTT"), (XS[:, 2, :, :], "XS2", VT, "VT")):
                    bk, bkk = bank()
                    for h in range(H):
                        p.op("pe", lambda e, bk=bk, h=h, src=src, cs=cs: e.transpose(bk[0:64, h * 64:(h + 1) * 64], src[:, h, cs], IDENT[:, :]), r=[skey, "IDENT"], w=[bkk])
                    p.op("act", lambda e, bk=bk, dst=dst, c=c: e.copy(out=dst[:, c, :, :].rearrange("p a b -> p (a b)"), in_=bk[0:64, :]), r=[bkk], w=[f"{dkey}{c}"])
                for src, skey, dst, dkey in ((BTL, "BTL", GB, "GB"), (KTL, "KTL", GK, "GK")):
                    for hq in range(2):
                        bk, bkk = bank()
                        for h4 in range(4):
                            h = 4 * hq + h4
                            p.op("pe", lambda e, bk=bk, h=h, h4=h4, src=src, cs=cs, c=c: e.matmul(bk[0:64, h4 * 128:(h4 + 1) * 128], lhsT=src[:, h, cs], rhs=QR[:, h, c, :, :].rearrange("p a b -> p (a b)"), start=True, stop=True),
                                 r=[skey, "QR"], w=[bkk])
                        hs = slice(4 * hq, 4 * hq + 4)
                        p.op("dve", lambda e, bk=bk, dst=dst, c=c, hs=hs: e.tensor_tensor(out=dst[:, c, hs, :, :].rearrange("p a b d -> p (a b d)"), in0=bk[0:64, :], in1=MU2[:, 0:4, :, :].rearrange("p a b d -> p (a b d)"), op=ALU.mult),
                             r=[bkk, "MU2"], w=[f"{dkey}{c}"])
                bk, bkk = bank()
                for h in range(H):
                    p.op("pe", lambda e, bk=bk, h=h, c=c, cs=cs: e.matmul(bk[0:64, h * 64:(h + 1) * 64], lhsT=QR[:, h, c, 0, :], rhs=BTL[:, h, cs], start=True, stop=True), r=["QR", "BTL"], w=[bkk])
                Ncur, Lcur, Acur, Bcur = NL[0][0], NL[1][0], NL[2][0], NL[3][0]
                nk = lambda n, i: f"inv{n}{i}"
                p.op("dve", lambda e, bk=bk, Lcur=Lcur: e.tensor_tensor(out=Lcur[:, :, :].rearrange("p a b -> p (a b)"), in0=bk[0:64, :], in1=ML[:, :, :].rearrange("p a b -> p (a b)"), op=ALU.mult), r=[bkk, "ML"], w=[nk("L", 0)])
                p.op("act", lambda e, Ncur=Ncur, c=c: e.copy(out=Ncur[:, :, :], in_=GB[:, c, :, 0, :]), r=[f"GB{c}"], w=[nk("N", 0)])
                TT("pool", f3(Acur), f3(ID8), f3(Ncur), ALU.subtract, ["ID8", nk("N", 0)], [nk("A", 0)])
                TT("pool", f3(Bcur), f3(ID8), f3(Lcur), ALU.subtract, ["ID8", nk("L", 0)], [nk("B", 0)])
                cur = 0
                for j in range(1, 6):
                    nxt = 1 - cur
                    Nc, Lc, Ac, Bc = (NL[n][cur] for n in range(4))
                    Nn, Ln, An, Bn = (NL[n][nxt] for n in range(4))
                    bN, bNk = bank()
                    for h in range(H):
                        p.op("pe", lambda e, bN=bN, h=h, Lc=Lc, Nc=Nc: e.matmul(bN[0:64, h * 64:(h + 1) * 64], lhsT=Lc[:, h, :], rhs=Nc[:, h, :], start=True, stop=True), r=[nk("L", cur), nk("N", cur)], w=[bNk])
                    p.op("act", lambda e, bN=bN, Nn=Nn: e.copy(out=Nn[:, :, :].rearrange("p a b -> p (a b)"), in_=bN[0:64, :]), r=[bNk], w=[nk("N", nxt)])
                    if j <= 4:
                        bL, bLk = bank()
                        for h in range(H):
                            p.op("pe", lambda e, bL=bL, h=h, Lc=Lc, Nc=Nc: e.matmul(bL[0:64, h * 64:(h + 1) * 64], lhsT=Nc[:, h, :], rhs=Lc[:, h, :], start=True, stop=True), r=[nk("L", cur), nk("N", cur)], w=[bLk])
                        p.op("act", lambda e, bL=bL, Ln=Ln: e.copy(out=Ln[:, :, :].rearrange("p a b -> p (a b)"), in_=bL[0:64, :]), r=[bLk], w=[nk("L", nxt)])
                    bA, bAk = bank()
                    for h in range(H):
                        p.op("pe", lambda e, bA=bA, h=h, Bc=Bc, Nn=Nn: e.matmul(bA[0:64, h * 64:(h + 1) * 64], lhsT=Bc[:, h, :], rhs=Nn[:, h, :], start=True, stop=True), r=[nk("B", cur), nk("N", nxt)], w=[bAk])
                    dstA = An[:, :, :] if j < 5 else TMT[:, c, :, :]
                    p.op("dve", lambda e, bA=bA, dstA=dstA, Ac=Ac: e.tensor_tensor(out=dstA.rearrange("p a b -> p (a b)"), in0=bA[0:64, :], in1=Ac[:, :, :].rearrange("p a b -> p (a b)"), op=ALU.add),
                         r=[bAk, nk("A", cur)], w=[nk("A", nxt) if j < 5 else f"TMT{c}"])
                    if j <= 4:
                        bB, bBk = bank()
                        for h in range(H):
                            p.op("pe", lambda e, bB=bB, h=h, Ac=Ac, Ln=Ln: e.matmul(bB[0:64, h * 64:(h + 1) * 64], lhsT=Ac[:, h, :], rhs=Ln[:, h, :], start=True, stop=True), r=[nk("A", cur), nk("L", nxt)], w=[bBk])
                        p.op("dve", lambda e, bB=bB, Bn=Bn, Bc=Bc: e.tensor_tensor(out=Bn[:, :, :].rearrange("p a b -> p (a b)"), in0=bB[0:64, :], in1=Bc[:, :, :].rearrange("p a b -> p (a b)"), op=ALU.add),
                             r=[bBk, nk("B", cur)], w=[nk("B", nxt)])
                    cur = nxt
            for c in range(2):
                cs = slice(c * 64, (c + 1) * 64)
                bX, bXk = bank()
                for h in range(H):
                    p.op("pe", lambda e, bX=bX, h=h, c=c: e.matmul(bX[0:64, h * 64:(h + 1) * 64], lhsT=QR[:, h, c, 0, :], rhs=M0[:, h, :], start=True, stop=False), r=["QR", "M0"], w=[bXk])
                    p.op("pe", lambda e, bX=bX, h=h, c=c: e.matmul(bX[0:64, h * 64:(h + 1) * 64], lhsT=GK[:, c, h, 0, :], rhs=VT[:, c, h, :], start=False, stop=True), r=[f"GK{c}", f"VT{c}"], w=[bXk])
                p.op("act", lambda e, bX=bX: e.copy(out=XSB[:, :, :].rearrange("p a b -> p (a b)"), in_=bX[0:64, :]), r=[bXk], w=["XSB"])
                bU, bUk = bank()
                for h in range(H):
                    p.op("pe", lambda e, bU=bU, h=h, c=c: e.matmul(bU[0:64, h * 64:(h + 1) * 64], lhsT=TMT[:, c, h, :], rhs=XSB[:, h, :], start=True, stop=True), r=[f"TMT{c}", "XSB"], w=[bUk])
                p.op("act", lambda e, bU=bU: e.mul(out=US[:, :, :].rearrange("p a b -> p (a b)"), in_=bU[0:64, :], mul=-1.0), r=[bUk], w=["US"])
                bY, bYk = bank()
                for h in range(H):
                    p.op("pe", lambda e, bY=bY, h=h, c=c: e.matmul(bY[0:64, h * 64:(h + 1) * 64], lhsT=QR[:, h, c, 1, :], rhs=M0[:, h, :], start=True, stop=False), r=["QR", "M0"], w=[bYk])
                    p.op("pe", lambda e, bY=bY, h=h, c=c: e.matmul(bY[0:64, h * 64:(h + 1) * 64], lhsT=GB[:, c, h, 1, :], rhs=US[:, h, :], start=False, stop=False), r=[f"GB{c}", "US"], w=[bYk])
                    p.op("pe", lambda e, bY=bY, h=h, c=c: e.matmul(bY[0:64, h * 64:(h + 1) * 64], lhsT=GK[:, c, h, 1, :], rhs=VT[:, c, h, :], start=False, stop=True), r=[f"GK{c}", f"VT{c}"], w=[bYk])
                bM, bMk = bank()
                for h in range(H):
                    p.op("pe", lambda e, bM=bM, h=h, c=c: e.matmul(bM[0:64, h * 64:(h + 1) * 64], lhsT=BTT[:, c, h, :], rhs=US[:, h, :], start=True, stop=False), r=[f"BTT{c}", "US"], w=[bMk])
                    p.op("pe", lambda e, bM=bM, h=h, c=c: e.matmul(bM[0:64, h * 64:(h + 1) * 64], lhsT=KTT[:, c, h, :], rhs=VT[:, c, h, :], start=False, stop=True), r=[f"KTT{c}", f"VT{c}"], w=[bMk])
                p.op("dve", lambda e, bM=bM: e.tensor_tensor(out=M0[:, :, :].rearrange("p a b -> p (a b)"), in0=bM[0:64, :], in1=M0[:, :, :].rearrange("p a b -> p (a b)"), op=ALU.add), r=[bMk, "M0"], w=["M0"])
                gc = G[:, :, c * 64 + 63:c * 64 + 64].to_broadcast([64, H, 64])
                TT("dve", f3(M0), f3(M0), gc, ALU.mult, ["M0", "G"], ["M0"])
                bR, bRk = bank()
                for h in range(H):
                    p.op("pe", lambda e, bR=bR, h=h, cs=cs: e.matmul(bR[0:64, h:h + 1], lhsT=RK[:, h, cs], rhs=PAR[:, 3, h:h + 1], start=True, stop=True), r=["RK", "PAR"], w=[bRk])
                bG, bGk = bank()
                p.op("pe", lambda e, bG=bG, cs=cs: e.matmul(bG[0:64, :], lhsT=SG[:, cs], rhs=GUP[:, :], start=True, stop=False), r=["SG", "GUP"], w=[bGk])
                p.op("pe", lambda e, bG=bG, cs=cs: e.matmul(bG[0:64, :], lhsT=SG2[:, cs], rhs=GUP2[:, :], start=False, stop=True), r=["SG2", "GUP2"], w=[bGk])
                p.op("act", lambda e, bY=bY: e.copy(out=YS[:, :, :].rearrange("p a b -> p (a b)"), in_=bY[0:64, :]), r=[bYk], w=["YS"])
                p.op("dve", lambda e: e.tensor_reduce(out=ST[:, 0, :], in_=f3(YS), axis=AX.X, op=ALU.add), r=["YS"], w=["ST"])
                p.op("dve", lambda e: e.tensor_scalar(out=ST[:, 0, :], in0=ST[:, 0, :], scalar1=1.0 / 64, scalar2=None, op0=ALU.mult), r=["ST"], w=["ST"])
                TT("dve", f3(YS), f3(YS), bc(ST[:, 0, :], 64), ALU.subtract, ["YS", "ST"], ["YS"])
                TT("pool", f3(YT), f3(YS), f3(YS), ALU.mult, ["YS"], ["YT"])
                p.op("dve", lambda e: e.tensor_reduce(out=ST[:, 1, :], in_=f3(YT), axis=AX.X, op=ALU.add), r=["YT"], w=["ST"])
                p.op("dve", lambda e: e.tensor_scalar(out=ST[:, 1, :], in0=ST[:, 1, :], scalar1=1.0 / 64, scalar2=LNX_EPS, op0=ALU.mult, op1=ALU.add), r=["ST"], w=["ST"])
                p.op("act", lambda e: e.sqrt(out=ST[:, 1, :], in_=ST[:, 1, :]), r=["ST"], w=["ST"])
                p.op("dve", lambda e: e.reciprocal(out=ST[:, 2, :], in_=ST[:, 1, :]), r=["ST"], w=["ST"])
                TT("dve", f3(YS), f3(YS), bc(ST[:, 2, :], 64), ALU.mult, ["YS", "ST"], ["YS"])
                YSf = YS[:, :, :].rearrange("p a b -> p (a b)")
                TT("pool", YSf, YSf, LNW[:, :], ALU.mult, ["YS", "LNW"], ["YS"])
                TT("pool", YSf, YSf, LNB[:, :], ALU.add, ["YS", "LNB"], ["YS"])
                p.op("act", lambda e, bR=bR: e.copy(out=ST[:, 3, :], in_=bR[0:64, 0:H]), r=[bRk], w=["ST"])
                TT("dve", f3(YT), VT[:, c, :, :], bc(ST[:, 3, :], 64), ALU.mult, [f"VT{c}", "ST"], ["YT"])
                TT("dve", f3(YS), f3(YS), f3(YT), ALU.add, ["YS", "YT"], ["YS"])
                ob = OUT[(2 * blk + c) % 2]; obk = f"OUT{(2 * blk + c) % 2}"
                TT("dve", ob[:, :], YSf, bG[0:64, :], ALU.mult, ["YS", bGk], [obk])
                r0 = t0 + c * 64
                p.dma("sp", y[r0:r0 + 64, :], ob[:, :], r=[obk], w=[f"y{r0}"])
    print("rwkv op counts", p.counts())
    if not own:
        return None
    return p.build()


def rwkv_inputs(PTc, h, inp, l, consts):
    T = lambda t: PTc[t * 128:(t + 1) * 128]
    pad = lambda a: np.concatenate([np.zeros(a.shape[:-1] + (1,), np.float32), a], -1)
    rkv = np.stack([np.concatenate([T(22 + 4 * i + j) for j in range(4)], 0).reshape(H, 64, S).transpose(1, 0, 2) for i in range(3)])
    own = slice(h * 512, (h + 1) * 512)
    mu = inp["rwkv_mu"][l]
    mu_rkv = np.stack([mu[i * 1024 + h * 512:i * 1024 + (h + 1) * 512].reshape(H, 64).T for i in range(3)], 1)
    muw = np.zeros((128, 3), np.float32)
    muw[:, 0] = mu[3072:3200]; muw[:, 1] = mu[3200:3328]; muw[:32, 2] = mu[3328:3360]
    gd = np.concatenate([T(35), PTc[36 * 128:36 * 128 + 32]], 0)
    fm = lambda a: np.ascontiguousarray(a[own].reshape(H, 64).T)
    par = np.stack([fm(inp["rwkv_a0"][l]), fm(inp["rwkv_k_k"][l]), fm(inp["rwkv_k_a"][l]),
                    np.ascontiguousarray(inp["rwkv_r_k"][l][h * 8:(h + 1) * 8].T)], 1)
    d = {
        "rkv": np.ascontiguousarray(pad(rkv)), "wad": np.ascontiguousarray(pad(T(34))), "gd": np.ascontiguousarray(pad(gd)),
        "mu": np.ascontiguousarray(mu_rkv), "muw": muw,
        "wa_up": np.ascontiguousarray(np.concatenate([inp["rwkv_w_up"][l][:, own], inp["rwkv_a_up"][l][:, own]], 0)),
        "g_up": np.ascontiguousarray(inp["rwkv_g_up"][l][:, own]),
        "w0b": np.ascontiguousarray(np.broadcast_to(inp["rwkv_w0"][l][own][None], (128, 512))),
        "par": np.ascontiguousarray(par),
        "lnwb": np.ascontiguousarray(np.stack([np.broadcast_to(inp["rwkv_lnx_w"][l][own][None], (64, 512)), np.broadcast_to(inp["rwkv_lnx_b"][l][own][None], (64, 512))])),
    }
    d.update(consts)
    return d


D = 2048
KT = 16
NTOK = 2048
PT_ = 1024
TC = 512
DFF = 8192


def build_C():
    p = Prog("C")
    D_ = lambda n, s, dt=F32: p.dram(n, s, dt, "ExternalInput")
    xT = D_("xT", [D, NTOK]); yT = D_("yT", [3, 1024, NTOK])
    wg = D_("wg", [3, D, D]); bgd = D_("bg", [128, 3, KT]); wb = D_("wb", [3, 1024, D]); wo = D_("wo", [D, D])
    wu = D_("wu", [D, DFF]); wd = D_("wd", [DFF, D]); gn = D_("gn", [128, 4, KT])
    x1T = p.dram("x1T", [D, NTOK], F32, "ExternalOutput")
    x2T = p.dram("x2T", [D, NTOK], F32, "ExternalOutput")

    R1 = p.sb([128, 16384], F32, "R1")
    R2 = p.sb([128, 8192], F32, "R2")
    uT = R1[:, 0:8192].bitcast(BF16).rearrange("p (k t) -> p k t", t=PT_)
    y01 = R1[:, 8192:16384].bitcast(BF16).rearrange("p (k t) -> p k t", t=PT_)
    y2 = R2[:, 0:4096].bitcast(BF16).rearrange("p (k t) -> p k t", t=PT_)
    Z = R1[:, :].rearrange("p (k t) -> p k t", t=PT_)
    HG = R2[:, :].bitcast(BF16).rearrange("p (k t) -> p k t", t=PT_)
    mT = p.sb([128, KT, PT_], BF16, "mT")
    xs = R2[:, :].rearrange("p (k t) -> p k t", t=TC)
    sq = p.sb([128, 4, TC], F32, "sq")
    rs = p.sb([128, TC], F32, "rs")
    tmpc = [p.sb([128, TC], F32, f"tmpc{i}") for i in range(2)]
    sg = [p.sb([128, TC], F32, f"sg{i}") for i in range(2)]
    macc = p.sb([128, 2, TC], F32, "macc")
    NW = 3
    wst = [p.sb([128, KT, 128], F32, f"wst{i}") for i in range(NW)]
    wbf = [p.sb([128, KT, 128], BF16, f"wbf{i}") for i in range(NW)]
    gcol = p.sb([128, 4, KT], F32, "gcol")
    bgs = p.sb([128, 3, KT], F32, "bgs")
    ones_f = p.sb([128, 128], F32, "ones_f")
    pss = [(p.ps([128, 512], F32, f"psb{i}"), f"psb{i}") for i in range(7)]
    psn = (p.ps([128, 512], F32, "psn"), "psn")
    cnt = dict(w=0, ps=0, t=0)

    def nextps():
        i = cnt["ps"] % len(pss); cnt["ps"] += 1
        return pss[i]

    def gemm_tile(wcols, kt, act, akey, nch, out_fn):
        b = cnt["w"] % NW; cnt["w"] += 1
        q = ("sp", "act")[cnt["w"] % 2]
        p.dma(q, wst[b][:, 0:kt, :], wcols.rearrange("(k q) n -> q k n", q=128), w=[f"wst{b}"])
        ceng = ("pool", "dve")[cnt["w"] % 2]
        p.op(ceng, lambda e: e.tensor_copy(out=wbf[b][:, 0:kt, :], in_=wst[b][:, 0:kt, :]), r=[f"wst{b}"], w=[f"wbf{b}"])
        for c in range(nch):
            ps, pk = nextps()
            for k in range(kt):
                p.op("pe", lambda e, k=k, ps=ps, c=c: e.matmul(ps[:, :], lhsT=wbf[b][:, k, :], rhs=act[:, k, c * TC:(c + 1) * TC], start=(k == 0), stop=(k == kt - 1)),
                     r=[f"wbf{b}", akey(c)], w=[pk])
            out_fn(c, ps, pk)

    def norm_rs(src, skeys, gi, eps=1e-6):
        for k in range(KT):
            j = k % 4
            sap = src(k)
            if k % 2 == 0:
                p.op("act", lambda e, sap=sap, j=j: e.activation(out=sq[:, j, :], in_=sap, func=AF.Square), r=[skeys(k)], w=[f"sq{j}"])
            else:
                p.op("pool", lambda e, sap=sap, j=j: e.tensor_tensor(out=sq[:, j, :], in0=sap, in1=sap, op=ALU.mult), r=[skeys(k)], w=[f"sq{j}"])
            p.op("pe", lambda e, k=k, j=j: e.matmul(psn[0][:, :], lhsT=ones_f[:, :], rhs=sq[:, j, :], start=(k == 0), stop=(k == KT - 1)), r=[f"sq{j}", "ones_f"], w=[psn[1]])
        p.op("dve", lambda e: e.tensor_scalar(out=rs[:, :], in0=psn[0][:, :], scalar1=1.0 / D, scalar2=eps, op0=ALU.mult, op1=ALU.add), r=[psn[1]], w=["rs"])
        p.op("act", lambda e: e.sqrt(out=rs[:, :], in_=rs[:, :]), r=["rs"], w=["rs"])
        p.op("dve", lambda e: e.reciprocal(out=rs[:, :], in_=rs[:, :]), r=["rs"], w=["rs"])

    def fence(keys):
        p.op("dve", lambda e: e.memset(rs[:, 0:1], 0.0), r=list(keys), w=list(keys) + ["rs"])

    R1K = ["uT0", "uT1", "y0", "y1", "Z0", "Z1"]
    R2K = ["xs_a", "xs_b", "y2", "HG0", "HG1"]
    xv = xT.rearrange("(k q) t -> q k t", q=128)
    x1v = x1T.rearrange("(k q) t -> q k t", q=128)
    x2v = x2T.rearrange("(k q) t -> q k t", q=128)

    with p.nc.allow_low_precision("bf16 matmul operands, fp32 accumulation"):
        p.dma("sp", gcol[:, :, :], gn, w=["gcol"])
        p.dma("sp", bgs[:, :, :], bgd, w=["bgs"])
        p.op("dve", lambda e: e.memset(ones_f[:, :], 1.0), w=["ones_f"])
        for ps_ in range(NTOK // PT_):
            tp0 = ps_ * PT_
            for c in range(PT_ // TC):
                gsl = slice(tp0 + c * TC, tp0 + (c + 1) * TC)
                p.dma("sp", xs[:, 0:8, :], xv[:, 0:8, gsl], w=["xs_a"])
                p.dma("act", xs[:, 8:16, :], xv[:, 8:16, gsl], w=["xs_b"])
                norm_rs(lambda k: xs[:, k, :], lambda k: "xs_a" if k < 8 else "xs_b", 0)
                for k in range(KT):
                    p.op("dve", lambda e, k=k, c=c: e.scalar_tensor_tensor(out=uT[:, k, c * TC:(c + 1) * TC], in0=xs[:, k, :], scalar=gcol[:, 0, k:k + 1], in1=rs[:, :], op0=ALU.mult, op1=ALU.mult),
                         r=["xs_a" if k < 8 else "xs_b", "rs", "gcol"], w=[f"uT{c}"])
            fence(R2K)
            for b in range(3):
                for ft in range(8):
                    i = cnt["t"] % 2; cnt["t"] += 1
                    st = tmpc[i]
                    for c in range(2):
                        buf = (tmpc, sg)[c][i]; bk = ("tmpc", "sg")[c] + str(i)
                        p.dma(("sp", "act")[c], buf[:, :], yT[b, ft * 128:(ft + 1) * 128, tp0 + c * TC:tp0 + (c + 1) * TC], w=[bk])
                        dst = (y01[:, b * 8 + ft, c * TC:(c + 1) * TC] if b < 2 else y2[:, ft, c * TC:(c + 1) * TC])
                        p.op(("pool", "dve")[c], lambda e, buf=buf, dst=dst: e.tensor_copy(out=dst, in_=buf[:, :]), r=[bk], w=[f"y{b}"])
            for nt in range(KT):
                nsl = slice(nt * 128, (nt + 1) * 128)
                for b in range(3):
                    def gate_out(c, ps, pk, b=b, nt=nt):
                        p.op("act", lambda e: e.activation(out=sg[c][:, :], in_=ps[:, :], func=AF.Sigmoid, bias=bgs[:, b, nt:nt + 1]), r=[pk, "bgs"], w=[f"sg{c}"])
                    gemm_tile(wg[b, :, nsl], KT, uT, lambda c: f"uT{c}", 2, gate_out)
                    yact = (y01[:, b * 8:(b + 1) * 8, :] if b < 2 else y2)

                    def br_out(c, ps, pk, b=b):
                        if b == 0:
                            p.op("dve", lambda e: e.tensor_tensor(out=macc[:, c, :], in0=sg[c][:, :], in1=ps[:, :], op=ALU.mult), r=[pk, f"sg{c}"], w=[f"macc{c}"])
                        else:
                            p.op("dve", lambda e: e.tensor_tensor(out=tmpc[c][:, :], in0=sg[c][:, :], in1=ps[:, :], op=ALU.mult), r=[pk, f"sg{c}"], w=[f"tmpc{c}"])
                            p.op("pool", lambda e: e.tensor_tensor(out=macc[:, c, :], in0=macc[:, c, :], in1=tmpc[c][:, :], op=ALU.add), r=[f"macc{c}", f"tmpc{c}"], w=[f"macc{c}"])
                    gemm_tile(wb[b, :, nsl], 8, yact, lambda c, b=b: f"y{b}", 2, br_out)
                for c in range(2):
                    p.op("act", lambda e, c=c, nt=nt: e.copy(out=mT[:, nt, c * TC:(c + 1) * TC], in_=macc[:, c, :]), r=[f"macc{c}"], w=[f"mT{c}"])
            fence(R1K + R2K)
            for nt in range(KT):
                def z_out(c, ps, pk, nt=nt):
                    eng = ("act", "dve")[c]
                    if eng == "act":
                        p.op("act", lambda e: e.copy(out=Z[:, nt, c * TC:(c + 1) * TC], in_=ps[:, :]), r=[pk], w=[f"Z{c}"])
                    else:
                        p.op("dve", lambda e: e.tensor_copy(out=Z[:, nt, c * TC:(c + 1) * TC], in_=ps[:, :]), r=[pk], w=[f"Z{c}"])
                gemm_tile(wo[:, nt * 128:(nt + 1) * 128], KT, mT, lambda c: f"mT{c}", 2, z_out)
            for c in range(PT_ // TC):
                csl = slice(c * TC, (c + 1) * TC)
                gsl = slice(tp0 + c * TC, tp0 + (c + 1) * TC)
                norm_rs(lambda k: Z[:, k, csl], lambda k: f"Z{c}", 1)
                p.dma("sp", xs[:, 0:8, :], xv[:, 0:8, gsl], w=["xs_a"])
                p.dma("act", xs[:, 8:16, :], xv[:, 8:16, gsl], w=["xs_b"])
                for k in range(KT):
                    xk = "xs_a" if k < 8 else "xs_b"
                    i = k % 2
                    p.op("dve", lambda e, k=k, i=i, csl=csl: e.scalar_tensor_tensor(out=tmpc[i][:, :], in0=Z[:, k, csl], scalar=gcol[:, 1, k:k + 1], in1=rs[:, :], op0=ALU.mult, op1=ALU.mult),
                         r=[f"Z{c}", "rs", "gcol"], w=[f"tmpc{i}"])
                    p.op("pool", lambda e, k=k, i=i: e.tensor_tensor(out=xs[:, k, :], in0=xs[:, k, :], in1=tmpc[i][:, :], op=ALU.add), r=[xk, f"tmpc{i}"], w=[xk])
                p.dma("sp", x1v[:, 0:8, gsl], xs[:, 0:8, :], r=["xs_a"], w=[f"x1a{ps_}_{c}"])
                p.dma("act", x1v[:, 8:16, gsl], xs[:, 8:16, :], r=["xs_b"], w=[f"x1b{ps_}_{c}"])
                norm_rs(lambda k: xs[:, k, :], lambda k: "xs_a" if k < 8 else "xs_b", 2)
                for k in range(KT):
                    p.op("dve", lambda e, k=k, csl=csl: e.scalar_tensor_tensor(out=mT[:, k, csl], in0=xs[:, k, :], scalar=gcol[:, 2, k:k + 1], in1=rs[:, :], op0=ALU.mult, op1=ALU.mult),
                         r=["xs_a" if k < 8 else "xs_b", "rs", "gcol"], w=[f"mT{c}"])
            fence(R2K)
            for ffg in range(4):
                for ft in range(16):
                    f = ffg * 16 + ft

                    def up_out(c, ps, pk, ft=ft):
                        p.op("act", lambda e: e.activation(out=tmpc[c][:, :], in_=ps[:, :], func=AF.Relu), r=[pk], w=[f"tmpc{c}"])
                        p.op("pool", lambda e: e.tensor_tensor(out=HG[:, ft, c * TC:(c + 1) * TC], in0=tmpc[c][:, :], in1=tmpc[c][:, :], op=ALU.mult), r=[f"tmpc{c}"], w=[f"HG{c}"])
                    gemm_tile(wu[:, f * 128:(f + 1) * 128], KT, mT, lambda c: f"mT{c}", 2, up_out)
                for nt in range(KT):
                    def dn_out(c, ps, pk, nt=nt, ffg=ffg):
                        if ffg == 0:
                            p.op("act", lambda e: e.copy(out=Z[:, nt, c * TC:(c + 1) * TC], in_=ps[:, :]), r=[pk], w=[f"Z{c}"])
                        else:
                            p.op("dve", lambda e: e.tensor_tensor(out=Z[:, nt, c * TC:(c + 1) * TC], in0=Z[:, nt, c * TC:(c + 1) * TC], in1=ps[:, :], op=ALU.add), r=[pk, f"Z{c}"], w=[f"Z{c}"])
                    gemm_tile(wd[ffg * 2048:(ffg + 1) * 2048, nt * 128:(nt + 1) * 128], KT, HG, lambda c: f"HG{c}", 2, dn_out)
            fence(R2K)
            for c in range(PT_ // TC):
                csl = slice(c * TC, (c + 1) * TC)
                gsl = slice(tp0 + c * TC, tp0 + (c + 1) * TC)
                norm_rs(lambda k: Z[:, k, csl], lambda k: f"Z{c}", 3)
                p.dma("sp", xs[:, 0:8, :], x1v[:, 0:8, gsl], r=[f"x1a{ps_}_{c}"], w=["xs_a"])
                p.dma("act", xs[:, 8:16, :], x1v[:, 8:16, gsl], r=[f"x1b{ps_}_{c}"], w=["xs_b"])
                for k in range(KT):
                    xk = "xs_a" if k < 8 else "xs_b"
                    i = k % 2
                    p.op("dve", lambda e, k=k, i=i, csl=csl: e.scalar_tensor_tensor(out=tmpc[i][:, :], in0=Z[:, k, csl], scalar=gcol[:, 3, k:k + 1], in1=rs[:, :], op0=ALU.mult, op1=ALU.mult),
                         r=[f"Z{c}", "rs", "gcol"], w=[f"tmpc{i}"])
                    p.op("pool", lambda e, k=k, i=i: e.tensor_tensor(out=xs[:, k, :], in0=xs[:, k, :], in1=tmpc[i][:, :], op=ALU.add), r=[xk, f"tmpc{i}"], w=[xk])
                p.dma("sp", x2v[:, 0:8, gsl], xs[:, 0:8, :], r=["xs_a"], w=[f"x2a{ps_}_{c}"])
                p.dma("act", x2v[:, 8:16, gsl], xs[:, 8:16, :], r=["xs_b"], w=[f"x2b{ps_}_{c}"])
            fence(R1K + R2K)
    print("C op counts", p.counts())
    return p.build()


def c_inputs(x_b, h, ycat_b, inp, l):
    tsl = slice(h * NTOK, (h + 1) * NTOK)
    fm = lambda a: np.ascontiguousarray(a.reshape(KT, 128).T)
    gn = np.stack([fm(inp["norm_pre_mix"][l]), fm(inp["norm_post_mix"][l]), fm(inp["norm_pre_mlp"][l]), fm(inp["norm_post_mlp"][l])], 1)
    bg = np.stack([fm(inp["b_gate"][l][b]) for b in range(3)], 1)
    return {
        "xT": np.ascontiguousarray(x_b[tsl].T),
        "yT": np.ascontiguousarray(ycat_b[tsl].T.reshape(3, 1024, NTOK)),
        "wg": np.ascontiguousarray(inp["w_gate"][l]), "bg": np.ascontiguousarray(bg), "wb": np.ascontiguousarray(inp["w_branch"][l]),
        "wo": np.ascontiguousarray(inp["w_out"][l]), "wu": np.ascontiguousarray(inp["w_up"][l]), "wd": np.ascontiguousarray(inp["w_down"][l]),
        "gn": np.ascontiguousarray(gn),
    }


_PROGS = {}


def _prog(name, fn):
    return fn()


import os as _os, time as _time
_DBG = _os.environ.get("KDBG")
_T0 = [_time.time()]


def _run(nc, in_maps, tag=""):
    print("launch", tag, "build+prep done at", round(_time.time() - _T0[0], 1), flush=True)
    r = run_bass_kernel_spmd(nc, in_maps, core_ids=list(range(8))).results
    print("launch", tag, "finished at", round(_time.time() - _T0[0], 1), flush=True)
    if _DBG:
        for c in (0, 1):
            for k, v in r[c].items():
                np.save(f"{_DBG}/{tag}_{c}_{k}.npy", np.asarray(v))
    return r


def build_mix():
    p = Prog("mix")
    for pre, fn in (("n_", build_nsa), ("d_", build_diff), ("r_", build_rwkv)):
        p.begin_phase(pre)
        fn(p)
        p.end_phase()
    return p.build()


def kernel(**inp):
    inp = {k: np.asarray(v) for k, v in inp.items()}
    x = np.ascontiguousarray(inp["x"], dtype=np.float32)
    pos = inp["positions"].astype(np.int32)
    B = x.shape[0]
    _T0[0] = _time.time()
    nconst = nsa_consts()
    rconst = rwkv_consts()
    for l in range(2):
        in_maps = []
        for c in range(8):
            b, h = c // 2, c % 2
            in_maps.append({"xT": np.ascontiguousarray(x[b].T), "w": np.ascontiguousarray(inp["w_in"][l][:, core_cols(h)]),
                            "g": np.ascontiguousarray(inp["norm_pre_mix"][l].reshape(KT, 128).T)})
        PT = [r["PT"] for r in _run(build_A(), in_maps, f"A{l}")]
        in_n, in_d, in_r = [], [], []
        for c in range(8):
            b, h = c // 2, c % 2
            in_n.append(nsa_inputs(PT[c], pos[b], inp["nsa_cmp_pos"][l], inp["nsa_cmp_w1"][l], inp["nsa_cmp_w2"][l], nconst))
            in_d.append(diff_inputs(PT[c], pos[b], inp["diff_lambda"][l], inp["diff_subln"][l], layer=l))
            in_r.append(rwkv_inputs(PT[c], h, inp, l, rconst))
        in_m = []
        for c in range(8):
            m = {"n_" + k: v for k, v in in_n[c].items()}
            m.update({"d_" + k: v for k, v in in_d[c].items()})
            m.update({"r_" + k: v for k, v in in_r[c].items()})
            in_m.append(m)
        del in_n, in_d, in_r, PT
        rm = _run(build_mix(), in_m, f"M{l}")
        del in_m
        yn = [r["n_y"] for r in rm]
        yd = [r["d_y"] for r in rm]
        yr = [r["r_y"] for r in rm]
        del rm
        in_c = []
        for c in range(8):
            b, h = c // 2, c % 2
            ycat = np.concatenate([yn[2 * b], yn[2 * b + 1],
                                   yd[2 * b][0], yd[2 * b][1], yd[2 * b + 1][0], yd[2 * b + 1][1],
                                   yr[2 * b], yr[2 * b + 1]], axis=1)
            in_c.append(c_inputs(x[b], h, ycat, inp, l))
        res = _run(build_C(), in_c, f"C{l}")
        xn = np.empty_like(x)
        for c in range(8):
            b, h = c // 2, c % 2
            xn[b, h * NTOK:(h + 1) * NTOK] = res[c]["x2T"].T
        x = xn
    return x
```
